# Optimizing a Trainium2 kernel written in Bass

```python
import math
import jax, jax.numpy as jnp
from jax import lax
import numpy as np

D_MODEL = 1024
BATCH = 16
SEQ = 256
DEPTH = 2
DEC_BATCH = 8
DEC_SEQ = 1024
PAST_LEN = 512

GRID_W = 64
EPS = 1e-6
N_A_LAYERS = (DEPTH + 1) // 2
N_C_LAYERS = DEPTH // 2

MLA_HEADS = D_MODEL // 128
Q_RANK = D_MODEL // 4
KV_RANK = D_MODEL // 8
QK_NOPE = 64
QK_ROPE = 32
V_HEAD = 64
MLA_WIDTH = MLA_HEADS * V_HEAD
ATTN_SCALE = (QK_NOPE + QK_ROPE) ** -0.5
ROPE_THETA = 10000.0
Q_BLOCK = 128

CONV_WIDTH = D_MODEL // 2
CONV_K = 31

A_SPLITS = [Q_RANK, Q_RANK + KV_RANK, Q_RANK + KV_RANK + QK_ROPE,
            Q_RANK + KV_RANK + QK_ROPE + 2 * CONV_WIDTH]
A_MIX = MLA_WIDTH + CONV_WIDTH
A_IN = A_SPLITS[-1] + A_MIX

SSD_INNER = 2 * D_MODEL
SSD_HEAD_DIM = 64
SSD_HEADS = SSD_INNER // SSD_HEAD_DIM
SSD_GROUPS = 4
SSD_HPG = SSD_HEADS // SSD_GROUPS
SSD_STATE = 128
SSD_CONV_K = 5
SSD_CHUNK = 128
SSD_CONV_CH = SSD_INNER + 2 * SSD_GROUPS * SSD_STATE
C_IN = SSD_INNER + SSD_CONV_CH + 2 * SSD_HEADS

kernel_name = 'hybrid_mla_conformer_ssd_diffusion_step'


def rmsnorm(x, g):
    xf = x.astype(jnp.float32)
    y = xf * lax.rsqrt(jnp.mean(xf * xf, axis=-1, keepdims=True) + EPS)
    return (y * g.astype(jnp.float32)).astype(x.dtype)


def layernorm(x, g, b):
    xf = x.astype(jnp.float32)
    mu = jnp.mean(xf, axis=-1, keepdims=True)
    xc = xf - mu
    y = xc * lax.rsqrt(jnp.mean(xc * xc, axis=-1, keepdims=True) + EPS)
    return (y * g.astype(jnp.float32) + b.astype(jnp.float32)).astype(x.dtype)


def modulation(cond, w_mod, b_mod):
    m = jax.nn.silu(cond) @ w_mod + b_mod
    shift, scale, gate = jnp.split(m, 3, axis=-1)
    return shift[..., None, :], scale[..., None, :], gate[..., None, :]


def dwconv(x, w, b):
    k = w.shape[0]
    y = lax.conv_general_dilated(x, w[:, None, :].astype(x.dtype), window_strides=(1,),
                                 padding=[(k // 2, k // 2)],
                                 dimension_numbers=('NWC', 'WIO', 'NWC'),
                                 feature_group_count=x.shape[-1])
    return y + b


def axial_rope(length):
    rows = length // GRID_W
    row = jnp.repeat(jnp.arange(rows, dtype=jnp.float32), GRID_W)
    col = jnp.tile(jnp.arange(GRID_W, dtype=jnp.float32), rows)
    n_freq = QK_ROPE // 4
    inv = jnp.power(ROPE_THETA, -jnp.arange(n_freq, dtype=jnp.float32) / n_freq)
    ang = jnp.concatenate([row[:, None] * inv, col[:, None] * inv], axis=-1)
    return jnp.cos(ang), jnp.sin(ang)


def apply_rope(x, cos, sin):
    xf = x.astype(jnp.float32)
    x1, x2 = xf[..., 0::2], xf[..., 1::2]
    out = jnp.stack([x1 * cos - x2 * sin, x1 * sin + x2 * cos], axis=-1).reshape(x.shape)
    return out.astype(x.dtype)


def mla_attend(q_nope, q_rope, k_nope, k_rope, v):
    b, lq, h, _ = q_nope.shape
    nblk = lq // Q_BLOCK

    def to_blocks(t):
        return t.reshape(b, nblk, Q_BLOCK, *t.shape[2:]).swapaxes(0, 1)

    def one_block(qs):
        qn, qr = qs
        s = (jnp.einsum('bqhd,bkhd->bhqk', qn, k_nope, preferred_element_type=jnp.float32)
             + jnp.einsum('bqhr,bkr->bhqk', qr, k_rope, preferred_element_type=jnp.float32)) * ATTN_SCALE
        p = jax.nn.softmax(s, axis=-1).astype(v.dtype)
        return jnp.einsum('bhqk,bkhd->bqhd', p, v)

    out = lax.map(one_block, (to_blocks(q_nope), to_blocks(q_rope)))
    return out.swapaxes(0, 1).reshape(b, lq, h * V_HEAD)


def layer_a(h, w_in, g_q, g_kv, w_uq, w_uk, w_uv, conv_w, conv_b, ln_g, ln_b, w_out,
            ctx_ckv=None, ctx_krope=None):
    b, L, _ = h.shape
    u = h @ w_in
    q_lat, kv_lat, k_rope, glu, gate = jnp.split(u, A_SPLITS, axis=-1)
    ckv = rmsnorm(kv_lat, g_kv)
    q = (rmsnorm(q_lat, g_q) @ w_uq).reshape(b, L, MLA_HEADS, QK_NOPE + QK_ROPE)
    q_nope, q_rope = q[..., :QK_NOPE], q[..., QK_NOPE:]
    if ctx_ckv is None:
        ckv_all, krope_all = ckv, k_rope
    else:
        cos, sin = axial_rope(L)
        q_rope = apply_rope(q_rope, cos[:, None, :], sin[:, None, :])
        krope_all = jnp.concatenate([ctx_krope, apply_rope(k_rope, cos, sin)], axis=1)
        ckv_all = jnp.concatenate([ctx_ckv, ckv], axis=1)
    lk = ckv_all.shape[1]
    k_nope = (ckv_all @ w_uk).reshape(b, lk, MLA_HEADS, QK_NOPE)
    v = (ckv_all @ w_uv).reshape(b, lk, MLA_HEADS, V_HEAD)
    attn = mla_attend(q_nope, q_rope, k_nope, krope_all, v)
    ga, gb = jnp.split(glu, 2, axis=-1)
    cv = ga * jax.nn.sigmoid(gb)
    cv = jax.nn.silu(layernorm(dwconv(cv, conv_w, conv_b), ln_g, ln_b))
    mix = jnp.concatenate([attn, cv], axis=-1) * jax.nn.silu(gate)
    return mix @ w_out, ckv, k_rope


def ssd_scan(x, dt, a_neg, bm, cm, init):
    b, L, G, E, P = x.shape
    N = bm.shape[-1]
    nc, Q = L // SSD_CHUNK, SSD_CHUNK
    f32 = jnp.float32
    xf = x.astype(f32).reshape(b, nc, Q, G, E, P)
    bf = bm.astype(f32).reshape(b, nc, Q, G, N)
    cf = cm.astype(f32).reshape(b, nc, Q, G, N)
    dtc = dt.reshape(b, nc, Q, G, E)
    a_cum = jnp.cumsum(dtc * a_neg.reshape(G, E), axis=2)
    xdt = xf * dtc[..., None]
    seg = a_cum[:, :, :, None] - a_cum[:, :, None, :]
    causal = jnp.tril(jnp.ones((Q, Q), dtype=bool))[:, :, None, None]
    decay = jnp.exp(jnp.where(causal, seg, -jnp.inf))
    cb = jnp.einsum('bclgn,bcsgn->bclsg', cf, bf)
    y_diag = jnp.einsum('bclsg,bclsge,bcsgep->bclgep', cb, decay, xdt)
    decay_end = jnp.exp(a_cum[:, :, -1:] - a_cum)
    chunk_states = jnp.einsum('bcsgn,bcsge,bcsgep->bcgepn', bf, decay_end, xdt)
    chunk_decay = jnp.exp(a_cum[:, :, -1])

    def step(s, inp):
        st, dec = inp
        return s * dec[..., None, None] + st, s

    s0 = init.astype(f32).reshape(b, G, E, P, N)
    final, prev = lax.scan(step, s0, (chunk_states.swapaxes(0, 1), chunk_decay.swapaxes(0, 1)))
    prev = prev.swapaxes(0, 1)
    y_off = jnp.einsum('bclgn,bcgepn,bclge->bclgep', cf, prev, jnp.exp(a_cum))
    y = (y_diag + y_off).reshape(b, L, G, E, P)
    return y.astype(x.dtype), final.reshape(b, G * E, P, N)


def layer_c(h, w_in, conv_w, conv_b, dt_bias, a_log, d_skip, g_norm, w_out, init_state=None):
    b, L, _ = h.shape
    u = h @ w_in
    z, xbc, dt = jnp.split(u, [SSD_INNER, SSD_INNER + SSD_CONV_CH], axis=-1)
    xbc = jax.nn.silu(dwconv(xbc, conv_w, conv_b))
    xs, bm, cm = jnp.split(xbc, [SSD_INNER, SSD_INNER + SSD_GROUPS * SSD_STATE], axis=-1)
    xs = xs.reshape(b, L, SSD_GROUPS, SSD_HPG, SSD_HEAD_DIM)
    bm = bm.reshape(b, L, SSD_GROUPS, SSD_STATE)
    cm = cm.reshape(b, L, SSD_GROUPS, SSD_STATE)
    dt = jax.nn.softplus(dt.astype(jnp.float32).reshape(b, L, 2, SSD_HEADS) + dt_bias.astype(jnp.float32))
    a_neg = -jnp.exp(a_log.astype(jnp.float32))
    if init_state is None:
        init_state = jnp.zeros((b, 2, SSD_HEADS, SSD_HEAD_DIM, SSD_STATE), jnp.float32)
    y_f, s_f = ssd_scan(xs, dt[:, :, 0], a_neg[0], bm, cm, init_state[:, 0])
    y_b, s_b = ssd_scan(jnp.flip(xs, 1), jnp.flip(dt[:, :, 1], 1), a_neg[1],
                        jnp.flip(bm, 1), jnp.flip(cm, 1), init_state[:, 1])
    y = y_f + jnp.flip(y_b, 1) + d_skip.reshape(SSD_GROUPS, SSD_HPG, 1) * xs
    y = rmsnorm(y.reshape(b, L, SSD_INNER) * jax.nn.silu(z), g_norm)
    return y @ w_out, jnp.stack([s_f, s_b], axis=1)


def setup_inputs(seed: int = 0) -> dict:
    key = jax.random.key(seed)
    ks = iter(jax.random.split(key, 40))

    def nrm(shape, s):
        return s * jax.random.normal(next(ks), shape, jnp.float32)

    def gain(shape):
        return 1.0 + nrm(shape, 0.05)

    D = D_MODEL
    dt0 = jnp.exp(jax.random.uniform(next(ks), (N_C_LAYERS, 2, SSD_HEADS), jnp.float32,
                                     math.log(1e-3), math.log(1e-1)))
    dt_bias = dt0 + jnp.log(-jnp.expm1(-dt0))
    a_log = jnp.log(jax.random.uniform(next(ks), (N_C_LAYERS, 2, SSD_HEADS), jnp.float32, 1.0, 16.0))
    return {
        'x_prompt': nrm((BATCH, SEQ, D), 1.0),
        'x_sample': nrm((DEC_BATCH, DEC_SEQ, D), 1.0),
        'cache_ckv': nrm((DEC_BATCH, N_A_LAYERS, PAST_LEN, KV_RANK), 1.0),
        'cache_krope': nrm((DEC_BATCH, N_A_LAYERS, PAST_LEN, QK_ROPE), 1.0),
        'state_ssd': nrm((DEC_BATCH, N_C_LAYERS, 2, SSD_HEADS, SSD_HEAD_DIM, SSD_STATE), 0.5),
        'c': nrm((DEC_BATCH, D), 1.0),
        'c_ctx': nrm((D,), 1.0),
        'w_mod': nrm((DEPTH, D, 3 * D), 0.5 * D ** -0.5),
        'b_mod': nrm((DEPTH, 3 * D), 0.01),
        'g_pre': gain((DEPTH, D)),
        'g_final': gain((D,)),
        'a_w_in': nrm((N_A_LAYERS, D, A_IN), D ** -0.5),
        'a_g_q': gain((N_A_LAYERS, Q_RANK)),
        'a_g_kv': gain((N_A_LAYERS, KV_RANK)),
        'a_w_uq': nrm((N_A_LAYERS, Q_RANK, MLA_HEADS * (QK_NOPE + QK_ROPE)), Q_RANK ** -0.5),
        'a_w_uk': nrm((N_A_LAYERS, KV_RANK, MLA_HEADS * QK_NOPE), KV_RANK ** -0.5),
        'a_w_uv': nrm((N_A_LAYERS, KV_RANK, MLA_HEADS * V_HEAD), KV_RANK ** -0.5),
        'a_conv_w': nrm((N_A_LAYERS, CONV_K, CONV_WIDTH), CONV_K ** -0.5),
        'a_conv_b': nrm((N_A_LAYERS, CONV_WIDTH), 0.01),
        'a_ln_g': gain((N_A_LAYERS, CONV_WIDTH)),
        'a_ln_b': nrm((N_A_LAYERS, CONV_WIDTH), 0.01),
        'a_w_out': nrm((N_A_LAYERS, A_MIX, D), A_MIX ** -0.5),
        'c_w_in': nrm((N_C_LAYERS, D, C_IN), D ** -0.5),
        'c_conv_w': nrm((N_C_LAYERS, SSD_CONV_K, SSD_CONV_CH), SSD_CONV_K ** -0.5),
        'c_conv_b': nrm((N_C_LAYERS, SSD_CONV_CH), 0.01),
        'c_dt_bias': dt_bias,
        'c_a_log': a_log,
        'c_d': gain((N_C_LAYERS, SSD_HEADS)),
        'c_g_norm': gain((N_C_LAYERS, SSD_INNER)),
        'c_w_out': nrm((N_C_LAYERS, SSD_INNER, D), SSD_INNER ** -0.5),
    }


def reference(x_prompt, x_sample, cache_ckv, cache_krope, state_ssd, c, c_ctx,
              w_mod, b_mod, g_pre, g_final,
              a_w_in, a_g_q, a_g_kv, a_w_uq, a_w_uk, a_w_uv, a_conv_w, a_conv_b, a_ln_g, a_ln_b, a_w_out,
              c_w_in, c_conv_w, c_conv_b, c_dt_bias, c_a_log, c_d, c_g_norm, c_w_out):
    xp, xs = x_prompt, x_sample
    new_ckv, new_krope, new_ssd = [], [], []
    for layer in range(DEPTH):
        i = layer // 2
        sh_p, sc_p, gt_p = modulation(c_ctx, w_mod[layer], b_mod[layer])
        sh_s, sc_s, gt_s = modulation(c, w_mod[layer], b_mod[layer])
        hp = rmsnorm(xp, g_pre[layer]) * (1.0 + sc_p) + sh_p
        hs = rmsnorm(xs, g_pre[layer]) * (1.0 + sc_s) + sh_s
        if layer % 2 == 0:
            wa = (a_w_in[i], a_g_q[i], a_g_kv[i], a_w_uq[i], a_w_uk[i], a_w_uv[i],
                  a_conv_w[i], a_conv_b[i], a_ln_g[i], a_ln_b[i], a_w_out[i])
            out_p, ckv_p, krope_p = layer_a(hp, *wa)
            out_s, _, _ = layer_a(hs, *wa, ctx_ckv=cache_ckv[:, i], ctx_krope=cache_krope[:, i])
            new_ckv.append(ckv_p)
            new_krope.append(krope_p)
        else:
            wc = (c_w_in[i], c_conv_w[i], c_conv_b[i], c_dt_bias[i], c_a_log[i], c_d[i], c_g_norm[i], c_w_out[i])
            out_p, st_p = layer_c(hp, *wc)
            out_s, _ = layer_c(hs, *wc, init_state=state_ssd[:, i])
            new_ssd.append(st_p)
        xp = xp + gt_p * out_p
        xs = xs + gt_s * out_s
    y_prompt = rmsnorm(xp, g_final)
    y_sample = rmsnorm(xs, g_final)
    new_cache_ckv = jnp.stack(new_ckv, axis=1)
    new_cache_krope = jnp.stack(new_krope, axis=1)
    new_state_ssd = jnp.stack(new_ssd, axis=1)
    return (y_prompt, y_sample, new_cache_ckv, new_cache_krope, new_state_ssd)
```

```python
import numpy as np
from contextlib import ExitStack
import concourse.bass as bass
import concourse.mybir as mybir
from concourse.bass_utils import run_bass_kernel_spmd
from concourse.alu_op_type import AluOpType as ALU

F32 = mybir.dt.float32
BF16 = mybir.dt.bfloat16
AF = mybir.ActivationFunctionType

D = 1024
NCORES = 8
EPS = 1e-6
ATTN_SCALE = 96 ** -0.5
ARENA_COLS = 53100


class Buf:
    __slots__ = ("ap", "w", "r", "name", "f32", "fence", "sticky")

    def __init__(self, ap, name=""):
        self.ap = ap
        self.f32 = Arena.LAST if ap.dtype == BF16 else None
        self.w = None
        self.r = {}
        self.name = name
        self.fence = False
        self.sticky = False

    def __getitem__(self, k):
        return self.ap[k]


class Prog:
    ENG = ("pe", "dve", "act", "pool", "sp")

    def __init__(self, nc, es):
        self.nc = nc
        self.lists = {e: [] for e in self.ENG}
        self.sem = {e: es.enter_context(nc.semaphore("sem_" + e)) for e in self.ENG}
        self.cnt = {e: 0 for e in self.ENG}
        self.waited = {e: {} for e in self.ENG}
        self.dsem = {}
        for q in ("pool", "sp"):
            self.dsem[q] = [[es.enter_context(nc.semaphore("dq_%s_%d" % (q, i))), 0] for i in range(24)]
        self.dptr = {"pool": 0, "sp": 0}
        self.out_toks = []
        self.pending_dma = []

    def barrier(self):
        toks = [(self.sem[e], self.cnt[e]) for e in self.ENG if self.cnt[e] > 0] + self.pending_dma
        self.pending_dma = []
        for e in self.ENG:
            for (sem, val) in toks:
                if sem is self.sem[e]:
                    continue
                k = id(sem)
                if self.waited[e].get(k, 0) >= val:
                    continue
                self.waited[e][k] = val
                self.lists[e].append(("w", sem, val))

    def _wait(self, eng, tok, raw=False):
        if tok is None:
            return
        sem, val = tok
        if sem is self.sem[eng]:
            if eng == "pe" or not raw:
                return
        k = id(sem)
        if self.waited[eng].get(k, 0) >= val:
            return
        self.waited[eng][k] = val
        self.lists[eng].append(("w", sem, val))

    def _deps(self, eng, reads, writes):
        for b in reads:
            self._wait(eng, b.w, raw=True)
        for b in writes:
            self._wait(eng, b.w, raw=b.fence)
            for t in b.r.values():
                self._wait(eng, t)

    def _mark(self, tok, reads, writes):
        k = id(tok[0])
        for b in reads:
            b.r[k] = tok
        for b in writes:
            b.w = tok
            b.r = {}
            b.fence = b.sticky

    def op(self, eng, fn, reads=(), writes=(), inc=True):
        self._deps(eng, reads, writes)
        self.lists[eng].append(("o", fn, inc))
        tok = (self.sem[eng], self.cnt[eng] + 1)
        if inc:
            self.cnt[eng] += 1
        self._mark(tok, reads, writes)

    def dma(self, q, out_ap, in_ap, reads=(), writes=(), is_output=False, **kw):
        self._deps(q, reads, writes)
        pool = self.dsem[q]
        i = self.dptr[q]
        self.dptr[q] = (i + 1) % len(pool)
        sem, uses = pool[i]
        if uses > 0:
            self._wait(q, (sem, 16 * uses))
        pool[i][1] = uses + 1
        tok = (sem, 16 * (uses + 1))
        self.lists[q].append(("d", out_ap, in_ap, sem, kw))
        self._mark(tok, reads, writes)
        self.pending_dma.append(tok)
        if is_output:
            self.out_toks.append(tok)

    def finish(self):
        for tok in self.out_toks:
            self._wait("sp", tok)

    def replay(self, block):
        nc = self.nc
        names = {"pe": "tensor", "dve": "vector", "act": "scalar", "pool": "gpsimd", "sp": "sync"}
        for e in self.ENG:
            items = self.lists[e]
            sem_e = self.sem[e]

            def body(eng, items=items, sem_e=sem_e):
                for it in items:
                    if it[0] == "w":
                        eng.wait_ge(it[1], it[2])
                    elif it[0] == "o":
                        ins = it[1](eng)
                        if it[2]:
                            ins.then_inc(sem_e, 1)
                    else:
                        eng.dma_start(out=it[1], in_=it[2], **it[4]).then_inc(it[3], 16)

            getattr(block, names[e])(body)


class Arena:
    LAST = None

    def __init__(self, ap, ncols):
        self.ap = ap
        self.n = ncols
        self.top = 0
        self.stack = []
        self.on_pop = None

    def push(self):
        self.stack.append(self.top)

    def pop(self):
        self.top = self.stack.pop()
        if self.on_pop is not None:
            self.on_pop()

    def f32(self, cols, name="", parts=128):
        cols_al = (cols + 15) // 16 * 16
        off = self.top
        self.top += cols_al
        assert self.top <= self.n, "arena overflow at %s: %d" % (name, self.top)
        return self.ap[0:parts, off:off + cols]

    def bf16(self, cols, name="", parts=128):
        c32 = (cols + 1) // 2
        c32a = (c32 + 15) // 16 * 16
        off = self.top
        self.top += c32a
        assert self.top <= self.n, "arena overflow at %s: %d" % (name, self.top)
        self.last_f32 = self.ap[0:parts, off:off + c32]
        Arena.LAST = self.last_f32
        return self.ap[0:parts, off:off + c32].bitcast(BF16)[:, 0:cols]


def bcast_last(ap, n):
    return ap.unsqueeze(len(ap.shape)).to_broadcast(list(ap.shape) + [n])


def build_program(debug=False):
    nc = bass.Bass("TRN2", target_bir_lowering=False)
    es = ExitStack()

    def din(name, shape, dt=F32):
        return nc.dram_tensor(name, list(shape), dt, kind="ExternalInput").ap()

    def dout(name, shape, dt=F32):
        return nc.dram_tensor(name, list(shape), dt, kind="ExternalOutput").ap()

    x_all = din("x_all", [1536, D])
    cckv = din("cckv", [512, 128])
    ckr = din("ckr", [512, 32])
    st0 = din("st0", [2, 128, 2048])
    condT = din("condT", [128, 16])
    w_mod = din("w_mod", [2, D, 3072])
    bmodT = din("bmodT", [128, 2 * 24])
    bmod_row = din("bmod_row", [1, 2 * 3072])
    gpreT = din("gpreT", [128, 16])
    gfin = din("gfin", [1, D])
    ropecs = din("ropecs", [1024, 32])
    a_w_in = din("a_w_in", [D, 2464])
    a_gq = din("a_gq", [1, 256])
    a_gkv = din("a_gkv", [1, 128])
    a_w_uq = din("a_w_uq", [256, 768])
    a_w_uk = din("a_w_uk", [128, 512])
    a_w_uv = din("a_w_uv", [128, 512])
    a_convwT = din("a_convwT", [128, 4 * 31])
    a_convb = din("a_convb", [1, 512])
    a_lngT = din("a_lngT", [128, 4])
    a_lnbT = din("a_lnbT", [128, 4])
    a_w_out = din("a_w_out", [D, D])
    c_w_in = din("c_w_in", [D, 5184])
    c_convwT = din("c_convwT", [128, 24 * 5])
    c_convb = din("c_convb", [1, 3072])
    c_dtb = din("c_dtb", [1, 64])
    c_alog = din("c_alog", [1, 64])
    c_dskip = din("c_dskip", [1, 32])
    c_gnT = din("c_gnT", [128, 16])
    c_w_out = din("c_w_out", [2048, D])
    k_ident = din("k_ident", [128, 128])
    k_tmf = din("k_tmf", [128, 128])
    k_tmb = din("k_tmb", [128, 128])
    k_sel = din("k_sel", [64, 32])
    c_convbT = din("c_convbT", [128, 24])

    y_all = dout("y_all", [1536, D])
    o_ckv = dout("o_ckv", [512, 128])
    o_kr = dout("o_kr", [512, 32])
    o_st = dout("o_st", [2, 2, 128, 2048])
    import os as _os
    DBG = _os.environ.get("KDBG")

    def dump(name, ap, bufs):
        if DBG:
            d = nc.dram_tensor("dbg_" + name, list(ap.shape), ap.dtype, kind="ExternalOutput").ap()
            P.dma("sp", d, ap, reads=bufs, is_output=True)

    arena_t = es.enter_context(nc.sbuf_tensor("arena", [128, ARENA_COLS], F32))
    AR = Arena(arena_t, ARENA_COLS)
    P = Prog(nc, es)
    AR.on_pop = P.barrier
    banks = []
    for i in range(8):
        pt = es.enter_context(nc.psum_tensor("psb%d" % i, [128, 512], F32))
        banks.append(Buf(pt[:, :], "bank%d" % i))
    bank_i = [0]

    wide = [True]
    bank_w = [0]

    def bank():
        if wide[0]:
            b = banks[bank_w[0]]
            bank_w[0] = (bank_w[0] + 1) % 8
            return b
        b = banks[4 + bank_i[0]]
        bank_i[0] = (bank_i[0] + 1) % 4
        return b

    def lbank(i):
        return banks[i % 4]

    def keep_warm(n_):
        dst = banks[3].ap
        for _ in range(n_):
            P.op("pe", lambda e: e.matmul(dst, ident_b.ap, ones_b.ap, start=True, stop=True),
                 reads=[ident_b, ones_b], writes=[banks[3]], inc=False)

    def mm(ps_ap, lhsT, rhs, start, stop, reads, ps_buf, inc):
        P.op("pe", lambda e: e.matmul(ps_ap, lhsT, rhs, start=start, stop=stop),
             reads=reads, writes=[ps_buf], inc=inc)

    def tr(ps_ap, in_ap, ident_ap, reads, ps_buf, inc):
        P.op("pe", lambda e: e.transpose(ps_ap, in_ap, ident_ap), reads=reads, writes=[ps_buf], inc=inc)

    def act(out, in_, func, reads, writes, eng="act", **kw):
        P.op(eng, lambda e: e.activation(out=out, in_=in_, func=func, **kw), reads=reads, writes=writes)

    def tt(out, in0, in1, op, reads, writes, eng="dve"):
        P.op(eng, lambda e: e.tensor_tensor(out=out, in0=in0, in1=in1, op=op), reads=reads, writes=writes)

    def ts(out, in0, s1, s2, op0, op1, reads, writes, eng="dve"):
        if op1 is None:
            P.op(eng, lambda e: e.tensor_scalar(out=out, in0=in0, scalar1=s1, scalar2=None, op0=op0),
                 reads=reads, writes=writes)
        else:
            P.op(eng, lambda e: e.tensor_scalar(out=out, in0=in0, scalar1=s1, scalar2=s2, op0=op0, op1=op1),
                 reads=reads, writes=writes)

    def stt(out, in0, scalar, in1, op0, op1, reads, writes):
        P.op("dve", lambda e: e.scalar_tensor_tensor(out=out, in0=in0, scalar=scalar, in1=in1, op0=op0, op1=op1),
             reads=reads, writes=writes)

    def cp(out, in_, reads, writes, eng="dve"):
        if eng == "act":
            P.op("act", lambda e: e.copy(out=out, in_=in_), reads=reads, writes=writes)
        else:
            P.op(eng, lambda e: e.tensor_copy(out=out, in_=in_), reads=reads, writes=writes)

    def memset(ap, val, writes, eng="pool"):
        P.op(eng, lambda e: e.memset(ap, val), writes=writes)
        for b_ in writes:
            b_.fence = True

    BF16_ONE_PAIR = float(np.frombuffer(np.array([0x3F803F80], dtype=np.uint32).tobytes(), dtype=np.float32)[0])

    def fill_bf16(buf, val):
        v = 0.0 if val == 0.0 else BF16_ONE_PAIR
        P.op("dve", lambda e: e.memset(buf.f32, v), writes=[buf])
        buf.fence = True

    def recip(out, in_, reads, writes):
        P.op("dve", lambda e: e.reciprocal(out=out, in_=in_), reads=reads, writes=writes)

    guard = AR.f32(512, "guard")
    XR = [Buf(AR.f32(D, "xr%d" % t), "xr%d" % t) for t in range(12)]
    ident_f = Buf(AR.f32(128, "identf"))
    ident_b = Buf(AR.bf16(128, "identb"))
    tmf = Buf(AR.f32(128, "tmf"))
    tmb = Buf(AR.f32(128, "tmb"))
    ones_f = Buf(AR.f32(128, "onesf"))
    ones_b = Buf(AR.bf16(512, "onesb"))
    tmf_b = Buf(AR.bf16(128, "tmfb"))
    tmb_b = Buf(AR.bf16(128, "tmbb"))
    sel = Buf(AR.bf16(32, "sel", parts=64))
    NWB = 2
    WB = [Buf(AR.bf16(8 * 512, "wb%d" % i)) for i in range(NWB)]
    wb_i = [0]
    small = Buf(AR.f32(1024, "small"))
    sm_off = [0]

    def sm(cols):
        o = sm_off[0]
        sm_off[0] += cols
        assert sm_off[0] <= 1024
        return small.ap[:, o:o + cols]

    condT_s = sm(16)
    bmodT_s = sm(48)
    gpreT_s = sm(16)
    lngT_s = sm(4)
    lnbT_s = sm(4)
    gnT_s = sm(16)
    aconvw_s = sm(124)
    cconvw_s = sm(120)
    eps_s = sm(1)
    modT = sm(2 * 48)
    gsT = sm(2 * 16)
    cconvb_s = sm(24)
    bcs = Buf(AR.f32(1024 + 256 + 128 + 64 + 64 + 32, "bcs"))
    gq_bc = bcs.ap[:, 1024:1280]
    gkv_bc = bcs.ap[:, 1280:1408]
    dtb_bc = bcs.ap[:, 1408:1472]
    aneg_bc = bcs.ap[:, 1472:1536]
    dsk_bc = bcs.ap[:, 1536:1568]
    scT = Buf(AR.bf16(16, "scT"))
    rope_s = Buf(AR.f32(8 * 32, "rope"))

    P.dma("sp", ident_f.ap, k_ident, writes=[ident_f])
    P.dma("pool", ident_b.ap, k_ident, writes=[ident_b])
    P.dma("sp", tmf.ap, k_tmf, writes=[tmf])
    P.dma("sp", tmb.ap, k_tmb, writes=[tmb])
    cp(tmf_b.ap, tmf.ap, [tmf], [tmf_b])
    cp(tmb_b.ap, tmb.ap, [tmb], [tmb_b])
    P.dma("pool", sel.ap, k_sel, writes=[sel])
    memset(ones_f.ap, 1.0, [ones_f])
    fill_bf16(ones_b, 1.0)
    for (dst, src) in ((condT_s, condT), (bmodT_s, bmodT), (gpreT_s, gpreT), (lngT_s, a_lngT), (lnbT_s, a_lnbT),
                       (gnT_s, c_gnT), (aconvw_s, a_convwT), (cconvw_s, c_convwT), (cconvb_s, c_convbT)):
        P.dma("sp", dst, src, writes=[small])
    memset(eps_s, EPS, [small], eng="dve")
    def pbc(src):
        return src.rearrange("o n -> (o n)").partition_broadcast(128)

    for (dst, src) in ((gq_bc, a_gq), (gkv_bc, a_gkv), (dtb_bc, c_dtb), (aneg_bc, c_alog), (dsk_bc, c_dskip)):
        P.dma("sp", dst, pbc(src), writes=[bcs])
    act(aneg_bc, aneg_bc, AF.Exp, [bcs], [bcs])
    ts(aneg_bc, aneg_bc, -1.0, None, ALU.mult, None, [bcs], [bcs])
    P.dma("sp", rope_s.ap.rearrange("p (t c) -> p t c", t=8), ropecs.rearrange("(t p) c -> p t c", p=128),
          writes=[rope_s])
    for t in range(12):
        P.dma("sp", XR[t].ap, x_all[t * 128:(t + 1) * 128, :], writes=[XR[t]])

    act(scT.ap, condT_s, AF.Silu, [small], [scT])

    def load_wblock(src_ap, ncols, kchunks=8, parts=128):
        wb = WB[wb_i[0]]
        wb_i[0] = (wb_i[0] + 1) % NWB
        view = wb.ap[0:parts, 0:kchunks * ncols].rearrange("p (k n) -> p k n", k=kchunks)
        P.dma("pool", view, src_ap.rearrange("(k p) n -> p k n", p=parts), writes=[wb])
        return wb, view

    def modulation_fm(l):
        wide[0] = False
        psm = lbank(0)
        psm_v = psm.ap[:, 0:32].rearrange("p (j g) -> p j g", g=2)
        for blk in range(4):
            wb, wv = load_wblock(w_mod[l][:, blk * 512:(blk + 1) * 512], 512)
            for jj in range(4):
                j = blk * 4 + jj
                for k in range(8):
                    mm(psm_v[:, j, :], wv[:, k, jj * 128:(jj + 1) * 128],
                       scT.ap.rearrange("p (k g) -> p k g", g=2)[:, k, :],
                       k == 0, k == 7, [wb, scT], psm, inc=(k == 7 and jj == 3))
        mo = modT[:, l * 32:(l + 1) * 32].rearrange("p (j g) -> p j g", g=2)
        tt(mo, psm_v, bcast_last(bmodT_s[:, l * 24: l * 24 + 16], 2), ALU.add, [psm, small], [small])
        gs = gsT[:, l * 16:(l + 1) * 16].rearrange("p (k g) -> p k g", g=2)
        ts(gs, mo[:, 8:16, :], 1.0, None, ALU.add, None, [small], [small])
        tt(gs, gs, bcast_last(gpreT_s[:, l * 8:(l + 1) * 8], 2), ALU.mult, [small], [small])
        wide[0] = True

    def gate_bc(l, g, gt):
        AR.push()
        screp = Buf(AR.bf16(8 * 128, "screp"))
        brow = Buf(AR.f32(1024, "brow"))
        srv = screp.ap.rearrange("p (k m) -> p k m", m=128)
        cp(srv, bcast_last(scT.ap.rearrange("p (k g) -> p k g", g=2)[:, :, g], 128), [scT], [screp])
        P.dma("sp", brow.ap, pbc(bmod_row[:, l * 3072 + 2048: l * 3072 + 3072]), writes=[brow])
        for nb in range(2):
            wb, wv = load_wblock(w_mod[l][:, 2048 + nb * 512: 2048 + (nb + 1) * 512], 512)
            pg = bank()
            for k in range(8):
                mm(pg.ap, srv[:, k, :], wv[:, k, :], k == 0, k == 7, [wb, screp], pg, inc=(k == 7))
            tt(gt.ap[:, nb * 512:(nb + 1) * 512], pg.ap, brow.ap[:, nb * 512:(nb + 1) * 512], ALU.add, [pg, brow], [gt])
        AR.pop()

    def build_hT(l, tiles, g, hT):
        nt = len(tiles)
        hv = hT.ap.rearrange("p (k t) -> p k t", k=8)
        AR.push()
        xn = [Buf(AR.bf16(D, "xn%d" % i)) for i in range(4)]
        junk = Buf(AR.f32(D, "junk"))
        st = Buf(AR.f32(8, "hst"))
        for tb in range(0, nt, 4):
            for i in range(4):
                t = tiles[tb + i]
                ssq = st.ap[:, i:i + 1]
                act(junk.ap, XR[t].ap, AF.Square, [XR[t]], [junk, st], accum_out=ssq)
                act(ssq, ssq, AF.Sqrt, [st, small], [st], scale=1.0 / D, bias=eps_s)
                recip(ssq, ssq, [st], [st])
                act(xn[i].ap, XR[t].ap, AF.Identity, [XR[t], st], [xn[i]], scale=ssq)
            for k in range(8):
                ps = bank()
                psb = ps.ap.bitcast(BF16)
                for i in range(4):
                    tr(psb[:, i * 128:(i + 1) * 128], xn[i].ap[:, k * 128:(k + 1) * 128], ident_b.ap,
                       [xn[i], ident_b], ps, inc=(i == 3))
                act(hv[:, k, tb * 128:(tb + 4) * 128], psb[:, 0:512], AF.Identity, [ps, small], [hT],
                    scale=gsT[:, l * 16 + k * 2 + g: l * 16 + k * 2 + g + 1],
                    bias=modT[:, l * 32 + k * 2 + g: l * 32 + k * 2 + g + 1])
        AR.pop()

    def layer_a():
        import os
        KSUB = int(os.environ.get("KSUB", "99"))
        KGRP = int(os.environ.get("KGRP", "2"))
        l = 0
        groups = [dict(g=0, tiles=[0, 1, 2, 3], seqs=[(0, 2), (2, 2)], ctx=0),
                  dict(g=1, tiles=list(range(4, 12)), seqs=[(0, 8)], ctx=4)]
        for grp in ([groups[1]] if os.environ.get("KGSEL") == "1" else groups[:KGRP]):
            g = grp["g"]
            tiles = grp["tiles"]
            nt = len(tiles)
            Tg = nt * 128
            nctx = grp["ctx"]
            nkt = nt + nctx
            Lk = nkt * 128
            AR.push()
            hT = Buf(AR.bf16(8 * Tg, "hT"))
            build_hT(l, tiles, g, hT)
            hv = hT.ap.rearrange("p (k t) -> p k t", k=8)
            mixTa = Buf(AR.bf16(4 * Tg, "mixTa"))
            mav = mixTa.ap.rearrange("p (h t) -> p h t", h=4)
            mixTc = Buf(AR.bf16(4 * Tg, "mixTc"))
            mcv = mixTc.ap.rearrange("p (j t) -> p j t", j=4)
            if KSUB < 1:
                AR.pop()
                continue
            AR.push()
            seqs = grp["seqs"]
            Ls = seqs[0][1] * 128
            LP = Ls + 30
            cvin = Buf(AR.bf16(4 * len(seqs) * LP, "cvin"))
            cvin_f32 = AR.last_f32
            cvv = cvin.ap.rearrange("p (j s t) -> p j s t", j=4, s=len(seqs))
            sgc = Buf(AR.bf16(4 * Tg, "sgc"))
            sgv = sgc.ap.rearrange("p (j t) -> p j t", j=4)
            sig = Buf(AR.f32(512, "sig"))
            xnb = Buf(AR.bf16(4 * 512, "xnb"))
            lst = Buf(AR.f32(16, "lst"))
            abrow = Buf(AR.bf16(512, "abrow"))
            fill_bf16(abrow, 0.0)
            P.dma("pool", abrow.ap[0:1, :], a_convb, writes=[abrow])
            memset(cvin_f32, 0.0, [cvin], eng="dve")
            wga, vga = load_wblock(a_w_in[:, 416:928], 512)
            wgb, vgb = load_wblock(a_w_in[:, 928:1440], 512)
            for j in range(4):
                for nb in range(Tg // 512):
                    pa = bank()
                    pb = bank()
                    for k in range(8):
                        mm(pa.ap, vga[:, k, j * 128:(j + 1) * 128], hv[:, k, nb * 512:(nb + 1) * 512], k == 0, k == 7,
                           [wga, hT], pa, inc=(k == 7))
                    for k in range(8):
                        mm(pb.ap, vgb[:, k, j * 128:(j + 1) * 128], hv[:, k, nb * 512:(nb + 1) * 512], k == 0, k == 7,
                           [wgb, hT], pb, inc=(k == 7))
                    act(sig.ap, pb.ap, AF.Sigmoid, [pb], [sig])
                    nseq_blk = 512 // Ls if Ls < 512 else 1
                    for si in range(nseq_blk):
                        if Ls < 512:
                            s_idx = nb * nseq_blk + si
                            c0, c1, d0 = si * Ls, (si + 1) * Ls, 15
                        else:
                            s_idx = 0
                            c0, c1, d0 = 0, 512, 15 + nb * 512
                        tt(cvv[:, j, s_idx, d0:d0 + (c1 - c0)], pa.ap[:, c0:c1], sig.ap[:, c0:c1], ALU.mult,
                           [pa, sig], [cvin])
            if g == 0:
                dump("abrow", abrow.ap, [abrow])
                dump("onesb", ones_b.ap, [ones_b])
                dump("cvin", cvin.ap, [cvin])
            wgc, vgc = load_wblock(a_w_in[:, 1952:2464], 512)
            for j in range(4):
                for nb in range(Tg // 512):
                    pa = bank()
                    for k in range(8):
                        mm(pa.ap, vgc[:, k, j * 128:(j + 1) * 128], hv[:, k, nb * 512:(nb + 1) * 512], k == 0, k == 7,
                           [wgc, hT], pa, inc=(k == 7))
                    act(sgv[:, j, nb * 512:(nb + 1) * 512], pa.ap, AF.Silu, [pa], [sgc])
            dgl = []
            for j in range(4):
                dg = Buf(AR.bf16(31 * 128, "dgj%d" % j))
                dgl.append(dg)
                tt(dg.ap.rearrange("p (k m) -> p k m", k=31),
                   ident_b.ap.unsqueeze(1).to_broadcast([128, 31, 128]),
                   bcast_last(aconvw_s[:, j * 31:(j + 1) * 31], 128), ALU.mult, [ident_b, small], [dg])
            if g == 0:
                dump("dg0", dgl[0].ap, [dgl[0]])
                dump("sgc", sgc.ap, [sgc])
            csb2 = [Buf(AR.f32(512, "convsb%d" % i)) for i in range(2)]
            cjunk = Buf(AR.f32(512, "cjunk"))
            for tb in range(0, nt, 4):
                for i in range(4):
                    ti = tb + i
                    s_idx = ti // seqs[0][1]
                    t0 = (ti % seqs[0][1]) * 128
                    cs_ = csb2[ti % 2]
                    for j in range(4):
                        ps = bank()
                        dgv = dgl[j].ap.rearrange("p (k m) -> p k m", k=31)
                        for k in range(31):
                            mm(ps.ap[:, 0:128], cvv[:, j, s_idx, t0 + k:t0 + k + 128], dgv[:, k, :],
                               k == 0, False, [cvin, dgl[j]], ps, inc=False)
                        mm(ps.ap[:, 0:128], ones_b.ap[:, 0:128],
                           abrow.ap[:, j * 128:(j + 1) * 128], False, True, [ones_b, abrow], ps, inc=True)
                        cp(cs_.ap[:, j * 128:(j + 1) * 128], ps.ap[:, 0:128], [ps], [cs_], eng="act")
                    act(cjunk.ap, cs_.ap, AF.Identity, [cs_], [cjunk, lst], accum_out=lst.ap[:, 0:1])
                    act(cjunk.ap, cs_.ap, AF.Square, [cs_], [cjunk, lst], accum_out=lst.ap[:, 1:2])
                    ts(lst.ap[:, 2:4], lst.ap[:, 0:2], 1.0 / 512, None, ALU.mult, None, [lst], [lst])
                    tt(lst.ap[:, 4:5], lst.ap[:, 2:3], lst.ap[:, 2:3], ALU.mult, [lst], [lst])
                    tt(lst.ap[:, 5:6], lst.ap[:, 3:4], lst.ap[:, 4:5], ALU.subtract, [lst], [lst])
                    act(lst.ap[:, 8:9], lst.ap[:, 5:6], AF.Sqrt, [lst, small], [lst], scale=1.0, bias=eps_s)
                    recip(lst.ap[:, 8:9], lst.ap[:, 8:9], [lst], [lst])
                    ts(xnb.ap[:, i * 512:(i + 1) * 512], cs_.ap, lst.ap[:, 2:3], lst.ap[:, 8:9], ALU.subtract, ALU.mult,
                       [cs_, lst], [xnb])
                if g == 0 and tb == 0:
                    dump("xnb", xnb.ap, [xnb])
                    dump("lst", lst.ap, [lst])
                for j in range(4):
                    ps = bank()
                    psb = ps.ap.bitcast(BF16)
                    for i in range(4):
                        tr(psb[:, i * 128:(i + 1) * 128], xnb.ap[:, i * 512 + j * 128: i * 512 + (j + 1) * 128],
                           ident_b.ap, [xnb, ident_b], ps, inc=(i == 3))
                    act(sig.ap, psb[:, 0:512], AF.Silu, [ps, small], [sig], scale=lngT_s[:, j:j + 1],
                        bias=lnbT_s[:, j:j + 1])
                    tt(mcv[:, j, tb * 128:(tb + 4) * 128], sig.ap, sgv[:, j, tb * 128:(tb + 4) * 128], ALU.mult,
                       [sig, sgc], [mixTc])
            AR.pop()

            if KSUB < 2:
                AR.pop()
                continue
            q_tm = Buf(AR.bf16(nt * 768, "q_tm"))
            qv = q_tm.ap.rearrange("p (t c) -> p t c", t=nt)
            ckvT = Buf(AR.bf16(Lk, "ckvT"))
            krT = Buf(AR.bf16(Lk, "krT", parts=96))
            VA = Buf(AR.bf16(nkt * 1024, "VA"))
            vav = VA.ap.rearrange("p (t h c) -> p t h c", t=nkt, h=8)
            wsm = Buf(AR.bf16(2 * 768 + 512 + 512, "wsm"))
            wuq = wsm.ap[:, 0:1536].rearrange("p (k n) -> p k n", k=2)
            wuk = wsm.ap[:, 1536:2048]
            wuv = wsm.ap[:, 2048:2560]
            P.dma("pool", wuq, a_w_uq.rearrange("(k p) n -> p k n", p=128), writes=[wsm])
            P.dma("pool", wuk, a_w_uk, writes=[wsm])
            P.dma("pool", wuv, a_w_uv, writes=[wsm])
            fill_bf16(VA, 1.0)

            AR.push()
            wb, wv = load_wblock(a_w_in[:, 0:416], 416)
            qnT = Buf(AR.bf16(2 * Tg, "qnT"))
            qnv = qnT.ap.rearrange("p (k t) -> p k t", k=2)
            tmp = Buf(AR.f32(512, "p2tmp"))
            st = Buf(AR.f32(8, "p2st"))
            qn = Buf(AR.bf16(256, "qn"))
            ckv_f = Buf(AR.f32(128, "ckv_f"))
            ckv_b = Buf(AR.bf16(128, "ckv_b"))
            kst = Buf(AR.bf16(96, "kst"))
            kr_f = Buf(AR.f32(32, "kr_f"))
            rt = Buf(AR.f32(64, "ropetmp"))
            qfb = [Buf(AR.f32(384, "qf%d" % i)) for i in range(2)]
            fill_bf16(kst, 0.0)

            def k_transposes(kt_idx, ckv_b, kst):
                ps = bank()
                psb = ps.ap.bitcast(BF16)
                tr(psb[:, 0:128], ckv_b.ap, ident_b.ap, [ckv_b, ident_b], ps, inc=False)
                tr(psb[0:96, 128:256], kst.ap, ident_b.ap, [kst, ident_b], ps, inc=True)
                cp(ckvT.ap[:, kt_idx * 128:(kt_idx + 1) * 128], psb[:, 0:128], [ps], [ckvT], eng="act")
                cp(krT.ap[64:96, kt_idx * 128:(kt_idx + 1) * 128], psb[64:96, 128:256], [ps], [krT], eng="act")

            for c in range(nctx):
                P.dma("sp", ckv_f.ap, cckv[c * 128:(c + 1) * 128, :], writes=[ckv_f])
                P.dma("sp", kr_f.ap, ckr[c * 128:(c + 1) * 128, :], writes=[kr_f])
                cp(ckv_b.ap, ckv_f.ap, [ckv_f], [ckv_b])
                cp(kst.ap[:, 64:96], kr_f.ap, [kr_f], [kst])
                k_transposes(c, ckv_b, kst)

            KP2 = int(os.environ.get("KP2", "99"))
            for ti in range(nt if KP2 >= 2 else 0):
                ps = bank()
                for k in range(8):
                    mm(ps.ap[:, 0:416], hv[:, k, ti * 128:(ti + 1) * 128], wv[:, k, :], k == 0, k == 7,
                       [hT, wb], ps, inc=(k == 7))
                act(tmp.ap[:, 0:256], ps.ap[:, 0:256], AF.Square, [ps], [tmp, st], accum_out=st.ap[:, 0:1])
                act(tmp.ap[:, 256:384], ps.ap[:, 256:384], AF.Square, [ps], [tmp, st], accum_out=st.ap[:, 1:2])
                act(st.ap[:, 0:1], st.ap[:, 0:1], AF.Sqrt, [st, small], [st], scale=1.0 / 256, bias=eps_s)
                act(st.ap[:, 1:2], st.ap[:, 1:2], AF.Sqrt, [st, small], [st], scale=1.0 / 128, bias=eps_s)
                recip(st.ap[:, 0:2], st.ap[:, 0:2], [st], [st])
                stt(qn.ap, ps.ap[:, 0:256], st.ap[:, 0:1], gq_bc, ALU.mult, ALU.mult, [ps, st, bcs], [qn])
                stt(ckv_f.ap, ps.ap[:, 256:384], st.ap[:, 1:2], gkv_bc, ALU.mult, ALU.mult, [ps, st, bcs], [ckv_f])
                cp(ckv_b.ap, ckv_f.ap, [ckv_f], [ckv_b])
                if g == 0:
                    cp(kr_f.ap, ps.ap[:, 384:416], [ps], [kr_f], eng="act")
                    P.dma("sp", o_ckv[ti * 128:(ti + 1) * 128, :], ckv_f.ap, reads=[ckv_f], is_output=True)
                    P.dma("sp", o_kr[ti * 128:(ti + 1) * 128, :], kr_f.ap, reads=[kr_f], is_output=True)
                    cp(kst.ap[:, 64:96], ps.ap[:, 384:416], [ps], [kst])
                else:
                    cs = rope_s.ap.rearrange("p (t c) -> p t c", t=8)[:, ti, :]
                    cosv, sinv = cs[:, 0:16], cs[:, 16:32]
                    kp = ps.ap[:, 384:416].rearrange("p (i two) -> p i two", two=2)
                    ko = kst.ap[:, 64:96].rearrange("p (i two) -> p i two", two=2)
                    tt(rt.ap[:, 0:16], kp[:, :, 0], cosv, ALU.mult, [ps, rope_s], [rt])
                    tt(rt.ap[:, 16:32], kp[:, :, 1], sinv, ALU.mult, [ps, rope_s], [rt])
                    tt(rt.ap[:, 32:48], kp[:, :, 0], sinv, ALU.mult, [ps, rope_s], [rt])
                    tt(rt.ap[:, 48:64], kp[:, :, 1], cosv, ALU.mult, [ps, rope_s], [rt])
                    tt(ko[:, :, 0], rt.ap[:, 0:16], rt.ap[:, 16:32], ALU.subtract, [rt], [kst])
                    tt(ko[:, :, 1], rt.ap[:, 32:48], rt.ap[:, 48:64], ALU.add, [rt], [kst])
                k_transposes(nctx + ti, ckv_b, kst)
                ps2 = bank()
                ps2b = ps2.ap.bitcast(BF16)
                for k in range(2):
                    tr(ps2b[:, k * 128:(k + 1) * 128], qn.ap[:, k * 128:(k + 1) * 128], ident_b.ap, [qn, ident_b], ps2,
                       inc=(k == 1))
                cp(qnv[:, :, ti * 128:(ti + 1) * 128], ps2b[:, 0:256].rearrange("p (k t) -> p k t", k=2), [ps2], [qnT],
                   eng="act")
            for kt in range(nkt if KP2 >= 3 else 0):
                ps = bank()
                mm(ps.ap, ckvT.ap[:, kt * 128:(kt + 1) * 128], wuv, True, True, [ckvT, wsm], ps, inc=True)
                pvv = ps.ap.rearrange("p (j two c) -> p j two c", j=4, two=2)
                vv5 = VA.ap.rearrange("p (t j two c) -> p t j two c", t=nkt, j=4, two=2)
                cp(vv5[:, kt, :, 0, 0:64], pvv[:, :, 0, :], [ps], [VA], eng="act")
                cp(vv5[:, kt, :, 1, 64:128], pvv[:, :, 1, :], [ps], [VA], eng="act")
            for ti in range(nt if KP2 >= 4 else 0):
                for half in range(2):
                    ps = bank()
                    for k in range(2):
                        mm(ps.ap[:, 0:384], qnv[:, k, ti * 128:(ti + 1) * 128], wuq[:, k, half * 384:(half + 1) * 384],
                           k == 0, k == 1, [qnT, wsm], ps, inc=(k == 1))
                    pv = ps.ap[:, 0:384].rearrange("p (h c) -> p h c", h=4)
                    qo = qv[:, ti, half * 384:(half + 1) * 384].rearrange("p (h c) -> p h c", h=4)
                    if g == 0:
                        act(qo, pv, AF.Copy, [ps], [q_tm], scale=ATTN_SCALE)
                    else:
                        qf = qfb[(ti * 2 + half) % 2]
                        act(qf.ap, ps.ap[:, 0:384], AF.Copy, [ps], [qf], scale=ATTN_SCALE)
                        qfv = qf.ap.rearrange("p (h c) -> p h c", h=4)
                        cp(qo[:, :, 0:64], qfv[:, :, 0:64], [qf], [q_tm])
                        cs = rope_s.ap.rearrange("p (t c) -> p t c", t=8)[:, ti, :]
                        cosv, sinv = cs[:, 0:16], cs[:, 16:32]
                        cosb = cosv.unsqueeze(1).to_broadcast([128, 4, 16])
                        sinb = sinv.unsqueeze(1).to_broadcast([128, 4, 16])
                        pr = qfv[:, :, 64:96].rearrange("p h (i two) -> p h i two", two=2)
                        qr = qo[:, :, 64:96].rearrange("p h (i two) -> p h i two", two=2)
                        rr = tmp.ap[:, 0:256].rearrange("p (a h i) -> p a h i", a=4, h=4)
                        tt(rr[:, 0], pr[:, :, :, 0], cosb, ALU.mult, [qf, rope_s], [tmp])
                        tt(rr[:, 1], pr[:, :, :, 1], sinb, ALU.mult, [qf, rope_s], [tmp])
                        tt(rr[:, 2], pr[:, :, :, 0], sinb, ALU.mult, [qf, rope_s], [tmp])
                        tt(rr[:, 3], pr[:, :, :, 1], cosb, ALU.mult, [qf, rope_s], [tmp])
                        tt(qr[:, :, :, 0], rr[:, 0], rr[:, 1], ALU.subtract, [tmp], [q_tm])
                        tt(qr[:, :, :, 1], rr[:, 2], rr[:, 3], ALU.add, [tmp], [q_tm])
            AR.pop()

            if KSUB < 3:
                AR.pop()
                continue
            AR.push()
            wgt, vgt = load_wblock(a_w_in[:, 1440:1952], 512)
            QT = [Buf(AR.bf16(Tg, "QT%d" % i, parts=96)) for i in range(2)]
            KT = [Buf(AR.bf16(Lk, "KT%d" % i, parts=96)) for i in range(2)]
            sgT = [Buf(AR.bf16(Tg, "sgT%d" % i)) for i in range(2)]
            PT = [Buf(AR.bf16(512, "PT%d" % i)) for i in range(3)]
            Rr = [Buf(AR.f32(512, "Rr%d" % i)) for i in range(2)]
            t1 = [Buf(AR.f32(512, "t1%d" % i)) for i in range(2)]
            pti = 0
            poi = [0]
            wide[0] = False
            for h in range(8):
                qt, ktb, sg = QT[h % 2], KT[h % 2], sgT[(h // 2) % 2]
                vlo, vhi = (0, 64) if h % 2 == 0 else (64, 128)
                dlo, dhi = (64, 128) if h % 2 == 0 else (0, 64)
                for tb in range(0, nt, 8):
                    nn = min(8, nt - tb)
                    ps = bank()
                    psb = ps.ap.bitcast(BF16)
                    for i in range(nn):
                        tr(psb[0:96, i * 128:(i + 1) * 128], qv[:, tb + i, h * 96:(h + 1) * 96], ident_b.ap,
                           [q_tm, ident_b], ps, inc=(i == nn - 1))
                    cp(qt.ap[:, tb * 128:(tb + nn) * 128], psb[0:96, 0:nn * 128], [ps], [qt], eng="act")
                for nb in range(0, Lk, 512):
                    w = min(512, Lk - nb)
                    ps = bank()
                    mm(ps.ap[0:64, 0:w], wuk[:, h * 64:(h + 1) * 64], ckvT.ap[:, nb:nb + w], True, True, [wsm, ckvT], ps,
                       inc=True)
                    cp(ktb.ap[0:64, nb:nb + w], ps.ap[0:64, 0:w], [ps], [ktb], eng="act")
                cp(ktb.ap[64:96, :], krT.ap[64:96, :], [krT], [ktb])
                for nb in range(Tg // 512 if h % 2 == 0 else 0):
                    ps = bank()
                    for k in range(8):
                        mm(ps.ap, vgt[:, k, h * 64:(h + 2) * 64], hv[:, k, nb * 512:(nb + 1) * 512], k == 0,
                           k == 7, [wgt, hT], ps, inc=(k == 7))
                    act(sg.ap[:, nb * 512:(nb + 1) * 512], ps.ap, AF.Silu, [ps], [sg])
                for (s0, sl) in seqs:
                    q0 = s0 * 128
                    qlen = sl * 128
                    if g == 0:
                        ktiles = list(range(s0, s0 + sl))
                    else:
                        ktiles = list(range(nkt))
                    for qb in range(0, qlen, 512):
                        w = min(512, qlen - qb)
                        po = lbank(poi[0])
                        poi[0] += 1
                        for ki, kt in enumerate(ktiles):
                            ps = bank()
                            mm(ps.ap[:, 0:w], ktb.ap[0:96, kt * 128:(kt + 1) * 128], qt.ap[0:96, q0 + qb:q0 + qb + w],
                               True, True, [ktb, qt], ps, inc=True)
                            pt = PT[pti % 3]
                            pti += 1
                            act(pt.ap[:, 0:w], ps.ap[:, 0:w], AF.Exp, [ps], [pt])
                            mm(po.ap[:, 0:w], vav[:, kt, h, :], pt.ap[:, 0:w], ki == 0, ki == len(ktiles) - 1, [VA, pt], po,
                               inc=(ki == len(ktiles) - 1))
                        rr = Rr[h % 2]
                        tt1 = t1[h % 2]
                        recip(rr.ap[dlo:dhi, 0:w], po.ap[dlo:dhi, 0:w], [po], [rr])
                        tt(tt1.ap[vlo:vhi, 0:w], po.ap[vlo:vhi, 0:w], rr.ap[dlo:dhi, 0:w], ALU.mult, [po, rr], [tt1])
                        tt(mav[vlo:vhi, h // 2, q0 + qb:q0 + qb + w], tt1.ap[vlo:vhi, 0:w],
                           sg.ap[vlo:vhi, q0 + qb:q0 + qb + w], ALU.mult, [tt1, sg], [mixTa])
            AR.pop()

            wide[0] = True
            if KSUB < 4:
                AR.pop()
                continue
            AR.push()
            otmp = [Buf(AR.f32(512, "otmp%d" % i)) for i in range(2)]
            gt = Buf(AR.f32(D, "gt_a"))
            gate_bc(l, g, gt)
            dump("mixTa%d" % g, mixTa.ap, [mixTa])
            dump("mixTc%d" % g, mixTc.ap, [mixTc])
            dump("gt%d" % g, gt.ap, [gt])
            dump("hT%d" % g, hT.ap, [hT])
            for nb in range(2):
                wo_b, wo = load_wblock(a_w_out[:, nb * 512:(nb + 1) * 512], 512)
                for ti in range(nt):
                    ps = bank()
                    for c in range(8):
                        lhs = mav[:, c, ti * 128:(ti + 1) * 128] if c < 4 else mcv[:, c - 4, ti * 128:(ti + 1) * 128]
                        mm(ps.ap, lhs, wo[:, c, :], c == 0, c == 7, [mixTa, mixTc, wo_b], ps, inc=(c == 7))
                    ot = otmp[ti % 2]
                    xr = XR[tiles[ti]]
                    tt(ot.ap, ps.ap, gt.ap[:, nb * 512:(nb + 1) * 512], ALU.mult, [ps, gt], [ot])
                    if ti == 0 and g == 0:
                        dump("ot%d" % nb, ot.ap, [ot])
                        dump("xrpre%d" % nb, xr.ap[:, nb * 512:(nb + 1) * 512], [xr])
                    tt(xr.ap[:, nb * 512:(nb + 1) * 512], xr.ap[:, nb * 512:(nb + 1) * 512], ot.ap, ALU.add, [xr, ot],
                       [xr], eng="dve")
            AR.pop()
            AR.pop()

    def layer_c():
        dump("xr0_afterA", XR[0].ap, [XR[0]])
        l = 1
        HP = 4
        CW = HP * 64
        groups = [dict(g=0, tiles=[0, 1, 2, 3], seqs=[(0, 2), (2, 2)]),
                  dict(g=1, tiles=list(range(4, 12)), seqs=[(0, 8)])]
        wdt_b = Buf(AR.bf16(8 * 64, "wdt"))
        wdtv = wdt_b.ap.rearrange("p (k n) -> p k n", k=8)
        P.dma("pool", wdtv, c_w_in[:, 5120:5184].rearrange("(k p) n -> p k n", p=128), writes=[wdt_b])
        selv = sel.ap
        import os
        KCG = int(os.environ.get("KCG", "2"))
        KCPASS = int(os.environ.get("KCPASS", "8"))
        KCSCAN = int(os.environ.get("KCSCAN", "1"))
        KCOUT = int(os.environ.get("KCOUT", "1"))
        def issue_pass_loads(hq_):
            sg_ = hq_ // 2
            wzx_ = WB[wb_i[0]]
            wb_i[0] = (wb_i[0] + 1) % NWB
            vzx_ = wzx_.ap[:, 0:8 * 512].rearrange("p (k n) -> p k n", k=8)
            P.dma("pool", vzx_[:, :, 0:CW], c_w_in[:, hq_ * CW:(hq_ + 1) * CW].rearrange("(k p) n -> p k n", p=128),
                  writes=[wzx_])
            P.dma("pool", vzx_[:, :, CW:2 * CW],
                  c_w_in[:, 2048 + hq_ * CW: 2048 + (hq_ + 1) * CW].rearrange("(k p) n -> p k n", p=128), writes=[wzx_])
            wbc_ = WB[wb_i[0]]
            wb_i[0] = (wb_i[0] + 1) % NWB
            vbc_ = wbc_.ap[:, 0:8 * 256].rearrange("p (k n) -> p k n", k=8)
            P.dma("pool", vbc_[:, :, 0:128],
                  c_w_in[:, 4096 + sg_ * 128: 4096 + (sg_ + 1) * 128].rearrange("(k p) n -> p k n", p=128), writes=[wbc_])
            P.dma("pool", vbc_[:, :, 128:256],
                  c_w_in[:, 4608 + sg_ * 128: 4608 + (sg_ + 1) * 128].rearrange("(k p) n -> p k n", p=128), writes=[wbc_])
            return (wzx_, vzx_, wbc_, vbc_)

        for grp in groups[:KCG]:
            g = grp["g"]
            tiles = grp["tiles"]
            seqs = grp["seqs"]
            nt = len(tiles)
            Tg = nt * 128
            AR.push()
            nxt_w = [issue_pass_loads(0) if KCPASS > 0 else None]
            hT = Buf(AR.bf16(8 * Tg, "hT"))
            build_hT(l, tiles, g, hT)
            hv = hT.ap.rearrange("p (k t) -> p k t", k=8)
            ygT = Buf(AR.bf16(16 * Tg, "ygT"))
            ygv = ygT.ap.rearrange("p (c t) -> p c t", c=16)
            A_tm = Buf(AR.f32(nt * 64, "A_tm"))
            eA = Buf(AR.f32(nt * 64, "eA"))
            cdec = Buf(AR.f32(nt * 64, "cdec"))
            wdt = Buf(AR.f32(nt * 64, "wdt_s"))
            wde = Buf(AR.f32(nt * 64, "wde_s"))
            ssq = Buf(AR.f32(nt * 8, "ssq"))
            Apc = [Buf(AR.bf16(Tg, "Apc%d" % d, parts=64)) for d in range(2)]
            v3 = lambda b_: b_.ap.rearrange("p (t c) -> p t c", t=nt)
            AR.push()
            a_s = Buf(AR.f32(nt * 64, "a_s"))
            a_res = Buf(AR.f32(nt * 64, "a_res"))
            a_hl = Buf(AR.bf16(nt * 128, "a_hl"))
            ahv = a_hl.ap.rearrange("p (t q c) -> p t q c", t=nt, q=2)
            arep = Buf(AR.bf16(nt * 256, "arep"))
            arv = arep.ap.rearrange("p (t d q r c) -> p t d q r c", t=nt, d=2, q=2, r=2)
            csb = Buf(AR.f32(128, "csb"))
            tmpd = Buf(AR.f32(64, "tmpd"))
            r1 = Buf(AR.f32(512, "r1", parts=64))
            for ti in range(nt):
                ps = bank()
                for k in range(8):
                    mm(ps.ap[:, 0:64], hv[:, k, ti * 128:(ti + 1) * 128], wdtv[:, k, :], k == 0, k == 7, [hT, wdt_b], ps,
                       inc=(k == 7))
                tt(tmpd.ap, ps.ap[:, 0:64], dtb_bc, ALU.add, [ps, bcs], [tmpd])
                act(tmpd.ap, tmpd.ap, AF.Exp, [tmpd], [tmpd])
                act(v3(wdt)[:, ti, :], tmpd.ap, AF.Ln, [tmpd], [wdt], bias=1.0)
                tt(v3(a_s)[:, ti, :], v3(wdt)[:, ti, :], aneg_bc, ALU.mult, [wdt, bcs], [a_s])
            a3 = a_s.ap.rearrange("p (t c) -> p t c", t=nt)
            cp(ahv[:, :, 0, :], a3, [a_s], [a_hl])
            tt(a_res.ap.rearrange("p (t c) -> p t c", t=nt), a3, ahv[:, :, 0, :], ALU.subtract, [a_s, a_hl], [a_res])
            cp(ahv[:, :, 1, :], a_res.ap.rearrange("p (t c) -> p t c", t=nt), [a_res], [a_hl])
            for q in range(2):
                for r_i in range(2):
                    cp(arv[:, :, :, q, r_i, :], ahv[:, :, q, :].rearrange("p t (d c) -> p t d c", d=2), [a_hl], [arep])
            for ti in range(nt):
                ps = bank()
                for q in range(2):
                    mm(ps.ap[:, 0:32], tmf_b.ap, ahv[:, ti, q, 0:32], q == 0, q == 1, [tmf_b, a_hl], ps, inc=False)
                for q in range(2):
                    mm(ps.ap[:, 32:64], tmb_b.ap, ahv[:, ti, q, 32:64], q == 0, q == 1, [tmb_b, a_hl], ps, inc=False)
                for q in range(2):
                    mm(ps.ap[:, 64:128], ones_b.ap[:, 0:128], ahv[:, ti, q, :], q == 0, q == 1, [ones_b, a_hl], ps,
                       inc=(q == 1))
                cp(csb.ap, ps.ap[:, 0:128], [ps], [csb])
                cp(v3(A_tm)[:, ti, :], csb.ap[:, 0:64], [csb], [A_tm], eng="pool")
                act(v3(eA)[:, ti, :], csb.ap[:, 0:64], AF.Exp, [csb], [eA])
                act(v3(cdec)[:, ti, :], csb.ap[:, 64:128], AF.Exp, [csb], [cdec])
                tt(tmpd.ap, csb.ap[:, 64:128], csb.ap[:, 0:64], ALU.subtract, [csb], [tmpd])
                act(tmpd.ap, tmpd.ap, AF.Exp, [tmpd], [tmpd])
                tt(v3(wde)[:, ti, :], tmpd.ap, v3(wdt)[:, ti, :], ALU.mult, [tmpd, wdt], [wde])
            for d in range(2):
                tmd = tmf_b if d == 0 else tmb_b
                for tb in range(0, nt, 4):
                    ps = bank()
                    for i in range(4):
                        for q in range(2):
                            mm(ps.ap[0:64, i * 128:(i + 1) * 128],
                               arv[:, tb + i, d, q, :, :].rearrange("p r c -> p (r c)"), tmd.ap, q == 0, q == 1,
                               [arep, tmd], ps, inc=(i == 3 and q == 1))
                    dst = Apc[d].ap[:, tb * 128:(tb + 4) * 128]
                    cp(dst, ps.ap[0:64, :], [ps], [Apc[d]])
                    tt(r1.ap[32:64, :], ps.ap[32:64, :], dst[32:64, :], ALU.subtract, [ps, Apc[d]], [r1])
                    cp(dst[32:64, :], r1.ap[32:64, :], [r1], [Apc[d]])
            AR.pop()

            for hq in range(KCPASS):
                sg = hq // 2
                h0 = hq * HP
                AR.push()
                sz = Buf(AR.bf16(nt * CW, "sz"))
                xs = Buf(AR.bf16(nt * CW, "xs"))
                Btm = Buf(AR.bf16(nt * 128, "Btm"))
                BT = Buf(AR.bf16(Tg, "BT"))
                CT = Buf(AR.bf16(Tg, "CT"))
                cbm = [Buf(AR.bf16(Tg, "cbm%d" % d)) for d in range(2)]
                prevb = Buf(AR.bf16(nt * CW, "prevb"))
                Sst = [Buf(AR.f32(CW, "S%d" % d)) for d in range(2)]
                brow = Buf(AR.bf16(CW + 128, "browc"))
                fill_bf16(brow, 0.0)
                Ls = seqs[0][1] * 128
                LP = Ls + 4
                raw = [Buf(AR.bf16(len(seqs) * LP, "raw%d" % i)) for i in range(2)]
                dg = [Buf(AR.bf16(5 * 128, "dgc%d" % i)) for i in range(2)]
                v4 = lambda b_: b_.ap.rearrange("p (t c) -> p t c", t=nt)
                for r_ in raw:
                    fill_bf16(r_, 0.0)
                P.dma("pool", brow.ap[0:1, 0:CW], c_convb[:, hq * CW:(hq + 1) * CW], writes=[brow])
                P.dma("pool", brow.ap[0:1, CW:CW + 128], c_convb[:, 2048 + sg * 128: 2048 + (sg + 1) * 128], writes=[brow])
                if nxt_w[0] is None:
                    nxt_w[0] = issue_pass_loads(hq)
                wzx, vzx, wbc_b, vbc = nxt_w[0]
                nxt_w[0] = None
                for ti in range(nt):
                    ps = bank()
                    for k in range(8):
                        mm(ps.ap[:, 0:CW], hv[:, k, ti * 128:(ti + 1) * 128], vzx[:, k, 0:CW], k == 0, k == 7, [hT, wzx], ps,
                           inc=(k == 7))
                    act(v4(sz)[:, ti, :], ps.ap[:, 0:CW], AF.Silu, [ps], [sz])
                chunk_specs = [("x", wzx, vzx[:, :, CW + j * 128: CW + (j + 1) * 128], hq * 2 + j, j) for j in range(2)]
                chunk_specs.append(("B", wbc_b, vbc[:, :, 0:128], 16 + sg, 0))
                chunk_specs.append(("C", wbc_b, vbc[:, :, 128:256], 20 + sg, 0))
                for ci, (kind, wbuf, wview, cch, j) in enumerate(chunk_specs):
                    rw = raw[ci % 2]
                    rv = rw.ap.rearrange("p (s t) -> p s t", s=len(seqs))
                    dgc = dg[ci % 2]
                    dgv = dgc.ap.rearrange("p (k m) -> p k m", k=5)
                    tt(dgv, ident_b.ap.unsqueeze(1).to_broadcast([128, 5, 128]),
                       bcast_last(cconvw_s[:, cch * 5:(cch + 1) * 5], 128), ALU.mult, [ident_b, small], [dgc])
                    for nb in range(Tg // 512):
                        ps = bank()
                        for k in range(8):
                            mm(ps.ap, wview[:, k, :], hv[:, k, nb * 512:(nb + 1) * 512], k == 0, k == 7, [wbuf, hT], ps,
                               inc=(k == 7))
                        if Ls < 512:
                            for si in range(512 // Ls):
                                cp(rv[:, nb * (512 // Ls) + si, 2:2 + Ls], ps.ap[:, si * Ls:(si + 1) * Ls], [ps], [rw],
                                   eng="act")
                        else:
                            cp(rv[:, 0, 2 + nb * 512: 2 + (nb + 1) * 512], ps.ap, [ps], [rw], eng="act")
                    if kind in ("x", "B"):
                        bias_row = brow.ap[:, j * 128:(j + 1) * 128] if kind == "x" else brow.ap[:, CW:CW + 128]
                        for tb in range(0, nt, 4):
                            ps = bank()
                            for i in range(4):
                                ti = tb + i
                                s_idx = ti // seqs[0][1]
                                t0 = (ti % seqs[0][1]) * 128
                                pso = ps.ap[:, i * 128:(i + 1) * 128]
                                for k in range(5):
                                    mm(pso, rv[:, s_idx, t0 + k:t0 + k + 128], dgv[:, k, :], k == 0, False, [rw, dgc], ps,
                                       inc=False)
                                mm(pso, ones_b.ap[:, 0:128], bias_row, False, True, [ones_b, brow], ps, inc=(i == 3))
                            pin = ps.ap.rearrange("p (i c) -> p i c", i=4)
                            if kind == "x":
                                act(v4(xs)[:, tb:tb + 4, j * 128:(j + 1) * 128], pin, AF.Silu, [ps], [xs])
                            else:
                                act(v4(Btm)[:, tb:tb + 4, :], pin, AF.Silu, [ps], [Btm])
                    if kind in ("B", "C"):
                        dstT = BT if kind == "B" else CT
                        for si in range(len(seqs)):
                            for nb in range(0, Ls, 512):
                                w = min(512, Ls - nb)
                                ps = bank()
                                for k in range(5):
                                    mm(ps.ap[:, 0:w], dgv[:, k, :], rv[:, si, nb + k:nb + k + w], k == 0, k == 4, [dgc, rw],
                                       ps, inc=(k == 4))
                                act(dstT.ap[:, si * Ls + nb: si * Ls + nb + w], ps.ap[:, 0:w], AF.Silu, [ps, small], [dstT],
                                    bias=cconvb_s[:, cch:cch + 1])
                if hq + 1 < KCPASS:
                    nxt_w[0] = issue_pass_loads(hq + 1)
                for tb in range(0, nt, 4):
                    ps = bank()
                    for i in range(4):
                        ti = tb + i
                        mm(ps.ap[:, i * 128:(i + 1) * 128], BT.ap[:, ti * 128:(ti + 1) * 128],
                           CT.ap[:, ti * 128:(ti + 1) * 128], True, True, [BT, CT], ps, inc=(i == 3))
                    pv = ps.ap.rearrange("p (i l) -> p i l", i=4)
                    for d in range(2):
                        tmd = tmf if d == 0 else tmb
                        tt(cbm[d].ap[:, tb * 128:(tb + 4) * 128].rearrange("p (i l) -> p i l", i=4), pv,
                           tmd.ap.unsqueeze(1).to_broadcast([128, 4, 128]), ALU.mult, [ps, tmd], [cbm[d]])
                if not KCSCAN:
                    AR.pop()
                    continue
                AR.push()
                arg = [Buf(AR.f32(512, "arg%d" % i)) for i in range(2)]
                Dm = [Buf(AR.bf16(512, "Dm%d" % i)) for i in range(2)]
                Mm = [Buf(AR.bf16(512, "Mm%d" % i)) for i in range(2)]
                xdt = Buf(AR.bf16(4 * 2 * CW, "xdt"))
                xde = [Buf(AR.bf16(CW, "xde%d" % i)) for i in range(2)]
                tfB = Buf(AR.f32(4 * CW, "tfB"))
                tbB = Buf(AR.f32(4 * CW, "tbB"))
                ytB = Buf(AR.f32(4 * CW, "ytB"))
                ygB = Buf(AR.bf16(4 * CW, "ygB"))
                AR_ygB_f32 = ygB.f32[:, 0:CW]
                xdeB = Buf(AR.bf16(4 * CW, "xdeB"))
                SbfB = [Buf(AR.bf16(CW, "SbfB%d" % i)) for i in range(4)]
                stmp = arg[0]
                stmp_ap = arg[0].ap[:, 0:CW]
                stmp2b = ygB
                stmp2_ap = AR_ygB_f32
                cnt = [0]
                hview = lambda ap_: ap_.rearrange("p (h q) -> p h q", h=HP)

                def scaled_x(dst_ap, dst_buf, ti, wsrc, d):
                    tt(hview(dst_ap), hview(v4(xs)[:, ti, :]),
                       bcast_last(v3(wsrc)[:, ti, d * 32 + h0: d * 32 + h0 + HP], 64), ALU.mult,
                       [xs, wsrc], [dst_buf])

                def state_step(d, ti):
                    xd = xde[cnt[0] % 2]
                    cnt[0] += 1
                    scaled_x(xd.ap, xd, ti, wde, d)
                    ps = bank()
                    mm(ps.ap[:, 0:CW], v4(Btm)[:, ti, :], xd.ap, True, True, [Btm, xd], ps, inc=True)
                    tt(hview(stmp_ap), hview(Sst[d].ap),
                       bcast_last(v3(cdec)[:, ti, d * 32 + h0: d * 32 + h0 + HP], 64), ALU.mult,
                       [Sst[d], cdec], [stmp])
                    tt(Sst[d].ap, stmp_ap, ps.ap[:, 0:CW], ALU.add, [stmp, ps], [Sst[d]])

                for si, (s0, sl) in enumerate(seqs):
                    for d in range(2):
                        if g == 1:
                            P.dma("sp", Sst[d].ap, st0[d][:, hq * CW:(hq + 1) * CW], writes=[Sst[d]])
                        else:
                            memset(Sst[d].ap, 0.0, [Sst[d]], eng="pool")
                    for ti in range(s0 + sl - 1, s0 - 1, -1):
                        cp(v4(prevb)[:, ti, :], Sst[1].ap, [Sst[1]], [prevb], eng="act")
                        state_step(1, ti)
                    if g == 0:
                        P.dma("sp", o_st[si][1][:, hq * CW:(hq + 1) * CW], Sst[1].ap, reads=[Sst[1]], is_output=True)
                    wide[0] = False
                    for tb in range(s0, s0 + sl, 4):
                        nb4 = min(4, s0 + sl - tb)
                        W = nb4 * 128
                        xbv = xdt.ap.rearrange("p (i d c) -> p i d c", i=4, d=2)
                        for d in range(2):
                            tt(xbv[:, 0:nb4, d, :].rearrange("p i (h q) -> p i h q", h=HP),
                               v4(xs)[:, tb:tb + nb4, :].rearrange("p i (h q) -> p i h q", h=HP),
                               bcast_last(v3(wdt)[:, tb:tb + nb4, d * 32 + h0: d * 32 + h0 + HP], 64), ALU.mult,
                               [xs, wdt], [xdt])
                        npair = nb4 // 2
                        yps = [lbank(pr) for pr in range(npair)]
                        xdv = xdeB.ap.rearrange("p (i c) -> p i c", i=4)
                        tt(xdv[:, 0:nb4, :].rearrange("p i (h q) -> p i h q", h=HP),
                           v4(xs)[:, tb:tb + nb4, :].rearrange("p i (h q) -> p i h q", h=HP),
                           bcast_last(v3(wde)[:, tb:tb + nb4, h0:h0 + HP], 64), ALU.mult, [xs, wde], [xdeB])
                        tfv = tfB.ap.rearrange("p (i c) -> p i c", i=4)
                        tbv = tbB.ap.rearrange("p (i c) -> p i c", i=4)

                        def rec_step(i):
                            ti = tb + i
                            cp(SbfB[i].ap, Sst[0].ap, [Sst[0]], [SbfB[i]], eng="act")
                            ps = bank()
                            mm(ps.ap[:, 0:CW], v4(Btm)[:, ti, :], xdv[:, i, :], True, True, [Btm, xdeB], ps, inc=True)
                            tt(hview(stmp2_ap), hview(Sst[0].ap),
                               bcast_last(v3(cdec)[:, ti, h0: h0 + HP], 64), ALU.mult, [Sst[0], cdec], [stmp2b])
                            tt(Sst[0].ap, stmp2_ap, ps.ap[:, 0:CW], ALU.add, [stmp2b, ps], [Sst[0]])

                        def yoff_pair(pr):
                            pf = bank()
                            pbk = bank()
                            for u in range(2):
                                ti = tb + pr * 2 + u
                                mm(pf.ap[:, u * CW:(u + 1) * CW], CT.ap[:, ti * 128:(ti + 1) * 128], SbfB[pr * 2 + u].ap,
                                   True, True, [CT, SbfB[pr * 2 + u]], pf, inc=(u == 1))
                            for u in range(2):
                                ti = tb + pr * 2 + u
                                mm(pbk.ap[:, u * CW:(u + 1) * CW], CT.ap[:, ti * 128:(ti + 1) * 128], v4(prevb)[:, ti, :],
                                   True, True, [CT, prevb], pbk, inc=(u == 1))
                            t0_ = tb + pr * 2
                            tt(tfv[:, pr * 2:pr * 2 + 2, :].rearrange("p i (h q) -> p i h q", h=HP),
                               pf.ap.rearrange("p (i h q) -> p i h q", i=2, h=HP),
                               bcast_last(v3(eA)[:, t0_:t0_ + 2, h0:h0 + HP], 64), ALU.mult, [pf, eA], [tfB])
                            tt(tbv[:, pr * 2:pr * 2 + 2, :].rearrange("p i (h q) -> p i h q", h=HP),
                               pbk.ap.rearrange("p (i h q) -> p i h q", i=2, h=HP),
                               bcast_last(v3(eA)[:, t0_:t0_ + 2, 32 + h0:32 + h0 + HP], 64), ALU.mult, [pbk, eA], [tbB])

                        for hh in range(HP):
                            hgl = h0 + hh
                            for d in range(2):
                                ps = bank()
                                mm(ps.ap[:, 0:W], bcast_last(selv[:, hgl], 128),
                                   Apc[d].ap[:, tb * 128: tb * 128 + W], True, True, [sel, Apc[d]], ps, inc=True)
                                a_, d_, m_ = arg[d], Dm[d], Mm[d]
                                tt(a_.ap[:, 0:W].rearrange("p (i l) -> p i l", i=nb4),
                                   ps.ap[:, 0:W].rearrange("p (i l) -> p i l", i=nb4),
                                   bcast_last(v3(A_tm)[:, tb:tb + nb4, d * 32 + hgl], 128), ALU.subtract,
                                   [ps, A_tm], [a_])
                                act(d_.ap[:, 0:W], a_.ap[:, 0:W], AF.Exp, [a_], [d_])
                                stt(m_.ap[:, 0:W], d_.ap[:, 0:W], 1.0, cbm[d].ap[:, tb * 128: tb * 128 + W], ALU.min,
                                    ALU.mult, [d_, cbm[d]], [m_])
                            keep_warm(4)
                            for i in range(nb4):
                                yp_ = yps[i // 2]
                                c0 = (i % 2) * CW + hh * 64
                                for d in range(2):
                                    mm(yp_.ap[:, c0:c0 + 64], Mm[d].ap[:, i * 128:(i + 1) * 128],
                                       xbv[:, i, d, hh * 64:(hh + 1) * 64], d == 0, d == 1, [Mm[d], xdt], yp_,
                                       inc=(d == 1))
                            if hh < nb4:
                                rec_step(hh)
                                if hh % 2 == 1:
                                    yoff_pair(hh // 2)
                        NB = nb4 * CW
                        tfa, tba, ya = tfB.ap[:, 0:NB], tbB.ap[:, 0:NB], ytB.ap[:, 0:NB]
                        tt(tfa, tfa, tba, ALU.add, [tfB, tbB], [tfB])
                        tt(ya.rearrange("p (i h q) -> p i h q", i=nb4, h=HP),
                           v4(xs)[:, tb:tb + nb4, :].rearrange("p i (h q) -> p i h q", h=HP),
                           bcast_last(dsk_bc[:, h0:h0 + HP].unsqueeze(1).to_broadcast([128, nb4, HP]), 64), ALU.mult,
                           [xs, bcs], [ytB])
                        tt(ya, ya, tfa, ALU.add, [ytB, tfB], [ytB])
                        for pr in range(npair):
                            tt(ya[:, pr * 2 * CW:(pr + 1) * 2 * CW], ya[:, pr * 2 * CW:(pr + 1) * 2 * CW],
                               yps[pr].ap[:, 0:2 * CW], ALU.add, [ytB, yps[pr]], [ytB])
                        tt(ya, ya, v4(sz)[:, tb:tb + nb4, :].rearrange("p i c -> p (i c)"), ALU.mult, [ytB, sz], [ytB])
                        for i in range(nb4):
                            ti = tb + i
                            act(tbB.ap[:, i * CW:(i + 1) * CW], ya[:, i * CW:(i + 1) * CW], AF.Square, [ytB], [tbB, ssq],
                                accum_out=ssq.ap[:, ti * 8 + hq: ti * 8 + hq + 1])
                        cp(ygB.ap[:, 0:NB], ya, [ytB], [ygB], eng="act")
                        ps = bank()
                        psb = ps.ap.bitcast(BF16)
                        for i in range(nb4):
                            for j in range(2):
                                tr(psb[:, (i * 2 + j) * 128:(i * 2 + j + 1) * 128],
                                   ygB.ap[:, i * CW + j * 128: i * CW + (j + 1) * 128], ident_b.ap,
                                   [ygB, ident_b], ps, inc=(i == nb4 - 1 and j == 1))
                        for j in range(2):
                            tt(ygv[:, hq * 2 + j, tb * 128:(tb + nb4) * 128].rearrange("p (i t) -> p i t", i=nb4),
                               psb[:, 0:nb4 * 256].rearrange("p (i j t) -> p i j t", i=nb4, j=2)[:, :, j, :],
                               gnT_s[:, hq * 2 + j: hq * 2 + j + 1].unsqueeze(2).to_broadcast([128, nb4, 128]),
                               ALU.mult, [ps, small], [ygT])
                    wide[0] = True
                    if g == 0:
                        P.dma("sp", o_st[si][0][:, hq * CW:(hq + 1) * CW], Sst[0].ap, reads=[Sst[0]], is_output=True)
                AR.pop()
                AR.pop()

            if not KCOUT:
                AR.pop()
                continue
            AR.push()
            rs = Buf(AR.f32(nt, "rs"))
            otmp = [Buf(AR.f32(512, "otmpc%d" % i)) for i in range(2)]
            gt = Buf(AR.f32(D, "gt_c"))
            gate_bc(l, g, gt)
            P.op("dve", lambda e, ssq=ssq, rs=rs, nt=nt: e.tensor_reduce(
                out=rs.ap, in_=ssq.ap.rearrange("p (t c) -> p t c", t=nt), axis=mybir.AxisListType.X, op=ALU.add),
                reads=[ssq], writes=[rs])
            act(rs.ap, rs.ap, AF.Sqrt, [rs, small], [rs], scale=1.0 / 2048, bias=eps_s)
            recip(rs.ap, rs.ap, [rs], [rs])
            for nb in range(2):
                w1, v1 = load_wblock(c_w_out[0:1024, nb * 512:(nb + 1) * 512], 512)
                w2, v2 = load_wblock(c_w_out[1024:2048, nb * 512:(nb + 1) * 512], 512)
                for ti in range(nt):
                    ps = bank()
                    for c in range(16):
                        wv_ = v1 if c < 8 else v2
                        wb_ = w1 if c < 8 else w2
                        mm(ps.ap, ygv[:, c, ti * 128:(ti + 1) * 128], wv_[:, c % 8, :], c == 0, c == 15, [ygT, wb_], ps,
                           inc=(c == 15))
                    ot = otmp[ti % 2]
                    xr = XR[tiles[ti]]
                    stt(ot.ap, ps.ap, rs.ap[:, ti:ti + 1], gt.ap[:, nb * 512:(nb + 1) * 512], ALU.mult, ALU.mult,
                        [ps, rs, gt], [ot])
                    tt(xr.ap[:, nb * 512:(nb + 1) * 512], xr.ap[:, nb * 512:(nb + 1) * 512], ot.ap, ALU.add, [xr, ot],
                       [xr], eng="dve")
            AR.pop()
            AR.pop()

    import os
    STAGE = int(os.environ.get("KSTAGE", "99"))
    if STAGE >= 1:
        modulation_fm(0)
    if STAGE >= 2:
        layer_a()
    if STAGE >= 3:
        modulation_fm(1)
    if STAGE >= 4:
        layer_c()
    dump("xr0_final", XR[0].ap, [XR[0]])
    dump("xr3_final", XR[3].ap, [XR[3]])
    AR.push()
    junk = Buf(AR.f32(D, "fjunk"))
    st = Buf(AR.f32(16, "fst"))
    gfb = Buf(AR.f32(D, "gfin_bc"))
    gfin_bc = gfb.ap
    P.dma("sp", gfin_bc, pbc(gfin), writes=[gfb])
    yo = [Buf(AR.f32(D, "yo%d" % i)) for i in range(2)]
    for t in range(12):
        s_ = st.ap[:, t:t + 1]
        act(junk.ap, XR[t].ap, AF.Square, [XR[t]], [junk, st], accum_out=s_)
        act(s_, s_, AF.Sqrt, [st, small], [st], scale=1.0 / D, bias=eps_s)
        recip(s_, s_, [st], [st])
        y_ = yo[t % 2]
        stt(y_.ap, XR[t].ap, s_, gfin_bc, ALU.mult, ALU.mult, [XR[t], st, gfb], [y_])
        P.dma("sp", y_all[t * 128:(t + 1) * 128, :], y_.ap, reads=[y_], is_output=True)
    AR.pop()
    P.finish()
    with nc.Block() as block:
        P.replay(block)
    es.close()
    return nc


def _consts():
    ident = np.eye(128, dtype=np.float32)
    s = np.arange(128)
    tmf = (s[:, None] <= s[None, :]).astype(np.float32)
    tmb = (s[:, None] >= s[None, :]).astype(np.float32)
    sel = np.zeros((64, 32), np.float32)
    for k in range(64):
        sel[k, k % 32] = 1.0
    rows_ = 1024 // 64
    row = np.repeat(np.arange(rows_, dtype=np.float32), 64)
    col = np.tile(np.arange(64, dtype=np.float32), rows_)
    n_freq = 8
    inv = np.power(np.float32(10000.0), -np.arange(n_freq, dtype=np.float32) / n_freq).astype(np.float32)
    ang = np.concatenate([row[:, None] * inv, col[:, None] * inv], axis=-1).astype(np.float32)
    ropecs = np.concatenate([np.cos(ang), np.sin(ang)], axis=-1).astype(np.float32)
    return ident, tmf, tmb, sel, ropecs


_NC_CACHE = {}


def kernel(x_prompt, x_sample, cache_ckv, cache_krope, state_ssd, c, c_ctx,
           w_mod, b_mod, g_pre, g_final,
           a_w_in, a_g_q, a_g_kv, a_w_uq, a_w_uk, a_w_uv, a_conv_w, a_conv_b, a_ln_g, a_ln_b, a_w_out,
           c_w_in, c_conv_w, c_conv_b, c_dt_bias, c_a_log, c_d, c_g_norm, c_w_out):
    f = lambda a: np.ascontiguousarray(np.asarray(a, dtype=np.float32))
    x_prompt, x_sample = f(x_prompt), f(x_sample)
    ident, tmf, tmb, sel, ropecs = _consts()
    fm = lambda v, n: f(np.asarray(v, np.float32).reshape(n, 128).T)
    b_mod = np.asarray(b_mod, np.float32)
    g_pre = np.asarray(g_pre, np.float32)
    shared = {
        "w_mod": f(w_mod),
        "bmodT": f(np.concatenate([fm(b_mod[0], 24), fm(b_mod[1], 24)], axis=1)),
        "bmod_row": f(b_mod.reshape(1, -1)),
        "gpreT": f(np.concatenate([fm(g_pre[0], 8), fm(g_pre[1], 8)], axis=1)),
        "gfin": f(np.asarray(g_final).reshape(1, -1)),
        "ropecs": ropecs,
        "a_w_in": f(a_w_in[0]), "a_gq": f(np.asarray(a_g_q[0]).reshape(1, -1)),
        "a_gkv": f(np.asarray(a_g_kv[0]).reshape(1, -1)),
        "a_w_uq": f(a_w_uq[0]), "a_w_uk": f(a_w_uk[0]), "a_w_uv": f(a_w_uv[0]),
        "a_convwT": f(np.asarray(a_conv_w[0], np.float32).reshape(31, 4, 128).transpose(2, 1, 0).reshape(128, 124)),
        "a_convb": f(np.asarray(a_conv_b[0]).reshape(1, -1)),
        "a_lngT": fm(a_ln_g[0], 4), "a_lnbT": fm(a_ln_b[0], 4),
        "a_w_out": f(a_w_out[0]),
        "c_w_in": f(c_w_in[0]),
        "c_convwT": f(np.asarray(c_conv_w[0], np.float32).reshape(5, 24, 128).transpose(2, 1, 0).reshape(128, 120)),
        "c_convb": f(np.asarray(c_conv_b[0]).reshape(1, -1)),
        "c_convbT": fm(c_conv_b[0], 24),
        "c_dtb": f(np.asarray(c_dt_bias[0]).reshape(1, 64)),
        "c_alog": f(np.asarray(c_a_log[0]).reshape(1, 64)),
        "c_dskip": f(np.asarray(c_d[0]).reshape(1, 32)),
        "c_gnT": fm(c_g_norm[0], 16),
        "c_w_out": f(c_w_out[0]),
        "k_ident": ident, "k_tmf": tmf, "k_tmb": tmb, "k_sel": sel,
    }
    c = np.asarray(c, np.float32)
    c_ctx = np.asarray(c_ctx, np.float32)
    in_maps = []
    for b in range(NCORES):
        m = dict(shared)
        m["x_all"] = f(np.concatenate([x_prompt[2 * b].reshape(256, D), x_prompt[2 * b + 1].reshape(256, D),
                                       x_sample[b]], axis=0))
        m["cckv"] = f(cache_ckv[b, 0])
        m["ckr"] = f(cache_krope[b, 0])
        m["st0"] = f(np.asarray(state_ssd[b, 0], np.float32).transpose(0, 3, 1, 2).reshape(2, 128, 2048))
        cond = np.stack([c_ctx, c[b]], axis=0)
        m["condT"] = f(cond.reshape(2, 8, 128).transpose(2, 1, 0).reshape(128, 16))
        in_maps.append(m)
    if "nc" not in _NC_CACHE:
        _NC_CACHE["nc"] = build_program()
    nc = _NC_CACHE["nc"]
    res = run_bass_kernel_spmd(nc, in_maps, core_ids=list(range(NCORES)))
    R = res.results
    import os as _os
    if _os.environ.get("KDBG"):
        _NC_CACHE["dbg"] = {k: np.asarray(v).astype(np.float32) for k, v in R[0].items() if k.startswith("dbg_")}
    y_prompt = np.zeros((16, 256, D), np.float32)
    y_sample = np.zeros((8, 1024, D), np.float32)
    n_ckv = np.zeros((16, 1, 256, 128), np.float32)
    n_kr = np.zeros((16, 1, 256, 32), np.float32)
    n_st = np.zeros((16, 1, 2, 32, 64, 128), np.float32)
    for b in range(NCORES):
        r = R[b]
        ya = np.asarray(r["y_all"])
        y_prompt[2 * b] = ya[0:256]
        y_prompt[2 * b + 1] = ya[256:512]
        y_sample[b] = ya[512:1536]
        ck = np.asarray(r["o_ckv"])
        kr = np.asarray(r["o_kr"])
        n_ckv[2 * b, 0] = ck[0:256]
        n_ckv[2 * b + 1, 0] = ck[256:512]
        n_kr[2 * b, 0] = kr[0:256]
        n_kr[2 * b + 1, 0] = kr[256:512]
        st = np.asarray(r["o_st"])
        for s_ in range(2):
            n_st[2 * b + s_, 0] = st[s_].reshape(2, 128, 32, 64).transpose(0, 2, 3, 1)
    return (y_prompt, y_sample, n_ckv, n_kr, n_st)
```

```python
import numpy as np
from contextlib import ExitStack
import concourse.bass as bass
import concourse.mybir as mybir
from concourse.bass_utils import run_bass_kernel_spmd
from concourse.alu_op_type import AluOpType as ALU

F32 = mybir.dt.float32
BF16 = mybir.dt.bfloat16
AF = mybir.ActivationFunctionType

D = 1024
NCORES = 8
EPS = 1e-6
ATTN_SCALE = 96 ** -0.5
ARENA_COLS = 53100


class Buf:
    __slots__ = ("ap", "w", "r", "name", "f32", "fence", "sticky")

    def __init__(self, ap, name=""):
        self.ap = ap
        self.f32 = Arena.LAST if ap.dtype == BF16 else None
        self.w = None
        self.r = {}
        self.name = name
        self.fence = False
        self.sticky = False

    def __getitem__(self, k):
        return self.ap[k]


class Prog:
    ENG = ("pe", "dve", "act", "pool", "sp")

    def __init__(self, nc, es):
        self.nc = nc
        self.lists = {e: [] for e in self.ENG}
        self.sem = {e: es.enter_context(nc.semaphore("sem_" + e)) for e in self.ENG}
        self.cnt = {e: 0 for e in self.ENG}
        self.waited = {e: {} for e in self.ENG}
        self.dsem = {}
        for q in ("pool", "sp"):
            self.dsem[q] = [[es.enter_context(nc.semaphore("dq_%s_%d" % (q, i))), 0] for i in range(24)]
        self.dptr = {"pool": 0, "sp": 0}
        self.out_toks = []
        self.pending_dma = []

    def barrier(self):
        toks = [(self.sem[e], self.cnt[e]) for e in self.ENG if self.cnt[e] > 0] + self.pending_dma
        self.pending_dma = []
        for e in self.ENG:
            for (sem, val) in toks:
                if sem is self.sem[e]:
                    continue
                k = id(sem)
                if self.waited[e].get(k, 0) >= val:
                    continue
                self.waited[e][k] = val
                self.lists[e].append(("w", sem, val))

    def _wait(self, eng, tok, raw=False):
        if tok is None:
            return
        sem, val = tok
        if sem is self.sem[eng]:
            if eng == "pe" or not raw:
                return
        k = id(sem)
        if self.waited[eng].get(k, 0) >= val:
            return
        self.waited[eng][k] = val
        self.lists[eng].append(("w", sem, val))

    def _deps(self, eng, reads, writes):
        for b in reads:
            self._wait(eng, b.w, raw=True)
        for b in writes:
            self._wait(eng, b.w, raw=b.fence)
            for t in b.r.values():
                self._wait(eng, t)

    def _mark(self, tok, reads, writes):
        k = id(tok[0])
        for b in reads:
            b.r[k] = tok
        for b in writes:
            b.w = tok
            b.r = {}
            b.fence = b.sticky

    def op(self, eng, fn, reads=(), writes=(), inc=True):
        self._deps(eng, reads, writes)
        self.lists[eng].append(("o", fn, inc))
        tok = (self.sem[eng], self.cnt[eng] + 1)
        if inc:
            self.cnt[eng] += 1
        self._mark(tok, reads, writes)

    def dma(self, q, out_ap, in_ap, reads=(), writes=(), is_output=False, **kw):
        self._deps(q, reads, writes)
        pool = self.dsem[q]
        i = self.dptr[q]
        self.dptr[q] = (i + 1) % len(pool)
        sem, uses = pool[i]
        if uses > 0:
            self._wait(q, (sem, 16 * uses))
        pool[i][1] = uses + 1
        tok = (sem, 16 * (uses + 1))
        self.lists[q].append(("d", out_ap, in_ap, sem, kw))
        self._mark(tok, reads, writes)
        self.pending_dma.append(tok)
        if is_output:
            self.out_toks.append(tok)

    def finish(self):
        for tok in self.out_toks:
            self._wait("sp", tok)

    def replay(self, block):
        nc = self.nc
        names = {"pe": "tensor", "dve": "vector", "act": "scalar", "pool": "gpsimd", "sp": "sync"}
        for e in self.ENG:
            items = self.lists[e]
            sem_e = self.sem[e]

            def body(eng, items=items, sem_e=sem_e):
                for it in items:
                    if it[0] == "w":
                        eng.wait_ge(it[1], it[2])
                    elif it[0] == "o":
                        ins = it[1](eng)
                        if it[2]:
                            ins.then_inc(sem_e, 1)
                    else:
                        eng.dma_start(out=it[1], in_=it[2], **it[4]).then_inc(it[3], 16)

            getattr(block, names[e])(body)


class Arena:
    LAST = None

    def __init__(self, ap, ncols):
        self.ap = ap
        self.n = ncols
        self.top = 0
        self.stack = []
        self.on_pop = None
        self.need_barrier = False

    def push(self):
        self.stack.append(self.top)

    def pop(self):
        self.top = self.stack.pop()
        self.need_barrier = True

    def _sync(self):
        if self.need_barrier and self.on_pop is not None:
            self.on_pop()
        self.need_barrier = False

    def f32(self, cols, name="", parts=128):
        self._sync()
        cols_al = (cols + 15) // 16 * 16
        off = self.top
        self.top += cols_al
        assert self.top <= self.n, "arena overflow at %s: %d" % (name, self.top)
        return self.ap[0:parts, off:off + cols]

    def bf16(self, cols, name="", parts=128):
        self._sync()
        c32 = (cols + 1) // 2
        c32a = (c32 + 15) // 16 * 16
        off = self.top
        self.top += c32a
        assert self.top <= self.n, "arena overflow at %s: %d" % (name, self.top)
        self.last_f32 = self.ap[0:parts, off:off + c32]
        Arena.LAST = self.last_f32
        return self.ap[0:parts, off:off + c32].bitcast(BF16)[:, 0:cols]


def bcast_last(ap, n):
    return ap.unsqueeze(len(ap.shape)).to_broadcast(list(ap.shape) + [n])


def build_program(debug=False):
    nc = bass.Bass("TRN2", target_bir_lowering=False)
    es = ExitStack()

    def din(name, shape, dt=F32):
        return nc.dram_tensor(name, list(shape), dt, kind="ExternalInput").ap()

    def dout(name, shape, dt=F32):
        return nc.dram_tensor(name, list(shape), dt, kind="ExternalOutput").ap()

    x_all = din("x_all", [1536, D])
    cckv = din("cckv", [512, 128])
    ckr = din("ckr", [512, 32])
    st0 = din("st0", [2, 128, 2048])
    condT = din("condT", [128, 16])
    w_mod = din("w_mod", [2, D, 3072])
    bmodT = din("bmodT", [128, 2 * 24])
    bmod_row = din("bmod_row", [1, 2 * 3072])
    gpreT = din("gpreT", [128, 16])
    gfin = din("gfin", [1, D])
    ropecs = din("ropecs", [1024, 32])
    a_w_in = din("a_w_in", [D, 2464])
    a_gq = din("a_gq", [1, 256])
    a_gkv = din("a_gkv", [1, 128])
    a_w_uq = din("a_w_uq", [256, 768])
    a_w_uk = din("a_w_uk", [128, 512])
    a_w_uv = din("a_w_uv", [128, 512])
    a_convwT = din("a_convwT", [128, 4 * 31])
    a_convb = din("a_convb", [1, 512])
    a_lngT = din("a_lngT", [128, 4])
    a_lnbT = din("a_lnbT", [128, 4])
    a_w_out = din("a_w_out", [D, D])
    c_w_in = din("c_w_in", [D, 5184])
    c_convwT = din("c_convwT", [128, 24 * 5])
    c_convb = din("c_convb", [1, 3072])
    c_dtb = din("c_dtb", [1, 64])
    c_alog = din("c_alog", [1, 64])
    c_dskip = din("c_dskip", [1, 32])
    c_gnT = din("c_gnT", [128, 16])
    c_w_out = din("c_w_out", [2048, D])
    k_ident = din("k_ident", [128, 128])
    k_tmf = din("k_tmf", [128, 128])
    k_tmb = din("k_tmb", [128, 128])
    k_sel = din("k_sel", [64, 32])
    c_convbT = din("c_convbT", [128, 24])

    y_all = dout("y_all", [1536, D])
    o_ckv = dout("o_ckv", [512, 128])
    o_kr = dout("o_kr", [512, 32])
    o_st = dout("o_st", [2, 2, 128, 2048])
    import os as _os
    DBG = _os.environ.get("KDBG")

    def dump(name, ap, bufs):
        if DBG:
            d = nc.dram_tensor("dbg_" + name, list(ap.shape), ap.dtype, kind="ExternalOutput").ap()
            P.dma("sp", d, ap, reads=bufs, is_output=True)

    arena_t = es.enter_context(nc.sbuf_tensor("arena", [128, ARENA_COLS], F32))
    AR = Arena(arena_t, ARENA_COLS)
    P = Prog(nc, es)
    AR.on_pop = P.barrier
    banks = []
    for i in range(8):
        pt = es.enter_context(nc.psum_tensor("psb%d" % i, [128, 512], F32))
        banks.append(Buf(pt[:, :], "bank%d" % i))
    bank_i = [0]

    wide = [True]
    bank_w = [0]

    def bank():
        if wide[0]:
            b = banks[bank_w[0]]
            bank_w[0] = (bank_w[0] + 1) % 8
            return b
        b = banks[4 + bank_i[0]]
        bank_i[0] = (bank_i[0] + 1) % 4
        return b

    def lbank(i):
        return banks[i % 4]

    def keep_warm(n_):
        dst = banks[3].ap
        for _ in range(n_):
            P.op("pe", lambda e: e.matmul(dst, ident_b.ap, ones_b.ap, start=True, stop=True),
                 reads=[ident_b, ones_b], writes=[banks[3]], inc=False)

    def mm(ps_ap, lhsT, rhs, start, stop, reads, ps_buf, inc):
        P.op("pe", lambda e: e.matmul(ps_ap, lhsT, rhs, start=start, stop=stop),
             reads=reads, writes=[ps_buf], inc=inc)

    def tr(ps_ap, in_ap, ident_ap, reads, ps_buf, inc):
        P.op("pe", lambda e: e.transpose(ps_ap, in_ap, ident_ap), reads=reads, writes=[ps_buf], inc=inc)

    def act(out, in_, func, reads, writes, eng="act", **kw):
        P.op(eng, lambda e: e.activation(out=out, in_=in_, func=func, **kw), reads=reads, writes=writes)

    def tt(out, in0, in1, op, reads, writes, eng="dve"):
        P.op(eng, lambda e: e.tensor_tensor(out=out, in0=in0, in1=in1, op=op), reads=reads, writes=writes)

    def ts(out, in0, s1, s2, op0, op1, reads, writes, eng="dve"):
        if op1 is None:
            P.op(eng, lambda e: e.tensor_scalar(out=out, in0=in0, scalar1=s1, scalar2=None, op0=op0),
                 reads=reads, writes=writes)
        else:
            P.op(eng, lambda e: e.tensor_scalar(out=out, in0=in0, scalar1=s1, scalar2=s2, op0=op0, op1=op1),
                 reads=reads, writes=writes)

    def stt(out, in0, scalar, in1, op0, op1, reads, writes):
        P.op("dve", lambda e: e.scalar_tensor_tensor(out=out, in0=in0, scalar=scalar, in1=in1, op0=op0, op1=op1),
             reads=reads, writes=writes)

    def cp(out, in_, reads, writes, eng="dve"):
        if eng == "act":
            P.op("act", lambda e: e.copy(out=out, in_=in_), reads=reads, writes=writes)
        else:
            P.op(eng, lambda e: e.tensor_copy(out=out, in_=in_), reads=reads, writes=writes)

    def memset(ap, val, writes, eng="pool"):
        P.op(eng, lambda e: e.memset(ap, val), writes=writes)
        for b_ in writes:
            b_.fence = True

    BF16_ONE_PAIR = float(np.frombuffer(np.array([0x3F803F80], dtype=np.uint32).tobytes(), dtype=np.float32)[0])

    def fill_bf16(buf, val):
        v = 0.0 if val == 0.0 else BF16_ONE_PAIR
        P.op("dve", lambda e: e.memset(buf.f32, v), writes=[buf])
        buf.fence = True

    def recip(out, in_, reads, writes):
        P.op("dve", lambda e: e.reciprocal(out=out, in_=in_), reads=reads, writes=writes)

    guard = AR.f32(512, "guard")
    XR = [Buf(AR.f32(D, "xr%d" % t), "xr%d" % t) for t in range(12)]
    ident_f = Buf(AR.f32(128, "identf"))
    ident_b = Buf(AR.bf16(128, "identb"))
    tmf = Buf(AR.f32(128, "tmf"))
    tmb = Buf(AR.f32(128, "tmb"))
    ones_f = Buf(AR.f32(128, "onesf"))
    ones_b = Buf(AR.bf16(512, "onesb"))
    tmf_b = Buf(AR.bf16(128, "tmfb"))
    tmb_b = Buf(AR.bf16(128, "tmbb"))
    sel = Buf(AR.bf16(32, "sel", parts=64))
    NWB = 2
    WB = [Buf(AR.bf16(8 * 512, "wb%d" % i)) for i in range(NWB)]
    wb_i = [0]
    small = Buf(AR.f32(1024, "small"))
    sm_off = [0]

    def sm(cols):
        o = sm_off[0]
        sm_off[0] += cols
        assert sm_off[0] <= 1024
        return small.ap[:, o:o + cols]

    condT_s = sm(16)
    bmodT_s = sm(48)
    gpreT_s = sm(16)
    lngT_s = sm(4)
    lnbT_s = sm(4)
    gnT_s = sm(16)
    aconvw_s = sm(124)
    cconvw_s = sm(120)
    eps_s = sm(1)
    modT = sm(2 * 48)
    gsT = sm(2 * 16)
    cconvb_s = sm(24)
    bcs = Buf(AR.f32(1024 + 256 + 128 + 64 + 64 + 32, "bcs"))
    gq_bc = bcs.ap[:, 1024:1280]
    gkv_bc = bcs.ap[:, 1280:1408]
    dtb_bc = bcs.ap[:, 1408:1472]
    aneg_bc = bcs.ap[:, 1472:1536]
    dsk_bc = bcs.ap[:, 1536:1568]
    scT = Buf(AR.bf16(16, "scT"))
    rope_s = Buf(AR.f32(8 * 32, "rope"))

    P.dma("sp", ident_f.ap, k_ident, writes=[ident_f])
    P.dma("pool", ident_b.ap, k_ident, writes=[ident_b])
    P.dma("sp", tmf.ap, k_tmf, writes=[tmf])
    P.dma("sp", tmb.ap, k_tmb, writes=[tmb])
    cp(tmf_b.ap, tmf.ap, [tmf], [tmf_b])
    cp(tmb_b.ap, tmb.ap, [tmb], [tmb_b])
    P.dma("pool", sel.ap, k_sel, writes=[sel])
    memset(ones_f.ap, 1.0, [ones_f])
    fill_bf16(ones_b, 1.0)
    for (dst, src) in ((condT_s, condT), (bmodT_s, bmodT), (gpreT_s, gpreT), (lngT_s, a_lngT), (lnbT_s, a_lnbT),
                       (gnT_s, c_gnT), (aconvw_s, a_convwT), (cconvw_s, c_convwT), (cconvb_s, c_convbT)):
        P.dma("sp", dst, src, writes=[small])
    memset(eps_s, EPS, [small], eng="dve")
    def pbc(src):
        return src.rearrange("o n -> (o n)").partition_broadcast(128)

    for (dst, src) in ((gq_bc, a_gq), (gkv_bc, a_gkv), (dtb_bc, c_dtb), (aneg_bc, c_alog), (dsk_bc, c_dskip)):
        P.dma("sp", dst, pbc(src), writes=[bcs])
    act(aneg_bc, aneg_bc, AF.Exp, [bcs], [bcs])
    ts(aneg_bc, aneg_bc, -1.0, None, ALU.mult, None, [bcs], [bcs])
    P.dma("sp", rope_s.ap.rearrange("p (t c) -> p t c", t=8), ropecs.rearrange("(t p) c -> p t c", p=128),
          writes=[rope_s])
    for t in range(12):
        P.dma("sp", XR[t].ap, x_all[t * 128:(t + 1) * 128, :], writes=[XR[t]])

    act(scT.ap, condT_s, AF.Silu, [small], [scT])

    def load_wblock(src_ap, ncols, kchunks=8, parts=128):
        wb = WB[wb_i[0]]
        wb_i[0] = (wb_i[0] + 1) % NWB
        view = wb.ap[0:parts, 0:kchunks * ncols].rearrange("p (k n) -> p k n", k=kchunks)
        P.dma("pool", view, src_ap.rearrange("(k p) n -> p k n", p=parts), writes=[wb])
        return wb, view

    def modulation_fm(l):
        wide[0] = False
        psm = lbank(0)
        psm_v = psm.ap[:, 0:32].rearrange("p (j g) -> p j g", g=2)
        for blk in range(4):
            wb, wv = load_wblock(w_mod[l][:, blk * 512:(blk + 1) * 512], 512)
            for jj in range(4):
                j = blk * 4 + jj
                for k in range(8):
                    mm(psm_v[:, j, :], wv[:, k, jj * 128:(jj + 1) * 128],
                       scT.ap.rearrange("p (k g) -> p k g", g=2)[:, k, :],
                       k == 0, k == 7, [wb, scT], psm, inc=(k == 7 and jj == 3))
        mo = modT[:, l * 32:(l + 1) * 32].rearrange("p (j g) -> p j g", g=2)
        tt(mo, psm_v, bcast_last(bmodT_s[:, l * 24: l * 24 + 16], 2), ALU.add, [psm, small], [small])
        gs = gsT[:, l * 16:(l + 1) * 16].rearrange("p (k g) -> p k g", g=2)
        ts(gs, mo[:, 8:16, :], 1.0, None, ALU.add, None, [small], [small])
        tt(gs, gs, bcast_last(gpreT_s[:, l * 8:(l + 1) * 8], 2), ALU.mult, [small], [small])
        wide[0] = True

    def gate_bc(l, g, gt):
        AR.push()
        screp = Buf(AR.bf16(8 * 128, "screp"))
        brow = Buf(AR.f32(1024, "brow"))
        srv = screp.ap.rearrange("p (k m) -> p k m", m=128)
        cp(srv, bcast_last(scT.ap.rearrange("p (k g) -> p k g", g=2)[:, :, g], 128), [scT], [screp])
        P.dma("sp", brow.ap, pbc(bmod_row[:, l * 3072 + 2048: l * 3072 + 3072]), writes=[brow])
        for nb in range(2):
            wb, wv = load_wblock(w_mod[l][:, 2048 + nb * 512: 2048 + (nb + 1) * 512], 512)
            pg = bank()
            for k in range(8):
                mm(pg.ap, srv[:, k, :], wv[:, k, :], k == 0, k == 7, [wb, screp], pg, inc=(k == 7))
            tt(gt.ap[:, nb * 512:(nb + 1) * 512], pg.ap, brow.ap[:, nb * 512:(nb + 1) * 512], ALU.add, [pg, brow], [gt])
        AR.pop()

    def build_hT(l, tiles, g, hT):
        nt = len(tiles)
        hv = hT.ap.rearrange("p (k t) -> p k t", k=8)
        AR.push()
        xn = [Buf(AR.bf16(D, "xn%d" % i)) for i in range(4)]
        junk = Buf(AR.f32(D, "junk"))
        st = Buf(AR.f32(8, "hst"))
        for tb in range(0, nt, 4):
            for i in range(4):
                t = tiles[tb + i]
                ssq = st.ap[:, i:i + 1]
                act(junk.ap, XR[t].ap, AF.Square, [XR[t]], [junk, st], accum_out=ssq)
                act(ssq, ssq, AF.Sqrt, [st, small], [st], scale=1.0 / D, bias=eps_s)
                recip(ssq, ssq, [st], [st])
                act(xn[i].ap, XR[t].ap, AF.Identity, [XR[t], st], [xn[i]], scale=ssq)
            for k in range(8):
                ps = bank()
                psb = ps.ap.bitcast(BF16)
                for i in range(4):
                    tr(psb[:, i * 128:(i + 1) * 128], xn[i].ap[:, k * 128:(k + 1) * 128], ident_b.ap,
                       [xn[i], ident_b], ps, inc=(i == 3))
                act(hv[:, k, tb * 128:(tb + 4) * 128], psb[:, 0:512], AF.Identity, [ps, small], [hT],
                    scale=gsT[:, l * 16 + k * 2 + g: l * 16 + k * 2 + g + 1],
                    bias=modT[:, l * 32 + k * 2 + g: l * 32 + k * 2 + g + 1])
        AR.pop()

    def layer_a():
        import os
        KSUB = int(os.environ.get("KSUB", "99"))
        KGRP = int(os.environ.get("KGRP", "2"))
        l = 0
        groups = [dict(g=0, tiles=[0, 1, 2, 3], seqs=[(0, 2), (2, 2)], ctx=0),
                  dict(g=1, tiles=list(range(4, 12)), seqs=[(0, 8)], ctx=4)]
        for grp in ([groups[1]] if os.environ.get("KGSEL") == "1" else groups[:KGRP]):
            g = grp["g"]
            tiles = grp["tiles"]
            nt = len(tiles)
            Tg = nt * 128
            nctx = grp["ctx"]
            nkt = nt + nctx
            Lk = nkt * 128
            AR.push()
            hT = Buf(AR.bf16(8 * Tg, "hT"))
            build_hT(l, tiles, g, hT)
            hv = hT.ap.rearrange("p (k t) -> p k t", k=8)
            mixTa = Buf(AR.bf16(4 * Tg, "mixTa"))
            mav = mixTa.ap.rearrange("p (h t) -> p h t", h=4)
            mixTc = Buf(AR.bf16(4 * Tg, "mixTc"))
            mcv = mixTc.ap.rearrange("p (j t) -> p j t", j=4)
            if KSUB < 1:
                AR.pop()
                continue
            AR.push()
            seqs = grp["seqs"]
            Ls = seqs[0][1] * 128
            LP = Ls + 30
            cvin = Buf(AR.bf16(4 * len(seqs) * LP, "cvin"))
            cvin_f32 = AR.last_f32
            cvv = cvin.ap.rearrange("p (j s t) -> p j s t", j=4, s=len(seqs))
            sgc = Buf(AR.bf16(4 * Tg, "sgc"))
            sgv = sgc.ap.rearrange("p (j t) -> p j t", j=4)
            sig = Buf(AR.f32(512, "sig"))
            xnb = Buf(AR.bf16(4 * 512, "xnb"))
            lst = Buf(AR.f32(16, "lst"))
            abrow = Buf(AR.bf16(512, "abrow"))
            fill_bf16(abrow, 0.0)
            P.dma("pool", abrow.ap[0:1, :], a_convb, writes=[abrow])
            memset(cvin_f32, 0.0, [cvin], eng="dve")
            wga, vga = load_wblock(a_w_in[:, 416:928], 512)
            wgb, vgb = load_wblock(a_w_in[:, 928:1440], 512)
            for j in range(4):
                for nb in range(Tg // 512):
                    pa = bank()
                    pb = bank()
                    for k in range(8):
                        mm(pa.ap, vga[:, k, j * 128:(j + 1) * 128], hv[:, k, nb * 512:(nb + 1) * 512], k == 0, k == 7,
                           [wga, hT], pa, inc=(k == 7))
                    for k in range(8):
                        mm(pb.ap, vgb[:, k, j * 128:(j + 1) * 128], hv[:, k, nb * 512:(nb + 1) * 512], k == 0, k == 7,
                           [wgb, hT], pb, inc=(k == 7))
                    act(sig.ap, pb.ap, AF.Sigmoid, [pb], [sig])
                    nseq_blk = 512 // Ls if Ls < 512 else 1
                    for si in range(nseq_blk):
                        if Ls < 512:
                            s_idx = nb * nseq_blk + si
                            c0, c1, d0 = si * Ls, (si + 1) * Ls, 15
                        else:
                            s_idx = 0
                            c0, c1, d0 = 0, 512, 15 + nb * 512
                        tt(cvv[:, j, s_idx, d0:d0 + (c1 - c0)], pa.ap[:, c0:c1], sig.ap[:, c0:c1], ALU.mult,
                           [pa, sig], [cvin])
            if g == 0:
                dump("abrow", abrow.ap, [abrow])
                dump("onesb", ones_b.ap, [ones_b])
                dump("cvin", cvin.ap, [cvin])
            wgc, vgc = load_wblock(a_w_in[:, 1952:2464], 512)
            for j in range(4):
                for nb in range(Tg // 512):
                    pa = bank()
                    for k in range(8):
                        mm(pa.ap, vgc[:, k, j * 128:(j + 1) * 128], hv[:, k, nb * 512:(nb + 1) * 512], k == 0, k == 7,
                           [wgc, hT], pa, inc=(k == 7))
                    act(sgv[:, j, nb * 512:(nb + 1) * 512], pa.ap, AF.Silu, [pa], [sgc])
            dgl = []
            for j in range(4):
                dg = Buf(AR.bf16(31 * 128, "dgj%d" % j))
                dgl.append(dg)
                tt(dg.ap.rearrange("p (k m) -> p k m", k=31),
                   ident_b.ap.unsqueeze(1).to_broadcast([128, 31, 128]),
                   bcast_last(aconvw_s[:, j * 31:(j + 1) * 31], 128), ALU.mult, [ident_b, small], [dg])
            if g == 0:
                dump("dg0", dgl[0].ap, [dgl[0]])
                dump("sgc", sgc.ap, [sgc])
            csb2 = [Buf(AR.f32(512, "convsb%d" % i)) for i in range(2)]
            cjunk = Buf(AR.f32(512, "cjunk"))
            for tb in range(0, nt, 4):
                for i in range(4):
                    ti = tb + i
                    s_idx = ti // seqs[0][1]
                    t0 = (ti % seqs[0][1]) * 128
                    cs_ = csb2[ti % 2]
                    for j in range(4):
                        ps = bank()
                        dgv = dgl[j].ap.rearrange("p (k m) -> p k m", k=31)
                        for k in range(31):
                            mm(ps.ap[:, 0:128], cvv[:, j, s_idx, t0 + k:t0 + k + 128], dgv[:, k, :],
                               k == 0, False, [cvin, dgl[j]], ps, inc=False)
                        mm(ps.ap[:, 0:128], ones_b.ap[:, 0:128],
                           abrow.ap[:, j * 128:(j + 1) * 128], False, True, [ones_b, abrow], ps, inc=True)
                        cp(cs_.ap[:, j * 128:(j + 1) * 128], ps.ap[:, 0:128], [ps], [cs_], eng="act")
                    act(cjunk.ap, cs_.ap, AF.Identity, [cs_], [cjunk, lst], accum_out=lst.ap[:, 0:1])
                    act(cjunk.ap, cs_.ap, AF.Square, [cs_], [cjunk, lst], accum_out=lst.ap[:, 1:2])
                    ts(lst.ap[:, 2:4], lst.ap[:, 0:2], 1.0 / 512, None, ALU.mult, None, [lst], [lst])
                    tt(lst.ap[:, 4:5], lst.ap[:, 2:3], lst.ap[:, 2:3], ALU.mult, [lst], [lst])
                    tt(lst.ap[:, 5:6], lst.ap[:, 3:4], lst.ap[:, 4:5], ALU.subtract, [lst], [lst])
                    act(lst.ap[:, 8:9], lst.ap[:, 5:6], AF.Sqrt, [lst, small], [lst], scale=1.0, bias=eps_s)
                    recip(lst.ap[:, 8:9], lst.ap[:, 8:9], [lst], [lst])
                    ts(xnb.ap[:, i * 512:(i + 1) * 512], cs_.ap, lst.ap[:, 2:3], lst.ap[:, 8:9], ALU.subtract, ALU.mult,
                       [cs_, lst], [xnb])
                if g == 0 and tb == 0:
                    dump("xnb", xnb.ap, [xnb])
                    dump("lst", lst.ap, [lst])
                for j in range(4):
                    ps = bank()
                    psb = ps.ap.bitcast(BF16)
                    for i in range(4):
                        tr(psb[:, i * 128:(i + 1) * 128], xnb.ap[:, i * 512 + j * 128: i * 512 + (j + 1) * 128],
                           ident_b.ap, [xnb, ident_b], ps, inc=(i == 3))
                    act(sig.ap, psb[:, 0:512], AF.Silu, [ps, small], [sig], scale=lngT_s[:, j:j + 1],
                        bias=lnbT_s[:, j:j + 1])
                    tt(mcv[:, j, tb * 128:(tb + 4) * 128], sig.ap, sgv[:, j, tb * 128:(tb + 4) * 128], ALU.mult,
                       [sig, sgc], [mixTc])
            AR.pop()

            if KSUB < 2:
                AR.pop()
                continue
            q_tm = Buf(AR.bf16(nt * 768, "q_tm"))
            qv = q_tm.ap.rearrange("p (t c) -> p t c", t=nt)
            ckvT = Buf(AR.bf16(Lk, "ckvT"))
            krT = Buf(AR.bf16(Lk, "krT", parts=96))
            VA = Buf(AR.bf16(nkt * 1024, "VA"))
            vav = VA.ap.rearrange("p (t h c) -> p t h c", t=nkt, h=8)
            wsm = Buf(AR.bf16(2 * 768 + 512 + 512, "wsm"))
            wuq = wsm.ap[:, 0:1536].rearrange("p (k n) -> p k n", k=2)
            wuk = wsm.ap[:, 1536:2048]
            wuv = wsm.ap[:, 2048:2560]
            P.dma("pool", wuq, a_w_uq.rearrange("(k p) n -> p k n", p=128), writes=[wsm])
            P.dma("pool", wuk, a_w_uk, writes=[wsm])
            P.dma("pool", wuv, a_w_uv, writes=[wsm])
            fill_bf16(VA, 1.0)

            AR.push()
            wb, wv = load_wblock(a_w_in[:, 0:416], 416)
            qnT = Buf(AR.bf16(2 * Tg, "qnT"))
            qnv = qnT.ap.rearrange("p (k t) -> p k t", k=2)
            tmp = Buf(AR.f32(512, "p2tmp"))
            st = Buf(AR.f32(8, "p2st"))
            qn = Buf(AR.bf16(256, "qn"))
            ckv_f = Buf(AR.f32(128, "ckv_f"))
            ckv_b = Buf(AR.bf16(128, "ckv_b"))
            kst = Buf(AR.bf16(96, "kst"))
            kr_f = Buf(AR.f32(32, "kr_f"))
            rt = Buf(AR.f32(64, "ropetmp"))
            qfb = [Buf(AR.f32(384, "qf%d" % i)) for i in range(2)]
            fill_bf16(kst, 0.0)

            def k_transposes(kt_idx, ckv_b, kst):
                ps = bank()
                psb = ps.ap.bitcast(BF16)
                tr(psb[:, 0:128], ckv_b.ap, ident_b.ap, [ckv_b, ident_b], ps, inc=False)
                tr(psb[0:96, 128:256], kst.ap, ident_b.ap, [kst, ident_b], ps, inc=True)
                cp(ckvT.ap[:, kt_idx * 128:(kt_idx + 1) * 128], psb[:, 0:128], [ps], [ckvT], eng="act")
                cp(krT.ap[64:96, kt_idx * 128:(kt_idx + 1) * 128], psb[64:96, 128:256], [ps], [krT], eng="act")

            for c in range(nctx):
                P.dma("sp", ckv_f.ap, cckv[c * 128:(c + 1) * 128, :], writes=[ckv_f])
                P.dma("sp", kr_f.ap, ckr[c * 128:(c + 1) * 128, :], writes=[kr_f])
                cp(ckv_b.ap, ckv_f.ap, [ckv_f], [ckv_b])
                cp(kst.ap[:, 64:96], kr_f.ap, [kr_f], [kst])
                k_transposes(c, ckv_b, kst)

            KP2 = int(os.environ.get("KP2", "99"))
            for ti in range(nt if KP2 >= 2 else 0):
                ps = bank()
                for k in range(8):
                    mm(ps.ap[:, 0:416], hv[:, k, ti * 128:(ti + 1) * 128], wv[:, k, :], k == 0, k == 7,
                       [hT, wb], ps, inc=(k == 7))
                act(tmp.ap[:, 0:256], ps.ap[:, 0:256], AF.Square, [ps], [tmp, st], accum_out=st.ap[:, 0:1])
                act(tmp.ap[:, 256:384], ps.ap[:, 256:384], AF.Square, [ps], [tmp, st], accum_out=st.ap[:, 1:2])
                act(st.ap[:, 0:1], st.ap[:, 0:1], AF.Sqrt, [st, small], [st], scale=1.0 / 256, bias=eps_s)
                act(st.ap[:, 1:2], st.ap[:, 1:2], AF.Sqrt, [st, small], [st], scale=1.0 / 128, bias=eps_s)
                recip(st.ap[:, 0:2], st.ap[:, 0:2], [st], [st])
                stt(qn.ap, ps.ap[:, 0:256], st.ap[:, 0:1], gq_bc, ALU.mult, ALU.mult, [ps, st, bcs], [qn])
                stt(ckv_f.ap, ps.ap[:, 256:384], st.ap[:, 1:2], gkv_bc, ALU.mult, ALU.mult, [ps, st, bcs], [ckv_f])
                cp(ckv_b.ap, ckv_f.ap, [ckv_f], [ckv_b])
                if g == 0:
                    cp(kr_f.ap, ps.ap[:, 384:416], [ps], [kr_f], eng="act")
                    P.dma("sp", o_ckv[ti * 128:(ti + 1) * 128, :], ckv_f.ap, reads=[ckv_f], is_output=True)
                    P.dma("sp", o_kr[ti * 128:(ti + 1) * 128, :], kr_f.ap, reads=[kr_f], is_output=True)
                    cp(kst.ap[:, 64:96], ps.ap[:, 384:416], [ps], [kst])
                else:
                    cs = rope_s.ap.rearrange("p (t c) -> p t c", t=8)[:, ti, :]
                    cosv, sinv = cs[:, 0:16], cs[:, 16:32]
                    kp = ps.ap[:, 384:416].rearrange("p (i two) -> p i two", two=2)
                    ko = kst.ap[:, 64:96].rearrange("p (i two) -> p i two", two=2)
                    tt(rt.ap[:, 0:16], kp[:, :, 0], cosv, ALU.mult, [ps, rope_s], [rt])
                    tt(rt.ap[:, 16:32], kp[:, :, 1], sinv, ALU.mult, [ps, rope_s], [rt])
                    tt(rt.ap[:, 32:48], kp[:, :, 0], sinv, ALU.mult, [ps, rope_s], [rt])
                    tt(rt.ap[:, 48:64], kp[:, :, 1], cosv, ALU.mult, [ps, rope_s], [rt])
                    tt(ko[:, :, 0], rt.ap[:, 0:16], rt.ap[:, 16:32], ALU.subtract, [rt], [kst])
                    tt(ko[:, :, 1], rt.ap[:, 32:48], rt.ap[:, 48:64], ALU.add, [rt], [kst])
                k_transposes(nctx + ti, ckv_b, kst)
                ps2 = bank()
                ps2b = ps2.ap.bitcast(BF16)
                for k in range(2):
                    tr(ps2b[:, k * 128:(k + 1) * 128], qn.ap[:, k * 128:(k + 1) * 128], ident_b.ap, [qn, ident_b], ps2,
                       inc=(k == 1))
                cp(qnv[:, :, ti * 128:(ti + 1) * 128], ps2b[:, 0:256].rearrange("p (k t) -> p k t", k=2), [ps2], [qnT],
                   eng="act")
            for kt in range(nkt if KP2 >= 3 else 0):
                ps = bank()
                mm(ps.ap, ckvT.ap[:, kt * 128:(kt + 1) * 128], wuv, True, True, [ckvT, wsm], ps, inc=True)
                pvv = ps.ap.rearrange("p (j two c) -> p j two c", j=4, two=2)
                vv5 = VA.ap.rearrange("p (t j two c) -> p t j two c", t=nkt, j=4, two=2)
                cp(vv5[:, kt, :, 0, 0:64], pvv[:, :, 0, :], [ps], [VA], eng="act")
                cp(vv5[:, kt, :, 1, 64:128], pvv[:, :, 1, :], [ps], [VA], eng="act")
            for ti in range(nt if KP2 >= 4 else 0):
                for half in range(2):
                    ps = bank()
                    for k in range(2):
                        mm(ps.ap[:, 0:384], qnv[:, k, ti * 128:(ti + 1) * 128], wuq[:, k, half * 384:(half + 1) * 384],
                           k == 0, k == 1, [qnT, wsm], ps, inc=(k == 1))
                    pv = ps.ap[:, 0:384].rearrange("p (h c) -> p h c", h=4)
                    qo = qv[:, ti, half * 384:(half + 1) * 384].rearrange("p (h c) -> p h c", h=4)
                    if g == 0:
                        act(qo, pv, AF.Copy, [ps], [q_tm], scale=ATTN_SCALE)
                    else:
                        qf = qfb[(ti * 2 + half) % 2]
                        act(qf.ap, ps.ap[:, 0:384], AF.Copy, [ps], [qf], scale=ATTN_SCALE)
                        qfv = qf.ap.rearrange("p (h c) -> p h c", h=4)
                        cp(qo[:, :, 0:64], qfv[:, :, 0:64], [qf], [q_tm])
                        cs = rope_s.ap.rearrange("p (t c) -> p t c", t=8)[:, ti, :]
                        cosv, sinv = cs[:, 0:16], cs[:, 16:32]
                        cosb = cosv.unsqueeze(1).to_broadcast([128, 4, 16])
                        sinb = sinv.unsqueeze(1).to_broadcast([128, 4, 16])
                        pr = qfv[:, :, 64:96].rearrange("p h (i two) -> p h i two", two=2)
                        qr = qo[:, :, 64:96].rearrange("p h (i two) -> p h i two", two=2)
                        rr = tmp.ap[:, 0:256].rearrange("p (a h i) -> p a h i", a=4, h=4)
                        tt(rr[:, 0], pr[:, :, :, 0], cosb, ALU.mult, [qf, rope_s], [tmp])
                        tt(rr[:, 1], pr[:, :, :, 1], sinb, ALU.mult, [qf, rope_s], [tmp])
                        tt(rr[:, 2], pr[:, :, :, 0], sinb, ALU.mult, [qf, rope_s], [tmp])
                        tt(rr[:, 3], pr[:, :, :, 1], cosb, ALU.mult, [qf, rope_s], [tmp])
                        tt(qr[:, :, :, 0], rr[:, 0], rr[:, 1], ALU.subtract, [tmp], [q_tm])
                        tt(qr[:, :, :, 1], rr[:, 2], rr[:, 3], ALU.add, [tmp], [q_tm])
            AR.pop()

            if KSUB < 3:
                AR.pop()
                continue
            AR.push()
            wgt, vgt = load_wblock(a_w_in[:, 1440:1952], 512)
            QT = [Buf(AR.bf16(Tg, "QT%d" % i, parts=96)) for i in range(2)]
            KT = [Buf(AR.bf16(Lk, "KT%d" % i, parts=96)) for i in range(2)]
            sgT = [Buf(AR.bf16(Tg, "sgT%d" % i)) for i in range(2)]
            PT = [Buf(AR.bf16(512, "PT%d" % i)) for i in range(3)]
            Rr = [Buf(AR.f32(512, "Rr%d" % i)) for i in range(2)]
            t1 = [Buf(AR.f32(512, "t1%d" % i)) for i in range(2)]
            pti = 0
            poi = [0]
            wide[0] = False
            for h in range(8):
                qt, ktb, sg = QT[h % 2], KT[h % 2], sgT[(h // 2) % 2]
                vlo, vhi = (0, 64) if h % 2 == 0 else (64, 128)
                dlo, dhi = (64, 128) if h % 2 == 0 else (0, 64)
                for tb in range(0, nt, 8):
                    nn = min(8, nt - tb)
                    ps = bank()
                    psb = ps.ap.bitcast(BF16)
                    for i in range(nn):
                        tr(psb[0:96, i * 128:(i + 1) * 128], qv[:, tb + i, h * 96:(h + 1) * 96], ident_b.ap,
                           [q_tm, ident_b], ps, inc=(i == nn - 1))
                    cp(qt.ap[:, tb * 128:(tb + nn) * 128], psb[0:96, 0:nn * 128], [ps], [qt], eng="act")
                for nb in range(0, Lk, 512):
                    w = min(512, Lk - nb)
                    ps = bank()
                    mm(ps.ap[0:64, 0:w], wuk[:, h * 64:(h + 1) * 64], ckvT.ap[:, nb:nb + w], True, True, [wsm, ckvT], ps,
                       inc=True)
                    cp(ktb.ap[0:64, nb:nb + w], ps.ap[0:64, 0:w], [ps], [ktb], eng="act")
                cp(ktb.ap[64:96, :], krT.ap[64:96, :], [krT], [ktb])
                for nb in range(Tg // 512 if h % 2 == 0 else 0):
                    ps = bank()
                    for k in range(8):
                        mm(ps.ap, vgt[:, k, h * 64:(h + 2) * 64], hv[:, k, nb * 512:(nb + 1) * 512], k == 0,
                           k == 7, [wgt, hT], ps, inc=(k == 7))
                    act(sg.ap[:, nb * 512:(nb + 1) * 512], ps.ap, AF.Silu, [ps], [sg])
                for (s0, sl) in seqs:
                    q0 = s0 * 128
                    qlen = sl * 128
                    if g == 0:
                        ktiles = list(range(s0, s0 + sl))
                    else:
                        ktiles = list(range(nkt))
                    for qb in range(0, qlen, 512):
                        w = min(512, qlen - qb)
                        po = lbank(poi[0])
                        poi[0] += 1
                        for ki, kt in enumerate(ktiles):
                            ps = bank()
                            mm(ps.ap[:, 0:w], ktb.ap[0:96, kt * 128:(kt + 1) * 128], qt.ap[0:96, q0 + qb:q0 + qb + w],
                               True, True, [ktb, qt], ps, inc=True)
                            pt = PT[pti % 3]
                            pti += 1
                            act(pt.ap[:, 0:w], ps.ap[:, 0:w], AF.Exp, [ps], [pt])
                            mm(po.ap[:, 0:w], vav[:, kt, h, :], pt.ap[:, 0:w], ki == 0, ki == len(ktiles) - 1, [VA, pt], po,
                               inc=(ki == len(ktiles) - 1))
                        rr = Rr[h % 2]
                        tt1 = t1[h % 2]
                        recip(rr.ap[dlo:dhi, 0:w], po.ap[dlo:dhi, 0:w], [po], [rr])
                        tt(tt1.ap[vlo:vhi, 0:w], po.ap[vlo:vhi, 0:w], rr.ap[dlo:dhi, 0:w], ALU.mult, [po, rr], [tt1])
                        tt(mav[vlo:vhi, h // 2, q0 + qb:q0 + qb + w], tt1.ap[vlo:vhi, 0:w],
                           sg.ap[vlo:vhi, q0 + qb:q0 + qb + w], ALU.mult, [tt1, sg], [mixTa])
            AR.pop()

            wide[0] = True
            if KSUB < 4:
                AR.pop()
                continue
            AR.push()
            otmp = [Buf(AR.f32(512, "otmp%d" % i)) for i in range(2)]
            gt = Buf(AR.f32(D, "gt_a"))
            gate_bc(l, g, gt)
            dump("mixTa%d" % g, mixTa.ap, [mixTa])
            dump("mixTc%d" % g, mixTc.ap, [mixTc])
            dump("gt%d" % g, gt.ap, [gt])
            dump("hT%d" % g, hT.ap, [hT])
            for nb in range(2):
                wo_b, wo = load_wblock(a_w_out[:, nb * 512:(nb + 1) * 512], 512)
                for ti in range(nt):
                    ps = bank()
                    for c in range(8):
                        lhs = mav[:, c, ti * 128:(ti + 1) * 128] if c < 4 else mcv[:, c - 4, ti * 128:(ti + 1) * 128]
                        mm(ps.ap, lhs, wo[:, c, :], c == 0, c == 7, [mixTa, mixTc, wo_b], ps, inc=(c == 7))
                    ot = otmp[ti % 2]
                    xr = XR[tiles[ti]]
                    tt(ot.ap, ps.ap, gt.ap[:, nb * 512:(nb + 1) * 512], ALU.mult, [ps, gt], [ot])
                    if ti == 0 and g == 0:
                        dump("ot%d" % nb, ot.ap, [ot])
                        dump("xrpre%d" % nb, xr.ap[:, nb * 512:(nb + 1) * 512], [xr])
                    tt(xr.ap[:, nb * 512:(nb + 1) * 512], xr.ap[:, nb * 512:(nb + 1) * 512], ot.ap, ALU.add, [xr, ot],
                       [xr], eng="dve")
            AR.pop()
            AR.pop()

    def layer_c():
        dump("xr0_afterA", XR[0].ap, [XR[0]])
        l = 1
        HP = 4
        CW = HP * 64
        groups = [dict(g=0, tiles=[0, 1, 2, 3], seqs=[(0, 2), (2, 2)]),
                  dict(g=1, tiles=list(range(4, 12)), seqs=[(0, 8)])]
        wdt_b = Buf(AR.bf16(8 * 64, "wdt"))
        wdtv = wdt_b.ap.rearrange("p (k n) -> p k n", k=8)
        P.dma("pool", wdtv, c_w_in[:, 5120:5184].rearrange("(k p) n -> p k n", p=128), writes=[wdt_b])
        selv = sel.ap
        import os
        KCG = int(os.environ.get("KCG", "2"))
        KCPASS = int(os.environ.get("KCPASS", "8"))
        KCSCAN = int(os.environ.get("KCSCAN", "1"))
        KCOUT = int(os.environ.get("KCOUT", "1"))
        for grp in groups[:KCG]:
            g = grp["g"]
            tiles = grp["tiles"]
            seqs = grp["seqs"]
            nt = len(tiles)
            Tg = nt * 128
            AR.push()
            hT = Buf(AR.bf16(8 * Tg, "hT"))
            build_hT(l, tiles, g, hT)
            hv = hT.ap.rearrange("p (k t) -> p k t", k=8)
            ygT = Buf(AR.bf16(16 * Tg, "ygT"))
            ygv = ygT.ap.rearrange("p (c t) -> p c t", c=16)
            A_tm = Buf(AR.f32(nt * 64, "A_tm"))
            eA = Buf(AR.f32(nt * 64, "eA"))
            cdec = Buf(AR.f32(nt * 64, "cdec"))
            wdt = Buf(AR.f32(nt * 64, "wdt_s"))
            wde = Buf(AR.f32(nt * 64, "wde_s"))
            ssq = Buf(AR.f32(nt * 8, "ssq"))
            Apc = [Buf(AR.bf16(Tg, "Apc%d" % d, parts=64)) for d in range(2)]
            v3 = lambda b_: b_.ap.rearrange("p (t c) -> p t c", t=nt)
            AR.push()
            a_s = Buf(AR.f32(nt * 64, "a_s"))
            a_res = Buf(AR.f32(nt * 64, "a_res"))
            a_hl = Buf(AR.bf16(nt * 128, "a_hl"))
            ahv = a_hl.ap.rearrange("p (t q c) -> p t q c", t=nt, q=2)
            arep = Buf(AR.bf16(nt * 256, "arep"))
            arv = arep.ap.rearrange("p (t d q r c) -> p t d q r c", t=nt, d=2, q=2, r=2)
            csb = Buf(AR.f32(128, "csb"))
            tmpd = Buf(AR.f32(64, "tmpd"))
            r1 = Buf(AR.f32(512, "r1", parts=64))
            for ti in range(nt):
                ps = bank()
                for k in range(8):
                    mm(ps.ap[:, 0:64], hv[:, k, ti * 128:(ti + 1) * 128], wdtv[:, k, :], k == 0, k == 7, [hT, wdt_b], ps,
                       inc=(k == 7))
                tt(tmpd.ap, ps.ap[:, 0:64], dtb_bc, ALU.add, [ps, bcs], [tmpd])
                act(tmpd.ap, tmpd.ap, AF.Exp, [tmpd], [tmpd])
                act(v3(wdt)[:, ti, :], tmpd.ap, AF.Ln, [tmpd], [wdt], bias=1.0)
                tt(v3(a_s)[:, ti, :], v3(wdt)[:, ti, :], aneg_bc, ALU.mult, [wdt, bcs], [a_s])
            a3 = a_s.ap.rearrange("p (t c) -> p t c", t=nt)
            cp(ahv[:, :, 0, :], a3, [a_s], [a_hl])
            tt(a_res.ap.rearrange("p (t c) -> p t c", t=nt), a3, ahv[:, :, 0, :], ALU.subtract, [a_s, a_hl], [a_res])
            cp(ahv[:, :, 1, :], a_res.ap.rearrange("p (t c) -> p t c", t=nt), [a_res], [a_hl])
            for q in range(2):
                for r_i in range(2):
                    cp(arv[:, :, :, q, r_i, :], ahv[:, :, q, :].rearrange("p t (d c) -> p t d c", d=2), [a_hl], [arep])
            for ti in range(nt):
                ps = bank()
                for q in range(2):
                    mm(ps.ap[:, 0:32], tmf_b.ap, ahv[:, ti, q, 0:32], q == 0, q == 1, [tmf_b, a_hl], ps, inc=False)
                for q in range(2):
                    mm(ps.ap[:, 32:64], tmb_b.ap, ahv[:, ti, q, 32:64], q == 0, q == 1, [tmb_b, a_hl], ps, inc=False)
                for q in range(2):
                    mm(ps.ap[:, 64:128], ones_b.ap[:, 0:128], ahv[:, ti, q, :], q == 0, q == 1, [ones_b, a_hl], ps,
                       inc=(q == 1))
                cp(csb.ap, ps.ap[:, 0:128], [ps], [csb])
                cp(v3(A_tm)[:, ti, :], csb.ap[:, 0:64], [csb], [A_tm], eng="pool")
                act(v3(eA)[:, ti, :], csb.ap[:, 0:64], AF.Exp, [csb], [eA])
                act(v3(cdec)[:, ti, :], csb.ap[:, 64:128], AF.Exp, [csb], [cdec])
                tt(tmpd.ap, csb.ap[:, 64:128], csb.ap[:, 0:64], ALU.subtract, [csb], [tmpd])
                act(tmpd.ap, tmpd.ap, AF.Exp, [tmpd], [tmpd])
                tt(v3(wde)[:, ti, :], tmpd.ap, v3(wdt)[:, ti, :], ALU.mult, [tmpd, wdt], [wde])
            for d in range(2):
                tmd = tmf_b if d == 0 else tmb_b
                for tb in range(0, nt, 4):
                    ps = bank()
                    for i in range(4):
                        for q in range(2):
                            mm(ps.ap[0:64, i * 128:(i + 1) * 128],
                               arv[:, tb + i, d, q, :, :].rearrange("p r c -> p (r c)"), tmd.ap, q == 0, q == 1,
                               [arep, tmd], ps, inc=(i == 3 and q == 1))
                    dst = Apc[d].ap[:, tb * 128:(tb + 4) * 128]
                    cp(dst, ps.ap[0:64, :], [ps], [Apc[d]])
                    tt(r1.ap[32:64, :], ps.ap[32:64, :], dst[32:64, :], ALU.subtract, [ps, Apc[d]], [r1])
                    cp(dst[32:64, :], r1.ap[32:64, :], [r1], [Apc[d]])
            AR.pop()

            def issue_pass_loads(hq_):
                sg_ = hq_ // 2
                wzx_ = WB[wb_i[0]]
                wb_i[0] = (wb_i[0] + 1) % NWB
                vzx_ = wzx_.ap[:, 0:8 * 512].rearrange("p (k n) -> p k n", k=8)
                P.dma("pool", vzx_[:, :, 0:CW], c_w_in[:, hq_ * CW:(hq_ + 1) * CW].rearrange("(k p) n -> p k n", p=128),
                      writes=[wzx_])
                P.dma("pool", vzx_[:, :, CW:2 * CW],
                      c_w_in[:, 2048 + hq_ * CW: 2048 + (hq_ + 1) * CW].rearrange("(k p) n -> p k n", p=128), writes=[wzx_])
                wbc_ = WB[wb_i[0]]
                wb_i[0] = (wb_i[0] + 1) % NWB
                vbc_ = wbc_.ap[:, 0:8 * 256].rearrange("p (k n) -> p k n", k=8)
                P.dma("pool", vbc_[:, :, 0:128],
                      c_w_in[:, 4096 + sg_ * 128: 4096 + (sg_ + 1) * 128].rearrange("(k p) n -> p k n", p=128), writes=[wbc_])
                P.dma("pool", vbc_[:, :, 128:256],
                      c_w_in[:, 4608 + sg_ * 128: 4608 + (sg_ + 1) * 128].rearrange("(k p) n -> p k n", p=128), writes=[wbc_])
                return (wzx_, vzx_, wbc_, vbc_)

            nxt_w = [None]
            for hq in range(KCPASS):
                sg = hq // 2
                h0 = hq * HP
                AR.push()
                sz = Buf(AR.bf16(nt * CW, "sz"))
                xs = Buf(AR.bf16(nt * CW, "xs"))
                Btm = Buf(AR.bf16(nt * 128, "Btm"))
                BT = Buf(AR.bf16(Tg, "BT"))
                CT = Buf(AR.bf16(Tg, "CT"))
                cbm = [Buf(AR.bf16(Tg, "cbm%d" % d)) for d in range(2)]
                prevb = Buf(AR.bf16(nt * CW, "prevb"))
                Sst = [Buf(AR.f32(CW, "S%d" % d)) for d in range(2)]
                brow = Buf(AR.bf16(CW + 128, "browc"))
                fill_bf16(brow, 0.0)
                Ls = seqs[0][1] * 128
                LP = Ls + 4
                raw = [Buf(AR.bf16(len(seqs) * LP, "raw%d" % i)) for i in range(2)]
                dg = [Buf(AR.bf16(5 * 128, "dgc%d" % i)) for i in range(2)]
                v4 = lambda b_: b_.ap.rearrange("p (t c) -> p t c", t=nt)
                for r_ in raw:
                    fill_bf16(r_, 0.0)
                P.dma("pool", brow.ap[0:1, 0:CW], c_convb[:, hq * CW:(hq + 1) * CW], writes=[brow])
                P.dma("pool", brow.ap[0:1, CW:CW + 128], c_convb[:, 2048 + sg * 128: 2048 + (sg + 1) * 128], writes=[brow])
                if nxt_w[0] is None:
                    nxt_w[0] = issue_pass_loads(hq)
                wzx, vzx, wbc_b, vbc = nxt_w[0]
                nxt_w[0] = None
                for ti in range(nt):
                    ps = bank()
                    for k in range(8):
                        mm(ps.ap[:, 0:CW], hv[:, k, ti * 128:(ti + 1) * 128], vzx[:, k, 0:CW], k == 0, k == 7, [hT, wzx], ps,
                           inc=(k == 7))
                    act(v4(sz)[:, ti, :], ps.ap[:, 0:CW], AF.Silu, [ps], [sz])
                chunk_specs = [("x", wzx, vzx[:, :, CW + j * 128: CW + (j + 1) * 128], hq * 2 + j, j) for j in range(2)]
                chunk_specs.append(("B", wbc_b, vbc[:, :, 0:128], 16 + sg, 0))
                chunk_specs.append(("C", wbc_b, vbc[:, :, 128:256], 20 + sg, 0))
                for ci, (kind, wbuf, wview, cch, j) in enumerate(chunk_specs):
                    rw = raw[ci % 2]
                    rv = rw.ap.rearrange("p (s t) -> p s t", s=len(seqs))
                    dgc = dg[ci % 2]
                    dgv = dgc.ap.rearrange("p (k m) -> p k m", k=5)
                    tt(dgv, ident_b.ap.unsqueeze(1).to_broadcast([128, 5, 128]),
                       bcast_last(cconvw_s[:, cch * 5:(cch + 1) * 5], 128), ALU.mult, [ident_b, small], [dgc])
                    for nb in range(Tg // 512):
                        ps = bank()
                        for k in range(8):
                            mm(ps.ap, wview[:, k, :], hv[:, k, nb * 512:(nb + 1) * 512], k == 0, k == 7, [wbuf, hT], ps,
                               inc=(k == 7))
                        if Ls < 512:
                            for si in range(512 // Ls):
                                cp(rv[:, nb * (512 // Ls) + si, 2:2 + Ls], ps.ap[:, si * Ls:(si + 1) * Ls], [ps], [rw],
                                   eng="act")
                        else:
                            cp(rv[:, 0, 2 + nb * 512: 2 + (nb + 1) * 512], ps.ap, [ps], [rw], eng="act")
                    if kind in ("x", "B"):
                        bias_row = brow.ap[:, j * 128:(j + 1) * 128] if kind == "x" else brow.ap[:, CW:CW + 128]
                        for tb in range(0, nt, 4):
                            ps = bank()
                            for i in range(4):
                                ti = tb + i
                                s_idx = ti // seqs[0][1]
                                t0 = (ti % seqs[0][1]) * 128
                                pso = ps.ap[:, i * 128:(i + 1) * 128]
                                for k in range(5):
                                    mm(pso, rv[:, s_idx, t0 + k:t0 + k + 128], dgv[:, k, :], k == 0, False, [rw, dgc], ps,
                                       inc=False)
                                mm(pso, ones_b.ap[:, 0:128], bias_row, False, True, [ones_b, brow], ps, inc=(i == 3))
                            pin = ps.ap.rearrange("p (i c) -> p i c", i=4)
                            if kind == "x":
                                act(v4(xs)[:, tb:tb + 4, j * 128:(j + 1) * 128], pin, AF.Silu, [ps], [xs])
                            else:
                                act(v4(Btm)[:, tb:tb + 4, :], pin, AF.Silu, [ps], [Btm])
                    if kind in ("B", "C"):
                        dstT = BT if kind == "B" else CT
                        for si in range(len(seqs)):
                            for nb in range(0, Ls, 512):
                                w = min(512, Ls - nb)
                                ps = bank()
                                for k in range(5):
                                    mm(ps.ap[:, 0:w], dgv[:, k, :], rv[:, si, nb + k:nb + k + w], k == 0, k == 4, [dgc, rw],
                                       ps, inc=(k == 4))
                                act(dstT.ap[:, si * Ls + nb: si * Ls + nb + w], ps.ap[:, 0:w], AF.Silu, [ps, small], [dstT],
                                    bias=cconvb_s[:, cch:cch + 1])
                if hq + 1 < KCPASS:
                    nxt_w[0] = issue_pass_loads(hq + 1)
                for tb in range(0, nt, 4):
                    ps = bank()
                    for i in range(4):
                        ti = tb + i
                        mm(ps.ap[:, i * 128:(i + 1) * 128], BT.ap[:, ti * 128:(ti + 1) * 128],
                           CT.ap[:, ti * 128:(ti + 1) * 128], True, True, [BT, CT], ps, inc=(i == 3))
                    pv = ps.ap.rearrange("p (i l) -> p i l", i=4)
                    for d in range(2):
                        tmd = tmf if d == 0 else tmb
                        tt(cbm[d].ap[:, tb * 128:(tb + 4) * 128].rearrange("p (i l) -> p i l", i=4), pv,
                           tmd.ap.unsqueeze(1).to_broadcast([128, 4, 128]), ALU.mult, [ps, tmd], [cbm[d]])
                if not KCSCAN:
                    AR.pop()
                    continue
                AR.push()
                arg = [Buf(AR.f32(512, "arg%d" % i)) for i in range(2)]
                Dm = [Buf(AR.bf16(512, "Dm%d" % i)) for i in range(2)]
                Mm = [Buf(AR.bf16(512, "Mm%d" % i)) for i in range(2)]
                xdt = Buf(AR.bf16(4 * 2 * CW, "xdt"))
                xde = [Buf(AR.bf16(CW, "xde%d" % i)) for i in range(2)]
                tfB = Buf(AR.f32(4 * CW, "tfB"))
                tbB = Buf(AR.f32(4 * CW, "tbB"))
                ytB = Buf(AR.f32(4 * CW, "ytB"))
                ygB = Buf(AR.bf16(4 * CW, "ygB"))
                AR_ygB_f32 = ygB.f32[:, 0:CW]
                xdeB = Buf(AR.bf16(4 * CW, "xdeB"))
                SbfB = [Buf(AR.bf16(CW, "SbfB%d" % i)) for i in range(4)]
                stmp = arg[0]
                stmp_ap = arg[0].ap[:, 0:CW]
                stmp2b = ygB
                stmp2_ap = AR_ygB_f32
                cnt = [0]
                hview = lambda ap_: ap_.rearrange("p (h q) -> p h q", h=HP)

                def scaled_x(dst_ap, dst_buf, ti, wsrc, d):
                    tt(hview(dst_ap), hview(v4(xs)[:, ti, :]),
                       bcast_last(v3(wsrc)[:, ti, d * 32 + h0: d * 32 + h0 + HP], 64), ALU.mult,
                       [xs, wsrc], [dst_buf])

                def state_step(d, ti):
                    xd = xde[cnt[0] % 2]
                    cnt[0] += 1
                    scaled_x(xd.ap, xd, ti, wde, d)
                    ps = bank()
                    mm(ps.ap[:, 0:CW], v4(Btm)[:, ti, :], xd.ap, True, True, [Btm, xd], ps, inc=True)
                    tt(hview(stmp_ap), hview(Sst[d].ap),
                       bcast_last(v3(cdec)[:, ti, d * 32 + h0: d * 32 + h0 + HP], 64), ALU.mult,
                       [Sst[d], cdec], [stmp])
                    tt(Sst[d].ap, stmp_ap, ps.ap[:, 0:CW], ALU.add, [stmp, ps], [Sst[d]])

                for si, (s0, sl) in enumerate(seqs):
                    for d in range(2):
                        if g == 1:
                            P.dma("sp", Sst[d].ap, st0[d][:, hq * CW:(hq + 1) * CW], writes=[Sst[d]])
                        else:
                            memset(Sst[d].ap, 0.0, [Sst[d]], eng="pool")
                    for ti in range(s0 + sl - 1, s0 - 1, -1):
                        cp(v4(prevb)[:, ti, :], Sst[1].ap, [Sst[1]], [prevb], eng="act")
                        state_step(1, ti)
                    if g == 0:
                        P.dma("sp", o_st[si][1][:, hq * CW:(hq + 1) * CW], Sst[1].ap, reads=[Sst[1]], is_output=True)
                    wide[0] = False
                    for tb in range(s0, s0 + sl, 4):
                        nb4 = min(4, s0 + sl - tb)
                        W = nb4 * 128
                        xbv = xdt.ap.rearrange("p (i d c) -> p i d c", i=4, d=2)
                        for d in range(2):
                            tt(xbv[:, 0:nb4, d, :].rearrange("p i (h q) -> p i h q", h=HP),
                               v4(xs)[:, tb:tb + nb4, :].rearrange("p i (h q) -> p i h q", h=HP),
                               bcast_last(v3(wdt)[:, tb:tb + nb4, d * 32 + h0: d * 32 + h0 + HP], 64), ALU.mult,
                               [xs, wdt], [xdt])
                        npair = nb4 // 2
                        yps = [lbank(pr) for pr in range(npair)]
                        xdv = xdeB.ap.rearrange("p (i c) -> p i c", i=4)
                        tt(xdv[:, 0:nb4, :].rearrange("p i (h q) -> p i h q", h=HP),
                           v4(xs)[:, tb:tb + nb4, :].rearrange("p i (h q) -> p i h q", h=HP),
                           bcast_last(v3(wde)[:, tb:tb + nb4, h0:h0 + HP], 64), ALU.mult, [xs, wde], [xdeB])
                        tfv = tfB.ap.rearrange("p (i c) -> p i c", i=4)
                        tbv = tbB.ap.rearrange("p (i c) -> p i c", i=4)

                        def rec_step(i):
                            ti = tb + i
                            cp(SbfB[i].ap, Sst[0].ap, [Sst[0]], [SbfB[i]], eng="act")
                            ps = bank()
                            mm(ps.ap[:, 0:CW], v4(Btm)[:, ti, :], xdv[:, i, :], True, True, [Btm, xdeB], ps, inc=True)
                            tt(hview(stmp2_ap), hview(Sst[0].ap),
                               bcast_last(v3(cdec)[:, ti, h0: h0 + HP], 64), ALU.mult, [Sst[0], cdec], [stmp2b])
                            tt(Sst[0].ap, stmp2_ap, ps.ap[:, 0:CW], ALU.add, [stmp2b, ps], [Sst[0]])

                        def yoff_pair(pr):
                            pf = bank()
                            pbk = bank()
                            for u in range(2):
                                ti = tb + pr * 2 + u
                                mm(pf.ap[:, u * CW:(u + 1) * CW], CT.ap[:, ti * 128:(ti + 1) * 128], SbfB[pr * 2 + u].ap,
                                   True, True, [CT, SbfB[pr * 2 + u]], pf, inc=(u == 1))
                            for u in range(2):
                                ti = tb + pr * 2 + u
                                mm(pbk.ap[:, u * CW:(u + 1) * CW], CT.ap[:, ti * 128:(ti + 1) * 128], v4(prevb)[:, ti, :],
                                   True, True, [CT, prevb], pbk, inc=(u == 1))
                            t0_ = tb + pr * 2
                            tt(tfv[:, pr * 2:pr * 2 + 2, :].rearrange("p i (h q) -> p i h q", h=HP),
                               pf.ap.rearrange("p (i h q) -> p i h q", i=2, h=HP),
                               bcast_last(v3(eA)[:, t0_:t0_ + 2, h0:h0 + HP], 64), ALU.mult, [pf, eA], [tfB])
                            tt(tbv[:, pr * 2:pr * 2 + 2, :].rearrange("p i (h q) -> p i h q", h=HP),
                               pbk.ap.rearrange("p (i h q) -> p i h q", i=2, h=HP),
                               bcast_last(v3(eA)[:, t0_:t0_ + 2, 32 + h0:32 + h0 + HP], 64), ALU.mult, [pbk, eA], [tbB])

                        for hh in range(HP):
                            hgl = h0 + hh
                            for d in range(2):
                                ps = bank()
                                mm(ps.ap[:, 0:W], bcast_last(selv[:, hgl], 128),
                                   Apc[d].ap[:, tb * 128: tb * 128 + W], True, True, [sel, Apc[d]], ps, inc=True)
                                a_, d_, m_ = arg[d], Dm[d], Mm[d]
                                tt(a_.ap[:, 0:W].rearrange("p (i l) -> p i l", i=nb4),
                                   ps.ap[:, 0:W].rearrange("p (i l) -> p i l", i=nb4),
                                   bcast_last(v3(A_tm)[:, tb:tb + nb4, d * 32 + hgl], 128), ALU.subtract,
                                   [ps, A_tm], [a_])
                                act(d_.ap[:, 0:W], a_.ap[:, 0:W], AF.Exp, [a_], [d_])
                                stt(m_.ap[:, 0:W], d_.ap[:, 0:W], 1.0, cbm[d].ap[:, tb * 128: tb * 128 + W], ALU.min,
                                    ALU.mult, [d_, cbm[d]], [m_])
                            keep_warm(4)
                            for i in range(nb4):
                                yp_ = yps[i // 2]
                                c0 = (i % 2) * CW + hh * 64
                                for d in range(2):
                                    mm(yp_.ap[:, c0:c0 + 64], Mm[d].ap[:, i * 128:(i + 1) * 128],
                                       xbv[:, i, d, hh * 64:(hh + 1) * 64], d == 0, d == 1, [Mm[d], xdt], yp_,
                                       inc=(d == 1))
                            if hh < nb4:
                                rec_step(hh)
                                if hh % 2 == 1:
                                    yoff_pair(hh // 2)
                        NB = nb4 * CW
                        tfa, tba, ya = tfB.ap[:, 0:NB], tbB.ap[:, 0:NB], ytB.ap[:, 0:NB]
                        tt(tfa, tfa, tba, ALU.add, [tfB, tbB], [tfB])
                        tt(ya.rearrange("p (i h q) -> p i h q", i=nb4, h=HP),
                           v4(xs)[:, tb:tb + nb4, :].rearrange("p i (h q) -> p i h q", h=HP),
                           bcast_last(dsk_bc[:, h0:h0 + HP].unsqueeze(1).to_broadcast([128, nb4, HP]), 64), ALU.mult,
                           [xs, bcs], [ytB])
                        tt(ya, ya, tfa, ALU.add, [ytB, tfB], [ytB])
                        for pr in range(npair):
                            tt(ya[:, pr * 2 * CW:(pr + 1) * 2 * CW], ya[:, pr * 2 * CW:(pr + 1) * 2 * CW],
                               yps[pr].ap[:, 0:2 * CW], ALU.add, [ytB, yps[pr]], [ytB])
                        tt(ya, ya, v4(sz)[:, tb:tb + nb4, :].rearrange("p i c -> p (i c)"), ALU.mult, [ytB, sz], [ytB])
                        for i in range(nb4):
                            ti = tb + i
                            act(tbB.ap[:, i * CW:(i + 1) * CW], ya[:, i * CW:(i + 1) * CW], AF.Square, [ytB], [tbB, ssq],
                                accum_out=ssq.ap[:, ti * 8 + hq: ti * 8 + hq + 1])
                        cp(ygB.ap[:, 0:NB], ya, [ytB], [ygB], eng="act")
                        ps = bank()
                        psb = ps.ap.bitcast(BF16)
                        for i in range(nb4):
                            for j in range(2):
                                tr(psb[:, (i * 2 + j) * 128:(i * 2 + j + 1) * 128],
                                   ygB.ap[:, i * CW + j * 128: i * CW + (j + 1) * 128], ident_b.ap,
                                   [ygB, ident_b], ps, inc=(i == nb4 - 1 and j == 1))
                        for j in range(2):
                            tt(ygv[:, hq * 2 + j, tb * 128:(tb + nb4) * 128].rearrange("p (i t) -> p i t", i=nb4),
                               psb[:, 0:nb4 * 256].rearrange("p (i j t) -> p i j t", i=nb4, j=2)[:, :, j, :],
                               gnT_s[:, hq * 2 + j: hq * 2 + j + 1].unsqueeze(2).to_broadcast([128, nb4, 128]),
                               ALU.mult, [ps, small], [ygT])
                    wide[0] = True
                    if g == 0:
                        P.dma("sp", o_st[si][0][:, hq * CW:(hq + 1) * CW], Sst[0].ap, reads=[Sst[0]], is_output=True)
                AR.pop()
                AR.pop()

            if not KCOUT:
                AR.pop()
                continue
            AR.push()
            rs = Buf(AR.f32(nt, "rs"))
            otmp = [Buf(AR.f32(512, "otmpc%d" % i)) for i in range(2)]
            gt = Buf(AR.f32(D, "gt_c"))
            gate_bc(l, g, gt)
            P.op("dve", lambda e, ssq=ssq, rs=rs, nt=nt: e.tensor_reduce(
                out=rs.ap, in_=ssq.ap.rearrange("p (t c) -> p t c", t=nt), axis=mybir.AxisListType.X, op=ALU.add),
                reads=[ssq], writes=[rs])
            act(rs.ap, rs.ap, AF.Sqrt, [rs, small], [rs], scale=1.0 / 2048, bias=eps_s)
            recip(rs.ap, rs.ap, [rs], [rs])
            for nb in range(2):
                w1, v1 = load_wblock(c_w_out[0:1024, nb * 512:(nb + 1) * 512], 512)
                w2, v2 = load_wblock(c_w_out[1024:2048, nb * 512:(nb + 1) * 512], 512)
                for ti in range(nt):
                    ps = bank()
                    for c in range(16):
                        wv_ = v1 if c < 8 else v2
                        wb_ = w1 if c < 8 else w2
                        mm(ps.ap, ygv[:, c, ti * 128:(ti + 1) * 128], wv_[:, c % 8, :], c == 0, c == 15, [ygT, wb_], ps,
                           inc=(c == 15))
                    ot = otmp[ti % 2]
                    xr = XR[tiles[ti]]
                    stt(ot.ap, ps.ap, rs.ap[:, ti:ti + 1], gt.ap[:, nb * 512:(nb + 1) * 512], ALU.mult, ALU.mult,
                        [ps, rs, gt], [ot])
                    tt(xr.ap[:, nb * 512:(nb + 1) * 512], xr.ap[:, nb * 512:(nb + 1) * 512], ot.ap, ALU.add, [xr, ot],
                       [xr], eng="dve")
            AR.pop()
            AR.pop()

    import os
    STAGE = int(os.environ.get("KSTAGE", "99"))
    if STAGE >= 1:
        modulation_fm(0)
    if STAGE >= 2:
        layer_a()
    if STAGE >= 3:
        modulation_fm(1)
    if STAGE >= 4:
        layer_c()
    dump("xr0_final", XR[0].ap, [XR[0]])
    dump("xr3_final", XR[3].ap, [XR[3]])
    AR.push()
    junk = Buf(AR.f32(D, "fjunk"))
    st = Buf(AR.f32(16, "fst"))
    gfb = Buf(AR.f32(D, "gfin_bc"))
    gfin_bc = gfb.ap
    P.dma("sp", gfin_bc, pbc(gfin), writes=[gfb])
    yo = [Buf(AR.f32(D, "yo%d" % i)) for i in range(2)]
    for t in range(12):
        s_ = st.ap[:, t:t + 1]
        act(junk.ap, XR[t].ap, AF.Square, [XR[t]], [junk, st], accum_out=s_)
        act(s_, s_, AF.Sqrt, [st, small], [st], scale=1.0 / D, bias=eps_s)
        recip(s_, s_, [st], [st])
        y_ = yo[t % 2]
        stt(y_.ap, XR[t].ap, s_, gfin_bc, ALU.mult, ALU.mult, [XR[t], st, gfb], [y_])
        P.dma("sp", y_all[t * 128:(t + 1) * 128, :], y_.ap, reads=[y_], is_output=True)
    AR.pop()
    P.finish()
    with nc.Block() as block:
        P.replay(block)
    es.close()
    return nc


def _consts():
    ident = np.eye(128, dtype=np.float32)
    s = np.arange(128)
    tmf = (s[:, None] <= s[None, :]).astype(np.float32)
    tmb = (s[:, None] >= s[None, :]).astype(np.float32)
    sel = np.zeros((64, 32), np.float32)
    for k in range(64):
        sel[k, k % 32] = 1.0
    rows_ = 1024 // 64
    row = np.repeat(np.arange(rows_, dtype=np.float32), 64)
    col = np.tile(np.arange(64, dtype=np.float32), rows_)
    n_freq = 8
    inv = np.power(np.float32(10000.0), -np.arange(n_freq, dtype=np.float32) / n_freq).astype(np.float32)
    ang = np.concatenate([row[:, None] * inv, col[:, None] * inv], axis=-1).astype(np.float32)
    ropecs = np.concatenate([np.cos(ang), np.sin(ang)], axis=-1).astype(np.float32)
    return ident, tmf, tmb, sel, ropecs


_NC_CACHE = {}


def kernel(x_prompt, x_sample, cache_ckv, cache_krope, state_ssd, c, c_ctx,
           w_mod, b_mod, g_pre, g_final,
           a_w_in, a_g_q, a_g_kv, a_w_uq, a_w_uk, a_w_uv, a_conv_w, a_conv_b, a_ln_g, a_ln_b, a_w_out,
           c_w_in, c_conv_w, c_conv_b, c_dt_bias, c_a_log, c_d, c_g_norm, c_w_out):
    f = lambda a: np.ascontiguousarray(np.asarray(a, dtype=np.float32))
    x_prompt, x_sample = f(x_prompt), f(x_sample)
    ident, tmf, tmb, sel, ropecs = _consts()
    fm = lambda v, n: f(np.asarray(v, np.float32).reshape(n, 128).T)
    b_mod = np.asarray(b_mod, np.float32)
    g_pre = np.asarray(g_pre, np.float32)
    shared = {
        "w_mod": f(w_mod),
        "bmodT": f(np.concatenate([fm(b_mod[0], 24), fm(b_mod[1], 24)], axis=1)),
        "bmod_row": f(b_mod.reshape(1, -1)),
        "gpreT": f(np.concatenate([fm(g_pre[0], 8), fm(g_pre[1], 8)], axis=1)),
        "gfin": f(np.asarray(g_final).reshape(1, -1)),
        "ropecs": ropecs,
        "a_w_in": f(a_w_in[0]), "a_gq": f(np.asarray(a_g_q[0]).reshape(1, -1)),
        "a_gkv": f(np.asarray(a_g_kv[0]).reshape(1, -1)),
        "a_w_uq": f(a_w_uq[0]), "a_w_uk": f(a_w_uk[0]), "a_w_uv": f(a_w_uv[0]),
        "a_convwT": f(np.asarray(a_conv_w[0], np.float32).reshape(31, 4, 128).transpose(2, 1, 0).reshape(128, 124)),
        "a_convb": f(np.asarray(a_conv_b[0]).reshape(1, -1)),
        "a_lngT": fm(a_ln_g[0], 4), "a_lnbT": fm(a_ln_b[0], 4),
        "a_w_out": f(a_w_out[0]),
        "c_w_in": f(c_w_in[0]),
        "c_convwT": f(np.asarray(c_conv_w[0], np.float32).reshape(5, 24, 128).transpose(2, 1, 0).reshape(128, 120)),
        "c_convb": f(np.asarray(c_conv_b[0]).reshape(1, -1)),
        "c_convbT": fm(c_conv_b[0], 24),
        "c_dtb": f(np.asarray(c_dt_bias[0]).reshape(1, 64)),
        "c_alog": f(np.asarray(c_a_log[0]).reshape(1, 64)),
        "c_dskip": f(np.asarray(c_d[0]).reshape(1, 32)),
        "c_gnT": fm(c_g_norm[0], 16),
        "c_w_out": f(c_w_out[0]),
        "k_ident": ident, "k_tmf": tmf, "k_tmb": tmb, "k_sel": sel,
    }
    c = np.asarray(c, np.float32)
    c_ctx = np.asarray(c_ctx, np.float32)
    in_maps = []
    for b in range(NCORES):
        m = dict(shared)
        m["x_all"] = f(np.concatenate([x_prompt[2 * b].reshape(256, D), x_prompt[2 * b + 1].reshape(256, D),
                                       x_sample[b]], axis=0))
        m["cckv"] = f(cache_ckv[b, 0])
        m["ckr"] = f(cache_krope[b, 0])
        m["st0"] = f(np.asarray(state_ssd[b, 0], np.float32).transpose(0, 3, 1, 2).reshape(2, 128, 2048))
        cond = np.stack([c_ctx, c[b]], axis=0)
        m["condT"] = f(cond.reshape(2, 8, 128).transpose(2, 1, 0).reshape(128, 16))
        in_maps.append(m)
    if "nc" not in _NC_CACHE:
        _NC_CACHE["nc"] = build_program()
    nc = _NC_CACHE["nc"]
    res = run_bass_kernel_spmd(nc, in_maps, core_ids=list(range(NCORES)))
    R = res.results
    import os as _os
    if _os.environ.get("KDBG"):
        _NC_CACHE["dbg"] = {k: np.asarray(v).astype(np.float32) for k, v in R[0].items() if k.startswith("dbg_")}
    y_prompt = np.zeros((16, 256, D), np.float32)
    y_sample = np.zeros((8, 1024, D), np.float32)
    n_ckv = np.zeros((16, 1, 256, 128), np.float32)
    n_kr = np.zeros((16, 1, 256, 32), np.float32)
    n_st = np.zeros((16, 1, 2, 32, 64, 128), np.float32)
    for b in range(NCORES):
        r = R[b]
        ya = np.asarray(r["y_all"])
        y_prompt[2 * b] = ya[0:256]
        y_prompt[2 * b + 1] = ya[256:512]
        y_sample[b] = ya[512:1536]
        ck = np.asarray(r["o_ckv"])
        kr = np.asarray(r["o_kr"])
        n_ckv[2 * b, 0] = ck[0:256]
        n_ckv[2 * b + 1, 0] = ck[256:512]
        n_kr[2 * b, 0] = kr[0:256]
        n_kr[2 * b + 1, 0] = kr[256:512]
        st = np.asarray(r["o_st"])
        for s_ in range(2):
            n_st[2 * b + s_, 0] = st[s_].reshape(2, 128, 32, 64).transpose(0, 2, 3, 1)
    return (y_prompt, y_sample, n_ckv, n_kr, n_st)
```

```python
import numpy as np
from contextlib import ExitStack
import concourse.bass as bass
import concourse.mybir as mybir
from concourse.bass_utils import run_bass_kernel_spmd
from concourse.alu_op_type import AluOpType as ALU

F32 = mybir.dt.float32
BF16 = mybir.dt.bfloat16
AF = mybir.ActivationFunctionType

D = 1024
NCORES = 8
EPS = 1e-6
ATTN_SCALE = 96 ** -0.5
ARENA_COLS = 53100


class Buf:
    __slots__ = ("ap", "w", "r", "name", "f32", "fence", "sticky")

    def __init__(self, ap, name=""):
        self.ap = ap
        self.f32 = Arena.LAST if ap.dtype == BF16 else None
        self.w = None
        self.r = {}
        self.name = name
        self.fence = False
        self.sticky = False

    def __getitem__(self, k):
        return self.ap[k]


class Prog:
    ENG = ("pe", "dve", "act", "pool", "sp")

    def __init__(self, nc, es):
        self.nc = nc
        self.lists = {e: [] for e in self.ENG}
        self.sem = {e: es.enter_context(nc.semaphore("sem_" + e)) for e in self.ENG}
        self.cnt = {e: 0 for e in self.ENG}
        self.waited = {e: {} for e in self.ENG}
        self.dsem = {}
        for q in ("pool", "sp"):
            self.dsem[q] = [[es.enter_context(nc.semaphore("dq_%s_%d" % (q, i))), 0] for i in range(24)]
        self.dptr = {"pool": 0, "sp": 0}
        self.out_toks = []
        self.pending_dma = []
        self.persist_ids = set()

    def barrier(self):
        toks = [(self.sem[e], self.cnt[e]) for e in self.ENG if self.cnt[e] > 0] + self.pending_dma
        self.pending_dma = []
        for e in self.ENG:
            for (sem, val) in toks:
                if sem is self.sem[e]:
                    continue
                k = id(sem)
                if self.waited[e].get(k, 0) >= val:
                    continue
                self.waited[e][k] = val
                self.lists[e].append(("w", sem, val))

    def _wait(self, eng, tok, raw=False):
        if tok is None:
            return
        sem, val = tok
        if sem is self.sem[eng]:
            if eng == "pe" or not raw:
                return
        k = id(sem)
        if self.waited[eng].get(k, 0) >= val:
            return
        self.waited[eng][k] = val
        self.lists[eng].append(("w", sem, val))

    def _deps(self, eng, reads, writes):
        for b in reads:
            self._wait(eng, b.w, raw=True)
        for b in writes:
            self._wait(eng, b.w, raw=b.fence)
            for t in b.r.values():
                self._wait(eng, t)

    def _mark(self, tok, reads, writes):
        k = id(tok[0])
        for b in reads:
            b.r[k] = tok
        for b in writes:
            b.w = tok
            b.r = {}
            b.fence = b.sticky

    def op(self, eng, fn, reads=(), writes=(), inc=True):
        self._deps(eng, reads, writes)
        self.lists[eng].append(("o", fn, inc))
        tok = (self.sem[eng], self.cnt[eng] + 1)
        if inc:
            self.cnt[eng] += 1
        self._mark(tok, reads, writes)

    def dma(self, q, out_ap, in_ap, reads=(), writes=(), is_output=False, **kw):
        self._deps(q, reads, writes)
        pool = self.dsem[q]
        i = self.dptr[q]
        self.dptr[q] = (i + 1) % len(pool)
        sem, uses = pool[i]
        if uses > 0:
            self._wait(q, (sem, 16 * uses))
        pool[i][1] = uses + 1
        tok = (sem, 16 * (uses + 1))
        self.lists[q].append(("d", out_ap, in_ap, sem, kw))
        self._mark(tok, reads, writes)
        if not writes or not all(id(b) in self.persist_ids for b in writes):
            self.pending_dma.append(tok)
        if is_output:
            self.out_toks.append(tok)

    def finish(self):
        for tok in self.out_toks:
            self._wait("sp", tok)

    def replay(self, block):
        nc = self.nc
        names = {"pe": "tensor", "dve": "vector", "act": "scalar", "pool": "gpsimd", "sp": "sync"}
        for e in self.ENG:
            items = self.lists[e]
            sem_e = self.sem[e]

            def body(eng, items=items, sem_e=sem_e):
                for it in items:
                    if it[0] == "w":
                        eng.wait_ge(it[1], it[2])
                    elif it[0] == "o":
                        ins = it[1](eng)
                        if it[2]:
                            ins.then_inc(sem_e, 1)
                    else:
                        eng.dma_start(out=it[1], in_=it[2], **it[4]).then_inc(it[3], 16)

            getattr(block, names[e])(body)


class Arena:
    LAST = None

    def __init__(self, ap, ncols):
        self.ap = ap
        self.n = ncols
        self.top = 0
        self.stack = []
        self.on_pop = None
        self.need_barrier = False

    def push(self):
        self.stack.append(self.top)

    def pop(self):
        self.top = self.stack.pop()
        self.need_barrier = True

    def _sync(self):
        if self.need_barrier and self.on_pop is not None:
            self.on_pop()
        self.need_barrier = False

    def f32(self, cols, name="", parts=128):
        self._sync()
        cols_al = (cols + 15) // 16 * 16
        off = self.top
        self.top += cols_al
        assert self.top <= self.n, "arena overflow at %s: %d" % (name, self.top)
        return self.ap[0:parts, off:off + cols]

    def bf16(self, cols, name="", parts=128):
        self._sync()
        c32 = (cols + 1) // 2
        c32a = (c32 + 15) // 16 * 16
        off = self.top
        self.top += c32a
        assert self.top <= self.n, "arena overflow at %s: %d" % (name, self.top)
        self.last_f32 = self.ap[0:parts, off:off + c32]
        Arena.LAST = self.last_f32
        return self.ap[0:parts, off:off + c32].bitcast(BF16)[:, 0:cols]


def bcast_last(ap, n):
    return ap.unsqueeze(len(ap.shape)).to_broadcast(list(ap.shape) + [n])


def build_program(debug=False):
    nc = bass.Bass("TRN2", target_bir_lowering=False)
    es = ExitStack()

    def din(name, shape, dt=F32):
        return nc.dram_tensor(name, list(shape), dt, kind="ExternalInput").ap()

    def dout(name, shape, dt=F32):
        return nc.dram_tensor(name, list(shape), dt, kind="ExternalOutput").ap()

    x_all = din("x_all", [1536, D])
    cckv = din("cckv", [512, 128])
    ckr = din("ckr", [512, 32])
    st0 = din("st0", [2, 128, 2048])
    condT = din("condT", [128, 16])
    w_mod = din("w_mod", [2, D, 3072])
    bmodT = din("bmodT", [128, 2 * 24])
    bmod_row = din("bmod_row", [1, 2 * 3072])
    gpreT = din("gpreT", [128, 16])
    gfin = din("gfin", [1, D])
    ropecs = din("ropecs", [1024, 32])
    a_w_in = din("a_w_in", [D, 2464])
    a_gq = din("a_gq", [1, 256])
    a_gkv = din("a_gkv", [1, 128])
    a_w_uq = din("a_w_uq", [256, 768])
    a_w_uk = din("a_w_uk", [128, 512])
    a_w_uv = din("a_w_uv", [128, 512])
    a_convwT = din("a_convwT", [128, 4 * 31])
    a_convb = din("a_convb", [1, 512])
    a_lngT = din("a_lngT", [128, 4])
    a_lnbT = din("a_lnbT", [128, 4])
    a_w_out = din("a_w_out", [D, D])
    c_w_in = din("c_w_in", [D, 5184])
    c_convwT = din("c_convwT", [128, 24 * 5])
    c_convb = din("c_convb", [1, 3072])
    c_dtb = din("c_dtb", [1, 64])
    c_alog = din("c_alog", [1, 64])
    c_dskip = din("c_dskip", [1, 32])
    c_gnT = din("c_gnT", [128, 16])
    c_w_out = din("c_w_out", [2048, D])
    k_ident = din("k_ident", [128, 128])
    k_tmf = din("k_tmf", [128, 128])
    k_tmb = din("k_tmb", [128, 128])
    k_sel = din("k_sel", [64, 32])
    c_convbT = din("c_convbT", [128, 24])

    y_all = dout("y_all", [1536, D])
    o_ckv = dout("o_ckv", [512, 128])
    o_kr = dout("o_kr", [512, 32])
    o_st = dout("o_st", [2, 2, 128, 2048])
    import os as _os
    DBG = _os.environ.get("KDBG")

    def dump(name, ap, bufs):
        if DBG:
            d = nc.dram_tensor("dbg_" + name, list(ap.shape), ap.dtype, kind="ExternalOutput").ap()
            P.dma("sp", d, ap, reads=bufs, is_output=True)

    arena_t = es.enter_context(nc.sbuf_tensor("arena", [128, ARENA_COLS], F32))
    AR = Arena(arena_t, ARENA_COLS)
    P = Prog(nc, es)
    AR.on_pop = P.barrier
    banks = []
    for i in range(8):
        pt = es.enter_context(nc.psum_tensor("psb%d" % i, [128, 512], F32))
        banks.append(Buf(pt[:, :], "bank%d" % i))
    bank_i = [0]

    wide = [True]
    bank_w = [0]

    def bank():
        if wide[0]:
            b = banks[bank_w[0]]
            bank_w[0] = (bank_w[0] + 1) % 8
            return b
        b = banks[4 + bank_i[0]]
        bank_i[0] = (bank_i[0] + 1) % 4
        return b

    def lbank(i):
        return banks[i % 4]

    def keep_warm(n_):
        dst = banks[3].ap
        for _ in range(n_):
            P.op("pe", lambda e: e.matmul(dst, ident_b.ap, ones_b.ap, start=True, stop=True),
                 reads=[ident_b, ones_b], writes=[banks[3]], inc=False)

    def mm(ps_ap, lhsT, rhs, start, stop, reads, ps_buf, inc):
        P.op("pe", lambda e: e.matmul(ps_ap, lhsT, rhs, start=start, stop=stop),
             reads=reads, writes=[ps_buf], inc=inc)

    def tr(ps_ap, in_ap, ident_ap, reads, ps_buf, inc):
        P.op("pe", lambda e: e.transpose(ps_ap, in_ap, ident_ap), reads=reads, writes=[ps_buf], inc=inc)

    def act(out, in_, func, reads, writes, eng="act", **kw):
        P.op(eng, lambda e: e.activation(out=out, in_=in_, func=func, **kw), reads=reads, writes=writes)

    def tt(out, in0, in1, op, reads, writes, eng="dve"):
        P.op(eng, lambda e: e.tensor_tensor(out=out, in0=in0, in1=in1, op=op), reads=reads, writes=writes)

    def ts(out, in0, s1, s2, op0, op1, reads, writes, eng="dve"):
        if op1 is None:
            P.op(eng, lambda e: e.tensor_scalar(out=out, in0=in0, scalar1=s1, scalar2=None, op0=op0),
                 reads=reads, writes=writes)
        else:
            P.op(eng, lambda e: e.tensor_scalar(out=out, in0=in0, scalar1=s1, scalar2=s2, op0=op0, op1=op1),
                 reads=reads, writes=writes)

    def stt(out, in0, scalar, in1, op0, op1, reads, writes):
        P.op("dve", lambda e: e.scalar_tensor_tensor(out=out, in0=in0, scalar=scalar, in1=in1, op0=op0, op1=op1),
             reads=reads, writes=writes)

    def cp(out, in_, reads, writes, eng="dve"):
        if eng == "act":
            P.op("act", lambda e: e.copy(out=out, in_=in_), reads=reads, writes=writes)
        else:
            P.op(eng, lambda e: e.tensor_copy(out=out, in_=in_), reads=reads, writes=writes)

    def memset(ap, val, writes, eng="pool"):
        P.op(eng, lambda e: e.memset(ap, val), writes=writes)
        for b_ in writes:
            b_.fence = True

    BF16_ONE_PAIR = float(np.frombuffer(np.array([0x3F803F80], dtype=np.uint32).tobytes(), dtype=np.float32)[0])

    def fill_bf16(buf, val):
        v = 0.0 if val == 0.0 else BF16_ONE_PAIR
        P.op("dve", lambda e: e.memset(buf.f32, v), writes=[buf])
        buf.fence = True

    def recip(out, in_, reads, writes):
        P.op("dve", lambda e: e.reciprocal(out=out, in_=in_), reads=reads, writes=writes)

    guard = AR.f32(512, "guard")
    XR = [Buf(AR.f32(D, "xr%d" % t), "xr%d" % t) for t in range(12)]
    ident_f = Buf(AR.f32(128, "identf"))
    ident_b = Buf(AR.bf16(128, "identb"))
    tmf = Buf(AR.f32(128, "tmf"))
    tmb = Buf(AR.f32(128, "tmb"))
    ones_f = Buf(AR.f32(128, "onesf"))
    ones_b = Buf(AR.bf16(512, "onesb"))
    tmf_b = Buf(AR.bf16(128, "tmfb"))
    tmb_b = Buf(AR.bf16(128, "tmbb"))
    sel = Buf(AR.bf16(32, "sel", parts=64))
    NWB = 2
    WB = [Buf(AR.bf16(8 * 512, "wb%d" % i)) for i in range(NWB)]
    wb_i = [0]
    P.persist_ids = set(id(b_) for b_ in WB)
    small = Buf(AR.f32(1024, "small"))
    sm_off = [0]

    def sm(cols):
        o = sm_off[0]
        sm_off[0] += cols
        assert sm_off[0] <= 1024
        return small.ap[:, o:o + cols]

    condT_s = sm(16)
    bmodT_s = sm(48)
    gpreT_s = sm(16)
    lngT_s = sm(4)
    lnbT_s = sm(4)
    gnT_s = sm(16)
    aconvw_s = sm(124)
    cconvw_s = sm(120)
    eps_s = sm(1)
    modT = sm(2 * 48)
    gsT = sm(2 * 16)
    cconvb_s = sm(24)
    bcs = Buf(AR.f32(1024 + 256 + 128 + 64 + 64 + 32, "bcs"))
    gq_bc = bcs.ap[:, 1024:1280]
    gkv_bc = bcs.ap[:, 1280:1408]
    dtb_bc = bcs.ap[:, 1408:1472]
    aneg_bc = bcs.ap[:, 1472:1536]
    dsk_bc = bcs.ap[:, 1536:1568]
    scT = Buf(AR.bf16(16, "scT"))
    rope_s = Buf(AR.f32(8 * 32, "rope"))

    P.dma("sp", ident_f.ap, k_ident, writes=[ident_f])
    P.dma("pool", ident_b.ap, k_ident, writes=[ident_b])
    P.dma("sp", tmf.ap, k_tmf, writes=[tmf])
    P.dma("sp", tmb.ap, k_tmb, writes=[tmb])
    cp(tmf_b.ap, tmf.ap, [tmf], [tmf_b])
    cp(tmb_b.ap, tmb.ap, [tmb], [tmb_b])
    P.dma("pool", sel.ap, k_sel, writes=[sel])
    memset(ones_f.ap, 1.0, [ones_f])
    fill_bf16(ones_b, 1.0)
    for (dst, src) in ((condT_s, condT), (bmodT_s, bmodT), (gpreT_s, gpreT), (lngT_s, a_lngT), (lnbT_s, a_lnbT),
                       (gnT_s, c_gnT), (aconvw_s, a_convwT), (cconvw_s, c_convwT), (cconvb_s, c_convbT)):
        P.dma("sp", dst, src, writes=[small])
    memset(eps_s, EPS, [small], eng="dve")
    def pbc(src):
        return src.rearrange("o n -> (o n)").partition_broadcast(128)

    for (dst, src) in ((gq_bc, a_gq), (gkv_bc, a_gkv), (dtb_bc, c_dtb), (aneg_bc, c_alog), (dsk_bc, c_dskip)):
        P.dma("sp", dst, pbc(src), writes=[bcs])
    act(aneg_bc, aneg_bc, AF.Exp, [bcs], [bcs])
    ts(aneg_bc, aneg_bc, -1.0, None, ALU.mult, None, [bcs], [bcs])
    P.dma("sp", rope_s.ap.rearrange("p (t c) -> p t c", t=8), ropecs.rearrange("(t p) c -> p t c", p=128),
          writes=[rope_s])
    for t in range(12):
        P.dma("sp", XR[t].ap, x_all[t * 128:(t + 1) * 128, :], writes=[XR[t]])

    act(scT.ap, condT_s, AF.Silu, [small], [scT])

    def load_wblock(src_ap, ncols, kchunks=8, parts=128):
        wb = WB[wb_i[0]]
        wb_i[0] = (wb_i[0] + 1) % NWB
        view = wb.ap[0:parts, 0:kchunks * ncols].rearrange("p (k n) -> p k n", k=kchunks)
        P.dma("pool", view, src_ap.rearrange("(k p) n -> p k n", p=parts), writes=[wb])
        return wb, view

    def modulation_fm(l):
        wide[0] = False
        psm = lbank(0)
        psm_v = psm.ap[:, 0:32].rearrange("p (j g) -> p j g", g=2)
        for blk in range(4):
            wb, wv = load_wblock(w_mod[l][:, blk * 512:(blk + 1) * 512], 512)
            for jj in range(4):
                j = blk * 4 + jj
                for k in range(8):
                    mm(psm_v[:, j, :], wv[:, k, jj * 128:(jj + 1) * 128],
                       scT.ap.rearrange("p (k g) -> p k g", g=2)[:, k, :],
                       k == 0, k == 7, [wb, scT], psm, inc=(k == 7 and jj == 3))
        mo = modT[:, l * 32:(l + 1) * 32].rearrange("p (j g) -> p j g", g=2)
        tt(mo, psm_v, bcast_last(bmodT_s[:, l * 24: l * 24 + 16], 2), ALU.add, [psm, small], [small])
        gs = gsT[:, l * 16:(l + 1) * 16].rearrange("p (k g) -> p k g", g=2)
        ts(gs, mo[:, 8:16, :], 1.0, None, ALU.add, None, [small], [small])
        tt(gs, gs, bcast_last(gpreT_s[:, l * 8:(l + 1) * 8], 2), ALU.mult, [small], [small])
        wide[0] = True

    def gate_bc(l, g, gt):
        AR.push()
        screp = Buf(AR.bf16(8 * 128, "screp"))
        brow = Buf(AR.f32(1024, "brow"))
        srv = screp.ap.rearrange("p (k m) -> p k m", m=128)
        cp(srv, bcast_last(scT.ap.rearrange("p (k g) -> p k g", g=2)[:, :, g], 128), [scT], [screp])
        P.dma("sp", brow.ap, pbc(bmod_row[:, l * 3072 + 2048: l * 3072 + 3072]), writes=[brow])
        for nb in range(2):
            wb, wv = load_wblock(w_mod[l][:, 2048 + nb * 512: 2048 + (nb + 1) * 512], 512)
            pg = bank()
            for k in range(8):
                mm(pg.ap, srv[:, k, :], wv[:, k, :], k == 0, k == 7, [wb, screp], pg, inc=(k == 7))
            tt(gt.ap[:, nb * 512:(nb + 1) * 512], pg.ap, brow.ap[:, nb * 512:(nb + 1) * 512], ALU.add, [pg, brow], [gt])
        AR.pop()

    def build_hT(l, tiles, g, hT):
        nt = len(tiles)
        hv = hT.ap.rearrange("p (k t) -> p k t", k=8)
        AR.push()
        xn = [Buf(AR.bf16(D, "xn%d" % i)) for i in range(4)]
        junk = Buf(AR.f32(D, "junk"))
        st = Buf(AR.f32(8, "hst"))
        for tb in range(0, nt, 4):
            for i in range(4):
                t = tiles[tb + i]
                ssq = st.ap[:, i:i + 1]
                act(junk.ap, XR[t].ap, AF.Square, [XR[t]], [junk, st], accum_out=ssq)
                act(ssq, ssq, AF.Sqrt, [st, small], [st], scale=1.0 / D, bias=eps_s)
                recip(ssq, ssq, [st], [st])
                act(xn[i].ap, XR[t].ap, AF.Identity, [XR[t], st], [xn[i]], scale=ssq)
            for k in range(8):
                ps = bank()
                psb = ps.ap.bitcast(BF16)
                for i in range(4):
                    tr(psb[:, i * 128:(i + 1) * 128], xn[i].ap[:, k * 128:(k + 1) * 128], ident_b.ap,
                       [xn[i], ident_b], ps, inc=(i == 3))
                act(hv[:, k, tb * 128:(tb + 4) * 128], psb[:, 0:512], AF.Identity, [ps, small], [hT],
                    scale=gsT[:, l * 16 + k * 2 + g: l * 16 + k * 2 + g + 1],
                    bias=modT[:, l * 32 + k * 2 + g: l * 32 + k * 2 + g + 1])
        AR.pop()

    def layer_a():
        import os
        KSUB = int(os.environ.get("KSUB", "99"))
        KGRP = int(os.environ.get("KGRP", "2"))
        l = 0
        groups = [dict(g=0, tiles=[0, 1, 2, 3], seqs=[(0, 2), (2, 2)], ctx=0),
                  dict(g=1, tiles=list(range(4, 12)), seqs=[(0, 8)], ctx=4)]
        for grp in ([groups[1]] if os.environ.get("KGSEL") == "1" else groups[:KGRP]):
            g = grp["g"]
            tiles = grp["tiles"]
            nt = len(tiles)
            Tg = nt * 128
            nctx = grp["ctx"]
            nkt = nt + nctx
            Lk = nkt * 128
            AR.push()
            hT = Buf(AR.bf16(8 * Tg, "hT"))
            build_hT(l, tiles, g, hT)
            hv = hT.ap.rearrange("p (k t) -> p k t", k=8)
            mixTa = Buf(AR.bf16(4 * Tg, "mixTa"))
            mav = mixTa.ap.rearrange("p (h t) -> p h t", h=4)
            mixTc = Buf(AR.bf16(4 * Tg, "mixTc"))
            mcv = mixTc.ap.rearrange("p (j t) -> p j t", j=4)
            if KSUB < 1:
                AR.pop()
                continue
            AR.push()
            seqs = grp["seqs"]
            Ls = seqs[0][1] * 128
            LP = Ls + 30
            cvin = Buf(AR.bf16(4 * len(seqs) * LP, "cvin"))
            cvin_f32 = AR.last_f32
            cvv = cvin.ap.rearrange("p (j s t) -> p j s t", j=4, s=len(seqs))
            sgc = Buf(AR.bf16(4 * Tg, "sgc"))
            sgv = sgc.ap.rearrange("p (j t) -> p j t", j=4)
            sig = Buf(AR.f32(512, "sig"))
            xnb = Buf(AR.bf16(4 * 512, "xnb"))
            lst = Buf(AR.f32(16, "lst"))
            abrow = Buf(AR.bf16(512, "abrow"))
            fill_bf16(abrow, 0.0)
            P.dma("pool", abrow.ap[0:1, :], a_convb, writes=[abrow])
            memset(cvin_f32, 0.0, [cvin], eng="dve")
            wga, vga = load_wblock(a_w_in[:, 416:928], 512)
            wgb, vgb = load_wblock(a_w_in[:, 928:1440], 512)
            for j in range(4):
                for nb in range(Tg // 512):
                    pa = bank()
                    pb = bank()
                    for k in range(8):
                        mm(pa.ap, vga[:, k, j * 128:(j + 1) * 128], hv[:, k, nb * 512:(nb + 1) * 512], k == 0, k == 7,
                           [wga, hT], pa, inc=(k == 7))
                    for k in range(8):
                        mm(pb.ap, vgb[:, k, j * 128:(j + 1) * 128], hv[:, k, nb * 512:(nb + 1) * 512], k == 0, k == 7,
                           [wgb, hT], pb, inc=(k == 7))
                    act(sig.ap, pb.ap, AF.Sigmoid, [pb], [sig])
                    nseq_blk = 512 // Ls if Ls < 512 else 1
                    for si in range(nseq_blk):
                        if Ls < 512:
                            s_idx = nb * nseq_blk + si
                            c0, c1, d0 = si * Ls, (si + 1) * Ls, 15
                        else:
                            s_idx = 0
                            c0, c1, d0 = 0, 512, 15 + nb * 512
                        tt(cvv[:, j, s_idx, d0:d0 + (c1 - c0)], pa.ap[:, c0:c1], sig.ap[:, c0:c1], ALU.mult,
                           [pa, sig], [cvin])
            if g == 0:
                dump("abrow", abrow.ap, [abrow])
                dump("onesb", ones_b.ap, [ones_b])
                dump("cvin", cvin.ap, [cvin])
            wgc, vgc = load_wblock(a_w_in[:, 1952:2464], 512)
            for j in range(4):
                for nb in range(Tg // 512):
                    pa = bank()
                    for k in range(8):
                        mm(pa.ap, vgc[:, k, j * 128:(j + 1) * 128], hv[:, k, nb * 512:(nb + 1) * 512], k == 0, k == 7,
                           [wgc, hT], pa, inc=(k == 7))
                    act(sgv[:, j, nb * 512:(nb + 1) * 512], pa.ap, AF.Silu, [pa], [sgc])
            dgl = []
            for j in range(4):
                dg = Buf(AR.bf16(31 * 128, "dgj%d" % j))
                dgl.append(dg)
                tt(dg.ap.rearrange("p (k m) -> p k m", k=31),
                   ident_b.ap.unsqueeze(1).to_broadcast([128, 31, 128]),
                   bcast_last(aconvw_s[:, j * 31:(j + 1) * 31], 128), ALU.mult, [ident_b, small], [dg])
            if g == 0:
                dump("dg0", dgl[0].ap, [dgl[0]])
                dump("sgc", sgc.ap, [sgc])
            csb2 = [Buf(AR.f32(512, "convsb%d" % i)) for i in range(2)]
            cjunk = Buf(AR.f32(512, "cjunk"))
            for tb in range(0, nt, 4):
                for i in range(4):
                    ti = tb + i
                    s_idx = ti // seqs[0][1]
                    t0 = (ti % seqs[0][1]) * 128
                    cs_ = csb2[ti % 2]
                    for j in range(4):
                        ps = bank()
                        dgv = dgl[j].ap.rearrange("p (k m) -> p k m", k=31)
                        for k in range(31):
                            mm(ps.ap[:, 0:128], cvv[:, j, s_idx, t0 + k:t0 + k + 128], dgv[:, k, :],
                               k == 0, False, [cvin, dgl[j]], ps, inc=False)
                        mm(ps.ap[:, 0:128], ones_b.ap[:, 0:128],
                           abrow.ap[:, j * 128:(j + 1) * 128], False, True, [ones_b, abrow], ps, inc=True)
                        cp(cs_.ap[:, j * 128:(j + 1) * 128], ps.ap[:, 0:128], [ps], [cs_], eng="act")
                    act(cjunk.ap, cs_.ap, AF.Identity, [cs_], [cjunk, lst], accum_out=lst.ap[:, 0:1])
                    act(cjunk.ap, cs_.ap, AF.Square, [cs_], [cjunk, lst], accum_out=lst.ap[:, 1:2])
                    ts(lst.ap[:, 2:4], lst.ap[:, 0:2], 1.0 / 512, None, ALU.mult, None, [lst], [lst])
                    tt(lst.ap[:, 4:5], lst.ap[:, 2:3], lst.ap[:, 2:3], ALU.mult, [lst], [lst])
                    tt(lst.ap[:, 5:6], lst.ap[:, 3:4], lst.ap[:, 4:5], ALU.subtract, [lst], [lst])
                    act(lst.ap[:, 8:9], lst.ap[:, 5:6], AF.Sqrt, [lst, small], [lst], scale=1.0, bias=eps_s)
                    recip(lst.ap[:, 8:9], lst.ap[:, 8:9], [lst], [lst])
                    ts(xnb.ap[:, i * 512:(i + 1) * 512], cs_.ap, lst.ap[:, 2:3], lst.ap[:, 8:9], ALU.subtract, ALU.mult,
                       [cs_, lst], [xnb])
                if g == 0 and tb == 0:
                    dump("xnb", xnb.ap, [xnb])
                    dump("lst", lst.ap, [lst])
                for j in range(4):
                    ps = bank()
                    psb = ps.ap.bitcast(BF16)
                    for i in range(4):
                        tr(psb[:, i * 128:(i + 1) * 128], xnb.ap[:, i * 512 + j * 128: i * 512 + (j + 1) * 128],
                           ident_b.ap, [xnb, ident_b], ps, inc=(i == 3))
                    act(sig.ap, psb[:, 0:512], AF.Silu, [ps, small], [sig], scale=lngT_s[:, j:j + 1],
                        bias=lnbT_s[:, j:j + 1])
                    tt(mcv[:, j, tb * 128:(tb + 4) * 128], sig.ap, sgv[:, j, tb * 128:(tb + 4) * 128], ALU.mult,
                       [sig, sgc], [mixTc])
            AR.pop()

            if KSUB < 2:
                AR.pop()
                continue
            q_tm = Buf(AR.bf16(nt * 768, "q_tm"))
            qv = q_tm.ap.rearrange("p (t c) -> p t c", t=nt)
            ckvT = Buf(AR.bf16(Lk, "ckvT"))
            krT = Buf(AR.bf16(Lk, "krT", parts=96))
            VA = Buf(AR.bf16(nkt * 1024, "VA"))
            vav = VA.ap.rearrange("p (t h c) -> p t h c", t=nkt, h=8)
            wsm = Buf(AR.bf16(2 * 768 + 512 + 512, "wsm"))
            wuq = wsm.ap[:, 0:1536].rearrange("p (k n) -> p k n", k=2)
            wuk = wsm.ap[:, 1536:2048]
            wuv = wsm.ap[:, 2048:2560]
            P.dma("pool", wuq, a_w_uq.rearrange("(k p) n -> p k n", p=128), writes=[wsm])
            P.dma("pool", wuk, a_w_uk, writes=[wsm])
            P.dma("pool", wuv, a_w_uv, writes=[wsm])
            fill_bf16(VA, 1.0)

            AR.push()
            wb, wv = load_wblock(a_w_in[:, 0:416], 416)
            qnT = Buf(AR.bf16(2 * Tg, "qnT"))
            qnv = qnT.ap.rearrange("p (k t) -> p k t", k=2)
            tmp = Buf(AR.f32(512, "p2tmp"))
            st = Buf(AR.f32(8, "p2st"))
            qn = Buf(AR.bf16(256, "qn"))
            ckv_f = Buf(AR.f32(128, "ckv_f"))
            ckv_b = Buf(AR.bf16(128, "ckv_b"))
            kst = Buf(AR.bf16(96, "kst"))
            kr_f = Buf(AR.f32(32, "kr_f"))
            rt = Buf(AR.f32(64, "ropetmp"))
            qfb = [Buf(AR.f32(384, "qf%d" % i)) for i in range(2)]
            fill_bf16(kst, 0.0)

            def k_transposes(kt_idx, ckv_b, kst):
                ps = bank()
                psb = ps.ap.bitcast(BF16)
                tr(psb[:, 0:128], ckv_b.ap, ident_b.ap, [ckv_b, ident_b], ps, inc=False)
                tr(psb[0:96, 128:256], kst.ap, ident_b.ap, [kst, ident_b], ps, inc=True)
                cp(ckvT.ap[:, kt_idx * 128:(kt_idx + 1) * 128], psb[:, 0:128], [ps], [ckvT], eng="act")
                cp(krT.ap[64:96, kt_idx * 128:(kt_idx + 1) * 128], psb[64:96, 128:256], [ps], [krT], eng="act")

            for c in range(nctx):
                P.dma("sp", ckv_f.ap, cckv[c * 128:(c + 1) * 128, :], writes=[ckv_f])
                P.dma("sp", kr_f.ap, ckr[c * 128:(c + 1) * 128, :], writes=[kr_f])
                cp(ckv_b.ap, ckv_f.ap, [ckv_f], [ckv_b])
                cp(kst.ap[:, 64:96], kr_f.ap, [kr_f], [kst])
                k_transposes(c, ckv_b, kst)

            KP2 = int(os.environ.get("KP2", "99"))
            for ti in range(nt if KP2 >= 2 else 0):
                ps = bank()
                for k in range(8):
                    mm(ps.ap[:, 0:416], hv[:, k, ti * 128:(ti + 1) * 128], wv[:, k, :], k == 0, k == 7,
                       [hT, wb], ps, inc=(k == 7))
                act(tmp.ap[:, 0:256], ps.ap[:, 0:256], AF.Square, [ps], [tmp, st], accum_out=st.ap[:, 0:1])
                act(tmp.ap[:, 256:384], ps.ap[:, 256:384], AF.Square, [ps], [tmp, st], accum_out=st.ap[:, 1:2])
                act(st.ap[:, 0:1], st.ap[:, 0:1], AF.Sqrt, [st, small], [st], scale=1.0 / 256, bias=eps_s)
                act(st.ap[:, 1:2], st.ap[:, 1:2], AF.Sqrt, [st, small], [st], scale=1.0 / 128, bias=eps_s)
                recip(st.ap[:, 0:2], st.ap[:, 0:2], [st], [st])
                stt(qn.ap, ps.ap[:, 0:256], st.ap[:, 0:1], gq_bc, ALU.mult, ALU.mult, [ps, st, bcs], [qn])
                stt(ckv_f.ap, ps.ap[:, 256:384], st.ap[:, 1:2], gkv_bc, ALU.mult, ALU.mult, [ps, st, bcs], [ckv_f])
                cp(ckv_b.ap, ckv_f.ap, [ckv_f], [ckv_b])
                if g == 0:
                    cp(kr_f.ap, ps.ap[:, 384:416], [ps], [kr_f], eng="act")
                    P.dma("sp", o_ckv[ti * 128:(ti + 1) * 128, :], ckv_f.ap, reads=[ckv_f], is_output=True)
                    P.dma("sp", o_kr[ti * 128:(ti + 1) * 128, :], kr_f.ap, reads=[kr_f], is_output=True)
                    cp(kst.ap[:, 64:96], ps.ap[:, 384:416], [ps], [kst])
                else:
                    cs = rope_s.ap.rearrange("p (t c) -> p t c", t=8)[:, ti, :]
                    cosv, sinv = cs[:, 0:16], cs[:, 16:32]
                    kp = ps.ap[:, 384:416].rearrange("p (i two) -> p i two", two=2)
                    ko = kst.ap[:, 64:96].rearrange("p (i two) -> p i two", two=2)
                    tt(rt.ap[:, 0:16], kp[:, :, 0], cosv, ALU.mult, [ps, rope_s], [rt])
                    tt(rt.ap[:, 16:32], kp[:, :, 1], sinv, ALU.mult, [ps, rope_s], [rt])
                    tt(rt.ap[:, 32:48], kp[:, :, 0], sinv, ALU.mult, [ps, rope_s], [rt])
                    tt(rt.ap[:, 48:64], kp[:, :, 1], cosv, ALU.mult, [ps, rope_s], [rt])
                    tt(ko[:, :, 0], rt.ap[:, 0:16], rt.ap[:, 16:32], ALU.subtract, [rt], [kst])
                    tt(ko[:, :, 1], rt.ap[:, 32:48], rt.ap[:, 48:64], ALU.add, [rt], [kst])
                k_transposes(nctx + ti, ckv_b, kst)
                ps2 = bank()
                ps2b = ps2.ap.bitcast(BF16)
                for k in range(2):
                    tr(ps2b[:, k * 128:(k + 1) * 128], qn.ap[:, k * 128:(k + 1) * 128], ident_b.ap, [qn, ident_b], ps2,
                       inc=(k == 1))
                cp(qnv[:, :, ti * 128:(ti + 1) * 128], ps2b[:, 0:256].rearrange("p (k t) -> p k t", k=2), [ps2], [qnT],
                   eng="act")
            for kt in range(nkt if KP2 >= 3 else 0):
                ps = bank()
                mm(ps.ap, ckvT.ap[:, kt * 128:(kt + 1) * 128], wuv, True, True, [ckvT, wsm], ps, inc=True)
                pvv = ps.ap.rearrange("p (j two c) -> p j two c", j=4, two=2)
                vv5 = VA.ap.rearrange("p (t j two c) -> p t j two c", t=nkt, j=4, two=2)
                cp(vv5[:, kt, :, 0, 0:64], pvv[:, :, 0, :], [ps], [VA], eng="act")
                cp(vv5[:, kt, :, 1, 64:128], pvv[:, :, 1, :], [ps], [VA], eng="act")
            for ti in range(nt if KP2 >= 4 else 0):
                for half in range(2):
                    ps = bank()
                    for k in range(2):
                        mm(ps.ap[:, 0:384], qnv[:, k, ti * 128:(ti + 1) * 128], wuq[:, k, half * 384:(half + 1) * 384],
                           k == 0, k == 1, [qnT, wsm], ps, inc=(k == 1))
                    pv = ps.ap[:, 0:384].rearrange("p (h c) -> p h c", h=4)
                    qo = qv[:, ti, half * 384:(half + 1) * 384].rearrange("p (h c) -> p h c", h=4)
                    if g == 0:
                        act(qo, pv, AF.Copy, [ps], [q_tm], scale=ATTN_SCALE)
                    else:
                        qf = qfb[(ti * 2 + half) % 2]
                        act(qf.ap, ps.ap[:, 0:384], AF.Copy, [ps], [qf], scale=ATTN_SCALE)
                        qfv = qf.ap.rearrange("p (h c) -> p h c", h=4)
                        cp(qo[:, :, 0:64], qfv[:, :, 0:64], [qf], [q_tm])
                        cs = rope_s.ap.rearrange("p (t c) -> p t c", t=8)[:, ti, :]
                        cosv, sinv = cs[:, 0:16], cs[:, 16:32]
                        cosb = cosv.unsqueeze(1).to_broadcast([128, 4, 16])
                        sinb = sinv.unsqueeze(1).to_broadcast([128, 4, 16])
                        pr = qfv[:, :, 64:96].rearrange("p h (i two) -> p h i two", two=2)
                        qr = qo[:, :, 64:96].rearrange("p h (i two) -> p h i two", two=2)
                        rr = tmp.ap[:, 0:256].rearrange("p (a h i) -> p a h i", a=4, h=4)
                        tt(rr[:, 0], pr[:, :, :, 0], cosb, ALU.mult, [qf, rope_s], [tmp])
                        tt(rr[:, 1], pr[:, :, :, 1], sinb, ALU.mult, [qf, rope_s], [tmp])
                        tt(rr[:, 2], pr[:, :, :, 0], sinb, ALU.mult, [qf, rope_s], [tmp])
                        tt(rr[:, 3], pr[:, :, :, 1], cosb, ALU.mult, [qf, rope_s], [tmp])
                        tt(qr[:, :, :, 0], rr[:, 0], rr[:, 1], ALU.subtract, [tmp], [q_tm])
                        tt(qr[:, :, :, 1], rr[:, 2], rr[:, 3], ALU.add, [tmp], [q_tm])
            AR.pop()

            if KSUB < 3:
                AR.pop()
                continue
            AR.push()
            wgt, vgt = load_wblock(a_w_in[:, 1440:1952], 512)
            QT = [Buf(AR.bf16(Tg, "QT%d" % i, parts=96)) for i in range(2)]
            KT = [Buf(AR.bf16(Lk, "KT%d" % i, parts=96)) for i in range(2)]
            sgT = [Buf(AR.bf16(Tg, "sgT%d" % i)) for i in range(2)]
            PT = [Buf(AR.bf16(512, "PT%d" % i)) for i in range(3)]
            Rr = [Buf(AR.f32(512, "Rr%d" % i)) for i in range(2)]
            t1 = [Buf(AR.f32(512, "t1%d" % i)) for i in range(2)]
            pti = 0
            poi = [0]
            wide[0] = False
            for h in range(8):
                qt, ktb, sg = QT[h % 2], KT[h % 2], sgT[(h // 2) % 2]
                vlo, vhi = (0, 64) if h % 2 == 0 else (64, 128)
                dlo, dhi = (64, 128) if h % 2 == 0 else (0, 64)
                for tb in range(0, nt, 8):
                    nn = min(8, nt - tb)
                    ps = bank()
                    psb = ps.ap.bitcast(BF16)
                    for i in range(nn):
                        tr(psb[0:96, i * 128:(i + 1) * 128], qv[:, tb + i, h * 96:(h + 1) * 96], ident_b.ap,
                           [q_tm, ident_b], ps, inc=(i == nn - 1))
                    cp(qt.ap[:, tb * 128:(tb + nn) * 128], psb[0:96, 0:nn * 128], [ps], [qt], eng="act")
                for nb in range(0, Lk, 512):
                    w = min(512, Lk - nb)
                    ps = bank()
                    mm(ps.ap[0:64, 0:w], wuk[:, h * 64:(h + 1) * 64], ckvT.ap[:, nb:nb + w], True, True, [wsm, ckvT], ps,
                       inc=True)
                    cp(ktb.ap[0:64, nb:nb + w], ps.ap[0:64, 0:w], [ps], [ktb], eng="act")
                cp(ktb.ap[64:96, :], krT.ap[64:96, :], [krT], [ktb])
                for nb in range(Tg // 512 if h % 2 == 0 else 0):
                    ps = bank()
                    for k in range(8):
                        mm(ps.ap, vgt[:, k, h * 64:(h + 2) * 64], hv[:, k, nb * 512:(nb + 1) * 512], k == 0,
                           k == 7, [wgt, hT], ps, inc=(k == 7))
                    act(sg.ap[:, nb * 512:(nb + 1) * 512], ps.ap, AF.Silu, [ps], [sg])
                for (s0, sl) in seqs:
                    q0 = s0 * 128
                    qlen = sl * 128
                    if g == 0:
                        ktiles = list(range(s0, s0 + sl))
                    else:
                        ktiles = list(range(nkt))
                    for qb in range(0, qlen, 512):
                        w = min(512, qlen - qb)
                        po = lbank(poi[0])
                        poi[0] += 1
                        for ki, kt in enumerate(ktiles):
                            ps = bank()
                            mm(ps.ap[:, 0:w], ktb.ap[0:96, kt * 128:(kt + 1) * 128], qt.ap[0:96, q0 + qb:q0 + qb + w],
                               True, True, [ktb, qt], ps, inc=True)
                            pt = PT[pti % 3]
                            pti += 1
                            act(pt.ap[:, 0:w], ps.ap[:, 0:w], AF.Exp, [ps], [pt])
                            mm(po.ap[:, 0:w], vav[:, kt, h, :], pt.ap[:, 0:w], ki == 0, ki == len(ktiles) - 1, [VA, pt], po,
                               inc=(ki == len(ktiles) - 1))
                        rr = Rr[h % 2]
                        tt1 = t1[h % 2]
                        recip(rr.ap[dlo:dhi, 0:w], po.ap[dlo:dhi, 0:w], [po], [rr])
                        tt(tt1.ap[vlo:vhi, 0:w], po.ap[vlo:vhi, 0:w], rr.ap[dlo:dhi, 0:w], ALU.mult, [po, rr], [tt1])
                        tt(mav[vlo:vhi, h // 2, q0 + qb:q0 + qb + w], tt1.ap[vlo:vhi, 0:w],
                           sg.ap[vlo:vhi, q0 + qb:q0 + qb + w], ALU.mult, [tt1, sg], [mixTa])
            AR.pop()

            wide[0] = True
            if KSUB < 4:
                AR.pop()
                continue
            AR.push()
            otmp = [Buf(AR.f32(512, "otmp%d" % i)) for i in range(2)]
            gt = Buf(AR.f32(D, "gt_a"))
            gate_bc(l, g, gt)
            dump("mixTa%d" % g, mixTa.ap, [mixTa])
            dump("mixTc%d" % g, mixTc.ap, [mixTc])
            dump("gt%d" % g, gt.ap, [gt])
            dump("hT%d" % g, hT.ap, [hT])
            for nb in range(2):
                wo_b, wo = load_wblock(a_w_out[:, nb * 512:(nb + 1) * 512], 512)
                for ti in range(nt):
                    ps = bank()
                    for c in range(8):
                        lhs = mav[:, c, ti * 128:(ti + 1) * 128] if c < 4 else mcv[:, c - 4, ti * 128:(ti + 1) * 128]
                        mm(ps.ap, lhs, wo[:, c, :], c == 0, c == 7, [mixTa, mixTc, wo_b], ps, inc=(c == 7))
                    ot = otmp[ti % 2]
                    xr = XR[tiles[ti]]
                    tt(ot.ap, ps.ap, gt.ap[:, nb * 512:(nb + 1) * 512], ALU.mult, [ps, gt], [ot])
                    if ti == 0 and g == 0:
                        dump("ot%d" % nb, ot.ap, [ot])
                        dump("xrpre%d" % nb, xr.ap[:, nb * 512:(nb + 1) * 512], [xr])
                    tt(xr.ap[:, nb * 512:(nb + 1) * 512], xr.ap[:, nb * 512:(nb + 1) * 512], ot.ap, ALU.add, [xr, ot],
                       [xr], eng="dve")
            AR.pop()
            AR.pop()

    def layer_c():
        dump("xr0_afterA", XR[0].ap, [XR[0]])
        l = 1
        HP = 4
        CW = HP * 64
        groups = [dict(g=0, tiles=[0, 1, 2, 3], seqs=[(0, 2), (2, 2)]),
                  dict(g=1, tiles=list(range(4, 12)), seqs=[(0, 8)])]
        wdt_b = Buf(AR.bf16(8 * 64, "wdt"))
        wdtv = wdt_b.ap.rearrange("p (k n) -> p k n", k=8)
        P.dma("pool", wdtv, c_w_in[:, 5120:5184].rearrange("(k p) n -> p k n", p=128), writes=[wdt_b])
        selv = sel.ap
        import os
        KCG = int(os.environ.get("KCG", "2"))
        KCPASS = int(os.environ.get("KCPASS", "8"))
        KCSCAN = int(os.environ.get("KCSCAN", "1"))
        KCOUT = int(os.environ.get("KCOUT", "1"))
        def issue_pass_loads(hq_):
            sg_ = hq_ // 2
            wzx_ = WB[wb_i[0]]
            wb_i[0] = (wb_i[0] + 1) % NWB
            vzx_ = wzx_.ap[:, 0:8 * 512].rearrange("p (k n) -> p k n", k=8)
            P.dma("pool", vzx_[:, :, 0:CW], c_w_in[:, hq_ * CW:(hq_ + 1) * CW].rearrange("(k p) n -> p k n", p=128),
                  writes=[wzx_])
            P.dma("pool", vzx_[:, :, CW:2 * CW],
                  c_w_in[:, 2048 + hq_ * CW: 2048 + (hq_ + 1) * CW].rearrange("(k p) n -> p k n", p=128), writes=[wzx_])
            wbc_ = WB[wb_i[0]]
            wb_i[0] = (wb_i[0] + 1) % NWB
            vbc_ = wbc_.ap[:, 0:8 * 256].rearrange("p (k n) -> p k n", k=8)
            P.dma("pool", vbc_[:, :, 0:128],
                  c_w_in[:, 4096 + sg_ * 128: 4096 + (sg_ + 1) * 128].rearrange("(k p) n -> p k n", p=128), writes=[wbc_])
            P.dma("pool", vbc_[:, :, 128:256],
                  c_w_in[:, 4608 + sg_ * 128: 4608 + (sg_ + 1) * 128].rearrange("(k p) n -> p k n", p=128), writes=[wbc_])
            return (wzx_, vzx_, wbc_, vbc_)

        for grp in groups[:KCG]:
            g = grp["g"]
            tiles = grp["tiles"]
            seqs = grp["seqs"]
            nt = len(tiles)
            Tg = nt * 128
            AR.push()
            nxt_w = [issue_pass_loads(0) if KCPASS > 0 else None]
            hT = Buf(AR.bf16(8 * Tg, "hT"))
            build_hT(l, tiles, g, hT)
            hv = hT.ap.rearrange("p (k t) -> p k t", k=8)
            ygT = Buf(AR.bf16(16 * Tg, "ygT"))
            ygv = ygT.ap.rearrange("p (c t) -> p c t", c=16)
            A_tm = Buf(AR.f32(nt * 64, "A_tm"))
            eA = Buf(AR.f32(nt * 64, "eA"))
            cdec = Buf(AR.f32(nt * 64, "cdec"))
            wdt = Buf(AR.f32(nt * 64, "wdt_s"))
            wde = Buf(AR.f32(nt * 64, "wde_s"))
            ssq = Buf(AR.f32(nt * 8, "ssq"))
            Apc = [Buf(AR.bf16(Tg, "Apc%d" % d, parts=64)) for d in range(2)]
            v3 = lambda b_: b_.ap.rearrange("p (t c) -> p t c", t=nt)
            AR.push()
            a_s = Buf(AR.f32(nt * 64, "a_s"))
            a_res = Buf(AR.f32(nt * 64, "a_res"))
            a_hl = Buf(AR.bf16(nt * 128, "a_hl"))
            ahv = a_hl.ap.rearrange("p (t q c) -> p t q c", t=nt, q=2)
            arep = Buf(AR.bf16(nt * 256, "arep"))
            arv = arep.ap.rearrange("p (t d q r c) -> p t d q r c", t=nt, d=2, q=2, r=2)
            csb = Buf(AR.f32(128, "csb"))
            tmpd = Buf(AR.f32(64, "tmpd"))
            r1 = Buf(AR.f32(512, "r1", parts=64))
            for ti in range(nt):
                ps = bank()
                for k in range(8):
                    mm(ps.ap[:, 0:64], hv[:, k, ti * 128:(ti + 1) * 128], wdtv[:, k, :], k == 0, k == 7, [hT, wdt_b], ps,
                       inc=(k == 7))
                tt(tmpd.ap, ps.ap[:, 0:64], dtb_bc, ALU.add, [ps, bcs], [tmpd])
                act(tmpd.ap, tmpd.ap, AF.Exp, [tmpd], [tmpd])
                act(v3(wdt)[:, ti, :], tmpd.ap, AF.Ln, [tmpd], [wdt], bias=1.0)
                tt(v3(a_s)[:, ti, :], v3(wdt)[:, ti, :], aneg_bc, ALU.mult, [wdt, bcs], [a_s])
            a3 = a_s.ap.rearrange("p (t c) -> p t c", t=nt)
            cp(ahv[:, :, 0, :], a3, [a_s], [a_hl])
            tt(a_res.ap.rearrange("p (t c) -> p t c", t=nt), a3, ahv[:, :, 0, :], ALU.subtract, [a_s, a_hl], [a_res])
            cp(ahv[:, :, 1, :], a_res.ap.rearrange("p (t c) -> p t c", t=nt), [a_res], [a_hl])
            for q in range(2):
                for r_i in range(2):
                    cp(arv[:, :, :, q, r_i, :], ahv[:, :, q, :].rearrange("p t (d c) -> p t d c", d=2), [a_hl], [arep])
            for ti in range(nt):
                ps = bank()
                for q in range(2):
                    mm(ps.ap[:, 0:32], tmf_b.ap, ahv[:, ti, q, 0:32], q == 0, q == 1, [tmf_b, a_hl], ps, inc=False)
                for q in range(2):
                    mm(ps.ap[:, 32:64], tmb_b.ap, ahv[:, ti, q, 32:64], q == 0, q == 1, [tmb_b, a_hl], ps, inc=False)
                for q in range(2):
                    mm(ps.ap[:, 64:128], ones_b.ap[:, 0:128], ahv[:, ti, q, :], q == 0, q == 1, [ones_b, a_hl], ps,
                       inc=(q == 1))
                cp(csb.ap, ps.ap[:, 0:128], [ps], [csb])
                cp(v3(A_tm)[:, ti, :], csb.ap[:, 0:64], [csb], [A_tm], eng="pool")
                act(v3(eA)[:, ti, :], csb.ap[:, 0:64], AF.Exp, [csb], [eA])
                act(v3(cdec)[:, ti, :], csb.ap[:, 64:128], AF.Exp, [csb], [cdec])
                tt(tmpd.ap, csb.ap[:, 64:128], csb.ap[:, 0:64], ALU.subtract, [csb], [tmpd])
                act(tmpd.ap, tmpd.ap, AF.Exp, [tmpd], [tmpd])
                tt(v3(wde)[:, ti, :], tmpd.ap, v3(wdt)[:, ti, :], ALU.mult, [tmpd, wdt], [wde])
            for d in range(2):
                tmd = tmf_b if d == 0 else tmb_b
                for tb in range(0, nt, 4):
                    ps = bank()
                    for i in range(4):
                        for q in range(2):
                            mm(ps.ap[0:64, i * 128:(i + 1) * 128],
                               arv[:, tb + i, d, q, :, :].rearrange("p r c -> p (r c)"), tmd.ap, q == 0, q == 1,
                               [arep, tmd], ps, inc=(i == 3 and q == 1))
                    dst = Apc[d].ap[:, tb * 128:(tb + 4) * 128]
                    cp(dst, ps.ap[0:64, :], [ps], [Apc[d]])
                    tt(r1.ap[32:64, :], ps.ap[32:64, :], dst[32:64, :], ALU.subtract, [ps, Apc[d]], [r1])
                    cp(dst[32:64, :], r1.ap[32:64, :], [r1], [Apc[d]])
            AR.pop()

            for hq in range(KCPASS):
                sg = hq // 2
                h0 = hq * HP
                AR.push()
                sz = Buf(AR.bf16(nt * CW, "sz"))
                xs = Buf(AR.bf16(nt * CW, "xs"))
                Btm = Buf(AR.bf16(nt * 128, "Btm"))
                BT = Buf(AR.bf16(Tg, "BT"))
                CT = Buf(AR.bf16(Tg, "CT"))
                cbm = [Buf(AR.bf16(Tg, "cbm%d" % d)) for d in range(2)]
                prevb = Buf(AR.bf16(nt * CW, "prevb"))
                Sst = [Buf(AR.f32(CW, "S%d" % d)) for d in range(2)]
                brow = Buf(AR.bf16(CW + 128, "browc"))
                fill_bf16(brow, 0.0)
                Ls = seqs[0][1] * 128
                LP = Ls + 4
                raw = [Buf(AR.bf16(len(seqs) * LP, "raw%d" % i)) for i in range(2)]
                dg = [Buf(AR.bf16(5 * 128, "dgc%d" % i)) for i in range(2)]
                v4 = lambda b_: b_.ap.rearrange("p (t c) -> p t c", t=nt)
                for r_ in raw:
                    fill_bf16(r_, 0.0)
                P.dma("pool", brow.ap[0:1, 0:CW], c_convb[:, hq * CW:(hq + 1) * CW], writes=[brow])
                P.dma("pool", brow.ap[0:1, CW:CW + 128], c_convb[:, 2048 + sg * 128: 2048 + (sg + 1) * 128], writes=[brow])
                if nxt_w[0] is None:
                    nxt_w[0] = issue_pass_loads(hq)
                wzx, vzx, wbc_b, vbc = nxt_w[0]
                nxt_w[0] = None
                for ti in range(nt):
                    ps = bank()
                    for k in range(8):
                        mm(ps.ap[:, 0:CW], hv[:, k, ti * 128:(ti + 1) * 128], vzx[:, k, 0:CW], k == 0, k == 7, [hT, wzx], ps,
                           inc=(k == 7))
                    act(v4(sz)[:, ti, :], ps.ap[:, 0:CW], AF.Silu, [ps], [sz])
                chunk_specs = [("x", wzx, vzx[:, :, CW + j * 128: CW + (j + 1) * 128], hq * 2 + j, j) for j in range(2)]
                chunk_specs.append(("B", wbc_b, vbc[:, :, 0:128], 16 + sg, 0))
                chunk_specs.append(("C", wbc_b, vbc[:, :, 128:256], 20 + sg, 0))
                for ci, (kind, wbuf, wview, cch, j) in enumerate(chunk_specs):
                    rw = raw[ci % 2]
                    rv = rw.ap.rearrange("p (s t) -> p s t", s=len(seqs))
                    dgc = dg[ci % 2]
                    dgv = dgc.ap.rearrange("p (k m) -> p k m", k=5)
                    tt(dgv, ident_b.ap.unsqueeze(1).to_broadcast([128, 5, 128]),
                       bcast_last(cconvw_s[:, cch * 5:(cch + 1) * 5], 128), ALU.mult, [ident_b, small], [dgc])
                    for nb in range(Tg // 512):
                        ps = bank()
                        for k in range(8):
                            mm(ps.ap, wview[:, k, :], hv[:, k, nb * 512:(nb + 1) * 512], k == 0, k == 7, [wbuf, hT], ps,
                               inc=(k == 7))
                        if Ls < 512:
                            for si in range(512 // Ls):
                                cp(rv[:, nb * (512 // Ls) + si, 2:2 + Ls], ps.ap[:, si * Ls:(si + 1) * Ls], [ps], [rw],
                                   eng="act")
                        else:
                            cp(rv[:, 0, 2 + nb * 512: 2 + (nb + 1) * 512], ps.ap, [ps], [rw], eng="act")
                    if kind in ("x", "B"):
                        bias_row = brow.ap[:, j * 128:(j + 1) * 128] if kind == "x" else brow.ap[:, CW:CW + 128]
                        for tb in range(0, nt, 4):
                            ps = bank()
                            for i in range(4):
                                ti = tb + i
                                s_idx = ti // seqs[0][1]
                                t0 = (ti % seqs[0][1]) * 128
                                pso = ps.ap[:, i * 128:(i + 1) * 128]
                                for k in range(5):
                                    mm(pso, rv[:, s_idx, t0 + k:t0 + k + 128], dgv[:, k, :], k == 0, False, [rw, dgc], ps,
                                       inc=False)
                                mm(pso, ones_b.ap[:, 0:128], bias_row, False, True, [ones_b, brow], ps, inc=(i == 3))
                            pin = ps.ap.rearrange("p (i c) -> p i c", i=4)
                            if kind == "x":
                                act(v4(xs)[:, tb:tb + 4, j * 128:(j + 1) * 128], pin, AF.Silu, [ps], [xs])
                            else:
                                act(v4(Btm)[:, tb:tb + 4, :], pin, AF.Silu, [ps], [Btm])
                    if kind in ("B", "C"):
                        dstT = BT if kind == "B" else CT
                        for si in range(len(seqs)):
                            for nb in range(0, Ls, 512):
                                w = min(512, Ls - nb)
                                ps = bank()
                                for k in range(5):
                                    mm(ps.ap[:, 0:w], dgv[:, k, :], rv[:, si, nb + k:nb + k + w], k == 0, k == 4, [dgc, rw],
                                       ps, inc=(k == 4))
                                act(dstT.ap[:, si * Ls + nb: si * Ls + nb + w], ps.ap[:, 0:w], AF.Silu, [ps, small], [dstT],
                                    bias=cconvb_s[:, cch:cch + 1])
                if hq + 1 < KCPASS:
                    nxt_w[0] = issue_pass_loads(hq + 1)
                for tb in range(0, nt, 4):
                    ps = bank()
                    for i in range(4):
                        ti = tb + i
                        mm(ps.ap[:, i * 128:(i + 1) * 128], BT.ap[:, ti * 128:(ti + 1) * 128],
                           CT.ap[:, ti * 128:(ti + 1) * 128], True, True, [BT, CT], ps, inc=(i == 3))
                    pv = ps.ap.rearrange("p (i l) -> p i l", i=4)
                    for d in range(2):
                        tmd = tmf if d == 0 else tmb
                        tt(cbm[d].ap[:, tb * 128:(tb + 4) * 128].rearrange("p (i l) -> p i l", i=4), pv,
                           tmd.ap.unsqueeze(1).to_broadcast([128, 4, 128]), ALU.mult, [ps, tmd], [cbm[d]])
                if not KCSCAN:
                    AR.pop()
                    continue
                AR.push()
                arg = [Buf(AR.f32(512, "arg%d" % i)) for i in range(2)]
                Dm = [Buf(AR.bf16(512, "Dm%d" % i)) for i in range(2)]
                Mm = [Buf(AR.bf16(512, "Mm%d" % i)) for i in range(2)]
                xdt = Buf(AR.bf16(4 * 2 * CW, "xdt"))
                xde = [Buf(AR.bf16(CW, "xde%d" % i)) for i in range(2)]
                tfB = Buf(AR.f32(4 * CW, "tfB"))
                tbB = Buf(AR.f32(4 * CW, "tbB"))
                ytB = Buf(AR.f32(4 * CW, "ytB"))
                ygB = Buf(AR.bf16(4 * CW, "ygB"))
                AR_ygB_f32 = ygB.f32[:, 0:CW]
                xdeB = Buf(AR.bf16(4 * CW, "xdeB"))
                SbfB = [Buf(AR.bf16(CW, "SbfB%d" % i)) for i in range(4)]
                stmp = arg[0]
                stmp_ap = arg[0].ap[:, 0:CW]
                stmp2b = ygB
                stmp2_ap = AR_ygB_f32
                cnt = [0]
                hview = lambda ap_: ap_.rearrange("p (h q) -> p h q", h=HP)

                def scaled_x(dst_ap, dst_buf, ti, wsrc, d):
                    tt(hview(dst_ap), hview(v4(xs)[:, ti, :]),
                       bcast_last(v3(wsrc)[:, ti, d * 32 + h0: d * 32 + h0 + HP], 64), ALU.mult,
                       [xs, wsrc], [dst_buf])

                def state_step(d, ti):
                    xd = xde[cnt[0] % 2]
                    cnt[0] += 1
                    scaled_x(xd.ap, xd, ti, wde, d)
                    ps = bank()
                    mm(ps.ap[:, 0:CW], v4(Btm)[:, ti, :], xd.ap, True, True, [Btm, xd], ps, inc=True)
                    tt(hview(stmp_ap), hview(Sst[d].ap),
                       bcast_last(v3(cdec)[:, ti, d * 32 + h0: d * 32 + h0 + HP], 64), ALU.mult,
                       [Sst[d], cdec], [stmp])
                    tt(Sst[d].ap, stmp_ap, ps.ap[:, 0:CW], ALU.add, [stmp, ps], [Sst[d]])

                for si, (s0, sl) in enumerate(seqs):
                    for d in range(2):
                        if g == 1:
                            P.dma("sp", Sst[d].ap, st0[d][:, hq * CW:(hq + 1) * CW], writes=[Sst[d]])
                        else:
                            memset(Sst[d].ap, 0.0, [Sst[d]], eng="pool")
                    for ti in range(s0 + sl - 1, s0 - 1, -1):
                        cp(v4(prevb)[:, ti, :], Sst[1].ap, [Sst[1]], [prevb], eng="act")
                        state_step(1, ti)
                    if g == 0:
                        P.dma("sp", o_st[si][1][:, hq * CW:(hq + 1) * CW], Sst[1].ap, reads=[Sst[1]], is_output=True)
                    wide[0] = False
                    for tb in range(s0, s0 + sl, 4):
                        nb4 = min(4, s0 + sl - tb)
                        W = nb4 * 128
                        xbv = xdt.ap.rearrange("p (i d c) -> p i d c", i=4, d=2)
                        for d in range(2):
                            tt(xbv[:, 0:nb4, d, :].rearrange("p i (h q) -> p i h q", h=HP),
                               v4(xs)[:, tb:tb + nb4, :].rearrange("p i (h q) -> p i h q", h=HP),
                               bcast_last(v3(wdt)[:, tb:tb + nb4, d * 32 + h0: d * 32 + h0 + HP], 64), ALU.mult,
                               [xs, wdt], [xdt])
                        npair = nb4 // 2
                        yps = [lbank(pr) for pr in range(npair)]
                        xdv = xdeB.ap.rearrange("p (i c) -> p i c", i=4)
                        tt(xdv[:, 0:nb4, :].rearrange("p i (h q) -> p i h q", h=HP),
                           v4(xs)[:, tb:tb + nb4, :].rearrange("p i (h q) -> p i h q", h=HP),
                           bcast_last(v3(wde)[:, tb:tb + nb4, h0:h0 + HP], 64), ALU.mult, [xs, wde], [xdeB])
                        tfv = tfB.ap.rearrange("p (i c) -> p i c", i=4)
                        tbv = tbB.ap.rearrange("p (i c) -> p i c", i=4)

                        def rec_step(i):
                            ti = tb + i
                            cp(SbfB[i].ap, Sst[0].ap, [Sst[0]], [SbfB[i]], eng="act")
                            ps = bank()
                            mm(ps.ap[:, 0:CW], v4(Btm)[:, ti, :], xdv[:, i, :], True, True, [Btm, xdeB], ps, inc=True)
                            tt(hview(stmp2_ap), hview(Sst[0].ap),
                               bcast_last(v3(cdec)[:, ti, h0: h0 + HP], 64), ALU.mult, [Sst[0], cdec], [stmp2b])
                            tt(Sst[0].ap, stmp2_ap, ps.ap[:, 0:CW], ALU.add, [stmp2b, ps], [Sst[0]])

                        def yoff_pair(pr):
                            pf = bank()
                            pbk = bank()
                            for u in range(2):
                                ti = tb + pr * 2 + u
                                mm(pf.ap[:, u * CW:(u + 1) * CW], CT.ap[:, ti * 128:(ti + 1) * 128], SbfB[pr * 2 + u].ap,
                                   True, True, [CT, SbfB[pr * 2 + u]], pf, inc=(u == 1))
                            for u in range(2):
                                ti = tb + pr * 2 + u
                                mm(pbk.ap[:, u * CW:(u + 1) * CW], CT.ap[:, ti * 128:(ti + 1) * 128], v4(prevb)[:, ti, :],
                                   True, True, [CT, prevb], pbk, inc=(u == 1))
                            t0_ = tb + pr * 2
                            tt(tfv[:, pr * 2:pr * 2 + 2, :].rearrange("p i (h q) -> p i h q", h=HP),
                               pf.ap.rearrange("p (i h q) -> p i h q", i=2, h=HP),
                               bcast_last(v3(eA)[:, t0_:t0_ + 2, h0:h0 + HP], 64), ALU.mult, [pf, eA], [tfB])
                            tt(tbv[:, pr * 2:pr * 2 + 2, :].rearrange("p i (h q) -> p i h q", h=HP),
                               pbk.ap.rearrange("p (i h q) -> p i h q", i=2, h=HP),
                               bcast_last(v3(eA)[:, t0_:t0_ + 2, 32 + h0:32 + h0 + HP], 64), ALU.mult, [pbk, eA], [tbB])

                        for hh in range(HP):
                            hgl = h0 + hh
                            for d in range(2):
                                ps = bank()
                                mm(ps.ap[:, 0:W], bcast_last(selv[:, hgl], 128),
                                   Apc[d].ap[:, tb * 128: tb * 128 + W], True, True, [sel, Apc[d]], ps, inc=True)
                                a_, d_, m_ = arg[d], Dm[d], Mm[d]
                                tt(a_.ap[:, 0:W].rearrange("p (i l) -> p i l", i=nb4),
                                   ps.ap[:, 0:W].rearrange("p (i l) -> p i l", i=nb4),
                                   bcast_last(v3(A_tm)[:, tb:tb + nb4, d * 32 + hgl], 128), ALU.subtract,
                                   [ps, A_tm], [a_])
                                act(d_.ap[:, 0:W], a_.ap[:, 0:W], AF.Exp, [a_], [d_])
                                stt(m_.ap[:, 0:W], d_.ap[:, 0:W], 1.0, cbm[d].ap[:, tb * 128: tb * 128 + W], ALU.min,
                                    ALU.mult, [d_, cbm[d]], [m_])
                            keep_warm(4)
                            for i in range(nb4):
                                yp_ = yps[i // 2]
                                c0 = (i % 2) * CW + hh * 64
                                for d in range(2):
                                    mm(yp_.ap[:, c0:c0 + 64], Mm[d].ap[:, i * 128:(i + 1) * 128],
                                       xbv[:, i, d, hh * 64:(hh + 1) * 64], d == 0, d == 1, [Mm[d], xdt], yp_,
                                       inc=(d == 1))
                            if hh < nb4:
                                rec_step(hh)
                                if hh % 2 == 1:
                                    yoff_pair(hh // 2)
                        NB = nb4 * CW
                        tfa, tba, ya = tfB.ap[:, 0:NB], tbB.ap[:, 0:NB], ytB.ap[:, 0:NB]
                        tt(tfa, tfa, tba, ALU.add, [tfB, tbB], [tfB])
                        tt(ya.rearrange("p (i h q) -> p i h q", i=nb4, h=HP),
                           v4(xs)[:, tb:tb + nb4, :].rearrange("p i (h q) -> p i h q", h=HP),
                           bcast_last(dsk_bc[:, h0:h0 + HP].unsqueeze(1).to_broadcast([128, nb4, HP]), 64), ALU.mult,
                           [xs, bcs], [ytB])
                        tt(ya, ya, tfa, ALU.add, [ytB, tfB], [ytB])
                        for pr in range(npair):
                            tt(ya[:, pr * 2 * CW:(pr + 1) * 2 * CW], ya[:, pr * 2 * CW:(pr + 1) * 2 * CW],
                               yps[pr].ap[:, 0:2 * CW], ALU.add, [ytB, yps[pr]], [ytB])
                        tt(ya, ya, v4(sz)[:, tb:tb + nb4, :].rearrange("p i c -> p (i c)"), ALU.mult, [ytB, sz], [ytB])
                        for i in range(nb4):
                            ti = tb + i
                            act(tbB.ap[:, i * CW:(i + 1) * CW], ya[:, i * CW:(i + 1) * CW], AF.Square, [ytB], [tbB, ssq],
                                accum_out=ssq.ap[:, ti * 8 + hq: ti * 8 + hq + 1])
                        cp(ygB.ap[:, 0:NB], ya, [ytB], [ygB], eng="act")
                        ps = bank()
                        psb = ps.ap.bitcast(BF16)
                        for i in range(nb4):
                            for j in range(2):
                                tr(psb[:, (i * 2 + j) * 128:(i * 2 + j + 1) * 128],
                                   ygB.ap[:, i * CW + j * 128: i * CW + (j + 1) * 128], ident_b.ap,
                                   [ygB, ident_b], ps, inc=(i == nb4 - 1 and j == 1))
                        for j in range(2):
                            tt(ygv[:, hq * 2 + j, tb * 128:(tb + nb4) * 128].rearrange("p (i t) -> p i t", i=nb4),
                               psb[:, 0:nb4 * 256].rearrange("p (i j t) -> p i j t", i=nb4, j=2)[:, :, j, :],
                               gnT_s[:, hq * 2 + j: hq * 2 + j + 1].unsqueeze(2).to_broadcast([128, nb4, 128]),
                               ALU.mult, [ps, small], [ygT])
                    wide[0] = True
                    if g == 0:
                        P.dma("sp", o_st[si][0][:, hq * CW:(hq + 1) * CW], Sst[0].ap, reads=[Sst[0]], is_output=True)
                AR.pop()
                AR.pop()

            if not KCOUT:
                AR.pop()
                continue
            AR.push()
            rs = Buf(AR.f32(nt, "rs"))
            otmp = [Buf(AR.f32(512, "otmpc%d" % i)) for i in range(2)]
            gt = Buf(AR.f32(D, "gt_c"))
            gate_bc(l, g, gt)
            P.op("dve", lambda e, ssq=ssq, rs=rs, nt=nt: e.tensor_reduce(
                out=rs.ap, in_=ssq.ap.rearrange("p (t c) -> p t c", t=nt), axis=mybir.AxisListType.X, op=ALU.add),
                reads=[ssq], writes=[rs])
            act(rs.ap, rs.ap, AF.Sqrt, [rs, small], [rs], scale=1.0 / 2048, bias=eps_s)
            recip(rs.ap, rs.ap, [rs], [rs])
            for nb in range(2):
                w1, v1 = load_wblock(c_w_out[0:1024, nb * 512:(nb + 1) * 512], 512)
                w2, v2 = load_wblock(c_w_out[1024:2048, nb * 512:(nb + 1) * 512], 512)
                for ti in range(nt):
                    ps = bank()
                    for c in range(16):
                        wv_ = v1 if c < 8 else v2
                        wb_ = w1 if c < 8 else w2
                        mm(ps.ap, ygv[:, c, ti * 128:(ti + 1) * 128], wv_[:, c % 8, :], c == 0, c == 15, [ygT, wb_], ps,
                           inc=(c == 15))
                    ot = otmp[ti % 2]
                    xr = XR[tiles[ti]]
                    stt(ot.ap, ps.ap, rs.ap[:, ti:ti + 1], gt.ap[:, nb * 512:(nb + 1) * 512], ALU.mult, ALU.mult,
                        [ps, rs, gt], [ot])
                    tt(xr.ap[:, nb * 512:(nb + 1) * 512], xr.ap[:, nb * 512:(nb + 1) * 512], ot.ap, ALU.add, [xr, ot],
                       [xr], eng="dve")
            AR.pop()
            AR.pop()

    import os
    STAGE = int(os.environ.get("KSTAGE", "99"))
    if STAGE >= 1:
        modulation_fm(0)
    if STAGE >= 2:
        layer_a()
    if STAGE >= 3:
        modulation_fm(1)
    if STAGE >= 4:
        layer_c()
    dump("xr0_final", XR[0].ap, [XR[0]])
    dump("xr3_final", XR[3].ap, [XR[3]])
    AR.push()
    junk = Buf(AR.f32(D, "fjunk"))
    st = Buf(AR.f32(16, "fst"))
    gfb = Buf(AR.f32(D, "gfin_bc"))
    gfin_bc = gfb.ap
    P.dma("sp", gfin_bc, pbc(gfin), writes=[gfb])
    yo = [Buf(AR.f32(D, "yo%d" % i)) for i in range(2)]
    for t in range(12):
        s_ = st.ap[:, t:t + 1]
        act(junk.ap, XR[t].ap, AF.Square, [XR[t]], [junk, st], accum_out=s_)
        act(s_, s_, AF.Sqrt, [st, small], [st], scale=1.0 / D, bias=eps_s)
        recip(s_, s_, [st], [st])
        y_ = yo[t % 2]
        stt(y_.ap, XR[t].ap, s_, gfin_bc, ALU.mult, ALU.mult, [XR[t], st, gfb], [y_])
        P.dma("sp", y_all[t * 128:(t + 1) * 128, :], y_.ap, reads=[y_], is_output=True)
    AR.pop()
    P.finish()
    with nc.Block() as block:
        P.replay(block)
    es.close()
    return nc


def _consts():
    ident = np.eye(128, dtype=np.float32)
    s = np.arange(128)
    tmf = (s[:, None] <= s[None, :]).astype(np.float32)
    tmb = (s[:, None] >= s[None, :]).astype(np.float32)
    sel = np.zeros((64, 32), np.float32)
    for k in range(64):
        sel[k, k % 32] = 1.0
    rows_ = 1024 // 64
    row = np.repeat(np.arange(rows_, dtype=np.float32), 64)
    col = np.tile(np.arange(64, dtype=np.float32), rows_)
    n_freq = 8
    inv = np.power(np.float32(10000.0), -np.arange(n_freq, dtype=np.float32) / n_freq).astype(np.float32)
    ang = np.concatenate([row[:, None] * inv, col[:, None] * inv], axis=-1).astype(np.float32)
    ropecs = np.concatenate([np.cos(ang), np.sin(ang)], axis=-1).astype(np.float32)
    return ident, tmf, tmb, sel, ropecs


_NC_CACHE = {}


def kernel(x_prompt, x_sample, cache_ckv, cache_krope, state_ssd, c, c_ctx,
           w_mod, b_mod, g_pre, g_final,
           a_w_in, a_g_q, a_g_kv, a_w_uq, a_w_uk, a_w_uv, a_conv_w, a_conv_b, a_ln_g, a_ln_b, a_w_out,
           c_w_in, c_conv_w, c_conv_b, c_dt_bias, c_a_log, c_d, c_g_norm, c_w_out):
    f = lambda a: np.ascontiguousarray(np.asarray(a, dtype=np.float32))
    x_prompt, x_sample = f(x_prompt), f(x_sample)
    ident, tmf, tmb, sel, ropecs = _consts()
    fm = lambda v, n: f(np.asarray(v, np.float32).reshape(n, 128).T)
    b_mod = np.asarray(b_mod, np.float32)
    g_pre = np.asarray(g_pre, np.float32)
    shared = {
        "w_mod": f(w_mod),
        "bmodT": f(np.concatenate([fm(b_mod[0], 24), fm(b_mod[1], 24)], axis=1)),
        "bmod_row": f(b_mod.reshape(1, -1)),
        "gpreT": f(np.concatenate([fm(g_pre[0], 8), fm(g_pre[1], 8)], axis=1)),
        "gfin": f(np.asarray(g_final).reshape(1, -1)),
        "ropecs": ropecs,
        "a_w_in": f(a_w_in[0]), "a_gq": f(np.asarray(a_g_q[0]).reshape(1, -1)),
        "a_gkv": f(np.asarray(a_g_kv[0]).reshape(1, -1)),
        "a_w_uq": f(a_w_uq[0]), "a_w_uk": f(a_w_uk[0]), "a_w_uv": f(a_w_uv[0]),
        "a_convwT": f(np.asarray(a_conv_w[0], np.float32).reshape(31, 4, 128).transpose(2, 1, 0).reshape(128, 124)),
        "a_convb": f(np.asarray(a_conv_b[0]).reshape(1, -1)),
        "a_lngT": fm(a_ln_g[0], 4), "a_lnbT": fm(a_ln_b[0], 4),
        "a_w_out": f(a_w_out[0]),
        "c_w_in": f(c_w_in[0]),
        "c_convwT": f(np.asarray(c_conv_w[0], np.float32).reshape(5, 24, 128).transpose(2, 1, 0).reshape(128, 120)),
        "c_convb": f(np.asarray(c_conv_b[0]).reshape(1, -1)),
        "c_convbT": fm(c_conv_b[0], 24),
        "c_dtb": f(np.asarray(c_dt_bias[0]).reshape(1, 64)),
        "c_alog": f(np.asarray(c_a_log[0]).reshape(1, 64)),
        "c_dskip": f(np.asarray(c_d[0]).reshape(1, 32)),
        "c_gnT": fm(c_g_norm[0], 16),
        "c_w_out": f(c_w_out[0]),
        "k_ident": ident, "k_tmf": tmf, "k_tmb": tmb, "k_sel": sel,
    }
    c = np.asarray(c, np.float32)
    c_ctx = np.asarray(c_ctx, np.float32)
    in_maps = []
    for b in range(NCORES):
        m = dict(shared)
        m["x_all"] = f(np.concatenate([x_prompt[2 * b].reshape(256, D), x_prompt[2 * b + 1].reshape(256, D),
                                       x_sample[b]], axis=0))
        m["cckv"] = f(cache_ckv[b, 0])
        m["ckr"] = f(cache_krope[b, 0])
        m["st0"] = f(np.asarray(state_ssd[b, 0], np.float32).transpose(0, 3, 1, 2).reshape(2, 128, 2048))
        cond = np.stack([c_ctx, c[b]], axis=0)
        m["condT"] = f(cond.reshape(2, 8, 128).transpose(2, 1, 0).reshape(128, 16))
        in_maps.append(m)
    if "nc" not in _NC_CACHE:
        _NC_CACHE["nc"] = build_program()
    nc = _NC_CACHE["nc"]
    res = run_bass_kernel_spmd(nc, in_maps, core_ids=list(range(NCORES)))
    R = res.results
    import os as _os
    if _os.environ.get("KDBG"):
        _NC_CACHE["dbg"] = {k: np.asarray(v).astype(np.float32) for k, v in R[0].items() if k.startswith("dbg_")}
    y_prompt = np.zeros((16, 256, D), np.float32)
    y_sample = np.zeros((8, 1024, D), np.float32)
    n_ckv = np.zeros((16, 1, 256, 128), np.float32)
    n_kr = np.zeros((16, 1, 256, 32), np.float32)
    n_st = np.zeros((16, 1, 2, 32, 64, 128), np.float32)
    for b in range(NCORES):
        r = R[b]
        ya = np.asarray(r["y_all"])
        y_prompt[2 * b] = ya[0:256]
        y_prompt[2 * b + 1] = ya[256:512]
        y_sample[b] = ya[512:1536]
        ck = np.asarray(r["o_ckv"])
        kr = np.asarray(r["o_kr"])
        n_ckv[2 * b, 0] = ck[0:256]
        n_ckv[2 * b + 1, 0] = ck[256:512]
        n_kr[2 * b, 0] = kr[0:256]
        n_kr[2 * b + 1, 0] = kr[256:512]
        st = np.asarray(r["o_st"])
        for s_ in range(2):
            n_st[2 * b + s_, 0] = st[s_].reshape(2, 128, 32, 64).transpose(0, 2, 3, 1)
    return (y_prompt, y_sample, n_ckv, n_kr, n_st)
```

```python
import numpy as np
from contextlib import ExitStack
import concourse.bass as bass
import concourse.mybir as mybir
from concourse.bass_utils import run_bass_kernel_spmd
from concourse.alu_op_type import AluOpType as ALU

F32 = mybir.dt.float32
BF16 = mybir.dt.bfloat16
AF = mybir.ActivationFunctionType

D = 1024
NCORES = 8
EPS = 1e-6
ATTN_SCALE = 96 ** -0.5
ARENA_COLS = 53100


class Buf:
    __slots__ = ("ap", "w", "r", "name", "f32", "fence", "sticky")

    def __init__(self, ap, name=""):
        self.ap = ap
        self.f32 = Arena.LAST if ap.dtype == BF16 else None
        self.w = None
        self.r = {}
        self.name = name
        self.fence = False
        self.sticky = False

    def __getitem__(self, k):
        return self.ap[k]


class Prog:
    ENG = ("pe", "dve", "act", "pool", "sp")

    def __init__(self, nc, es):
        self.nc = nc
        self.lists = {e: [] for e in self.ENG}
        self.sem = {e: es.enter_context(nc.semaphore("sem_" + e)) for e in self.ENG}
        self.cnt = {e: 0 for e in self.ENG}
        self.waited = {e: {} for e in self.ENG}
        self.dsem = {}
        for q in ("pool", "sp"):
            self.dsem[q] = [[es.enter_context(nc.semaphore("dq_%s_%d" % (q, i))), 0] for i in range(24)]
        self.dptr = {"pool": 0, "sp": 0}
        self.out_toks = []
        self.pending_dma = []
        self.persist_ids = set()

    def barrier(self):
        toks = [(self.sem[e], self.cnt[e]) for e in self.ENG if self.cnt[e] > 0] + self.pending_dma
        self.pending_dma = []
        for e in self.ENG:
            for (sem, val) in toks:
                if sem is self.sem[e]:
                    continue
                k = id(sem)
                if self.waited[e].get(k, 0) >= val:
                    continue
                self.waited[e][k] = val
                self.lists[e].append(("w", sem, val))

    def _wait(self, eng, tok, raw=False):
        if tok is None:
            return
        sem, val = tok
        if sem is self.sem[eng]:
            if eng == "pe" or not raw:
                return
        k = id(sem)
        if self.waited[eng].get(k, 0) >= val:
            return
        self.waited[eng][k] = val
        self.lists[eng].append(("w", sem, val))

    def _deps(self, eng, reads, writes):
        for b in reads:
            self._wait(eng, b.w, raw=True)
        for b in writes:
            self._wait(eng, b.w, raw=b.fence)
            for t in b.r.values():
                self._wait(eng, t)

    def _mark(self, tok, reads, writes):
        k = id(tok[0])
        for b in reads:
            b.r[k] = tok
        for b in writes:
            b.w = tok
            b.r = {}
            b.fence = b.sticky

    def op(self, eng, fn, reads=(), writes=(), inc=True):
        self._deps(eng, reads, writes)
        self.lists[eng].append(("o", fn, inc))
        tok = (self.sem[eng], self.cnt[eng] + 1)
        if inc:
            self.cnt[eng] += 1
        self._mark(tok, reads, writes)

    def dma(self, q, out_ap, in_ap, reads=(), writes=(), is_output=False, **kw):
        self._deps(q, reads, writes)
        pool = self.dsem[q]
        i = self.dptr[q]
        self.dptr[q] = (i + 1) % len(pool)
        sem, uses = pool[i]
        if uses > 0:
            self._wait(q, (sem, 16 * uses))
        pool[i][1] = uses + 1
        tok = (sem, 16 * (uses + 1))
        self.lists[q].append(("d", out_ap, in_ap, sem, kw))
        self._mark(tok, reads, writes)
        if not writes or not all(id(b) in self.persist_ids for b in writes):
            self.pending_dma.append(tok)
        if is_output:
            self.out_toks.append(tok)

    def finish(self):
        for tok in self.out_toks:
            self._wait("sp", tok)

    def replay(self, block):
        nc = self.nc
        names = {"pe": "tensor", "dve": "vector", "act": "scalar", "pool": "gpsimd", "sp": "sync"}
        for e in self.ENG:
            items = self.lists[e]
            sem_e = self.sem[e]

            def body(eng, items=items, sem_e=sem_e):
                for it in items:
                    if it[0] == "w":
                        eng.wait_ge(it[1], it[2])
                    elif it[0] == "o":
                        ins = it[1](eng)
                        if it[2]:
                            ins.then_inc(sem_e, 1)
                    else:
                        eng.dma_start(out=it[1], in_=it[2], **it[4]).then_inc(it[3], 16)

            getattr(block, names[e])(body)


class Arena:
    LAST = None

    def __init__(self, ap, ncols):
        self.ap = ap
        self.n = ncols
        self.top = 0
        self.stack = []
        self.on_pop = None
        self.need_barrier = False

    def push(self):
        self.stack.append(self.top)

    def pop(self):
        self.top = self.stack.pop()
        self.need_barrier = True

    def _sync(self):
        if self.need_barrier and self.on_pop is not None:
            self.on_pop()
        self.need_barrier = False

    def f32(self, cols, name="", parts=128):
        self._sync()
        cols_al = (cols + 15) // 16 * 16
        off = self.top
        self.top += cols_al
        assert self.top <= self.n, "arena overflow at %s: %d" % (name, self.top)
        return self.ap[0:parts, off:off + cols]

    def bf16(self, cols, name="", parts=128):
        self._sync()
        c32 = (cols + 1) // 2
        c32a = (c32 + 15) // 16 * 16
        off = self.top
        self.top += c32a
        assert self.top <= self.n, "arena overflow at %s: %d" % (name, self.top)
        self.last_f32 = self.ap[0:parts, off:off + c32]
        Arena.LAST = self.last_f32
        return self.ap[0:parts, off:off + c32].bitcast(BF16)[:, 0:cols]


def bcast_last(ap, n):
    return ap.unsqueeze(len(ap.shape)).to_broadcast(list(ap.shape) + [n])


def build_program(debug=False):
    nc = bass.Bass("TRN2", target_bir_lowering=False)
    es = ExitStack()

    def din(name, shape, dt=F32):
        return nc.dram_tensor(name, list(shape), dt, kind="ExternalInput").ap()

    def dout(name, shape, dt=F32):
        return nc.dram_tensor(name, list(shape), dt, kind="ExternalOutput").ap()

    x_all = din("x_all", [1536, D])
    cckv = din("cckv", [512, 128])
    ckr = din("ckr", [512, 32])
    st0 = din("st0", [2, 128, 2048])
    condT = din("condT", [128, 16])
    w_mod = din("w_mod", [2, D, 3072])
    bmodT = din("bmodT", [128, 2 * 24])
    bmod_row = din("bmod_row", [1, 2 * 3072])
    gpreT = din("gpreT", [128, 16])
    gfin = din("gfin", [1, D])
    ropecs = din("ropecs", [1024, 32])
    a_w_in = din("a_w_in", [D, 2464])
    a_gq = din("a_gq", [1, 256])
    a_gkv = din("a_gkv", [1, 128])
    a_w_uq = din("a_w_uq", [256, 768])
    a_w_uk = din("a_w_uk", [128, 512])
    a_w_uv = din("a_w_uv", [128, 512])
    a_convwT = din("a_convwT", [128, 4 * 31])
    a_convb = din("a_convb", [1, 512])
    a_lngT = din("a_lngT", [128, 4])
    a_lnbT = din("a_lnbT", [128, 4])
    a_w_out = din("a_w_out", [D, D])
    c_w_in = din("c_w_in", [D, 5184])
    c_convwT = din("c_convwT", [128, 24 * 5])
    c_convb = din("c_convb", [1, 3072])
    c_dtb = din("c_dtb", [1, 64])
    c_alog = din("c_alog", [1, 64])
    c_dskip = din("c_dskip", [1, 32])
    c_gnT = din("c_gnT", [128, 16])
    c_w_out = din("c_w_out", [2048, D])
    k_ident = din("k_ident", [128, 128])
    k_tmf = din("k_tmf", [128, 128])
    k_tmb = din("k_tmb", [128, 128])
    k_sel = din("k_sel", [64, 32])
    c_convbT = din("c_convbT", [128, 24])

    y_all = dout("y_all", [1536, D])
    o_ckv = dout("o_ckv", [512, 128])
    o_kr = dout("o_kr", [512, 32])
    o_st = dout("o_st", [2, 2, 128, 2048])
    import os as _os
    DBG = _os.environ.get("KDBG")

    def dump(name, ap, bufs):
        if DBG:
            d = nc.dram_tensor("dbg_" + name, list(ap.shape), ap.dtype, kind="ExternalOutput").ap()
            P.dma("sp", d, ap, reads=bufs, is_output=True)

    arena_t = es.enter_context(nc.sbuf_tensor("arena", [128, ARENA_COLS], F32))
    AR = Arena(arena_t, ARENA_COLS)
    P = Prog(nc, es)
    AR.on_pop = P.barrier
    banks = []
    for i in range(8):
        pt = es.enter_context(nc.psum_tensor("psb%d" % i, [128, 512], F32))
        banks.append(Buf(pt[:, :], "bank%d" % i))
    bank_i = [0]

    wide = [True]
    bank_w = [0]

    def bank():
        if wide[0]:
            b = banks[bank_w[0]]
            bank_w[0] = (bank_w[0] + 1) % 8
            return b
        b = banks[4 + bank_i[0]]
        bank_i[0] = (bank_i[0] + 1) % 4
        return b

    def lbank(i):
        return banks[i % 4]

    def keep_warm(n_):
        dst = banks[3].ap
        for _ in range(n_):
            P.op("pe", lambda e: e.matmul(dst, ident_b.ap, ones_b.ap, start=True, stop=True),
                 reads=[ident_b, ones_b], writes=[banks[3]], inc=False)

    def mm(ps_ap, lhsT, rhs, start, stop, reads, ps_buf, inc):
        P.op("pe", lambda e: e.matmul(ps_ap, lhsT, rhs, start=start, stop=stop),
             reads=reads, writes=[ps_buf], inc=inc)

    def tr(ps_ap, in_ap, ident_ap, reads, ps_buf, inc):
        P.op("pe", lambda e: e.transpose(ps_ap, in_ap, ident_ap), reads=reads, writes=[ps_buf], inc=inc)

    def act(out, in_, func, reads, writes, eng="act", **kw):
        P.op(eng, lambda e: e.activation(out=out, in_=in_, func=func, **kw), reads=reads, writes=writes)

    def tt(out, in0, in1, op, reads, writes, eng="dve"):
        P.op(eng, lambda e: e.tensor_tensor(out=out, in0=in0, in1=in1, op=op), reads=reads, writes=writes)

    def ts(out, in0, s1, s2, op0, op1, reads, writes, eng="dve"):
        if op1 is None:
            P.op(eng, lambda e: e.tensor_scalar(out=out, in0=in0, scalar1=s1, scalar2=None, op0=op0),
                 reads=reads, writes=writes)
        else:
            P.op(eng, lambda e: e.tensor_scalar(out=out, in0=in0, scalar1=s1, scalar2=s2, op0=op0, op1=op1),
                 reads=reads, writes=writes)

    def stt(out, in0, scalar, in1, op0, op1, reads, writes):
        P.op("dve", lambda e: e.scalar_tensor_tensor(out=out, in0=in0, scalar=scalar, in1=in1, op0=op0, op1=op1),
             reads=reads, writes=writes)

    def cp(out, in_, reads, writes, eng="dve"):
        if eng == "act":
            P.op("act", lambda e: e.copy(out=out, in_=in_), reads=reads, writes=writes)
        else:
            P.op(eng, lambda e: e.tensor_copy(out=out, in_=in_), reads=reads, writes=writes)

    def memset(ap, val, writes, eng="pool"):
        P.op(eng, lambda e: e.memset(ap, val), writes=writes)
        for b_ in writes:
            b_.fence = True

    BF16_ONE_PAIR = float(np.frombuffer(np.array([0x3F803F80], dtype=np.uint32).tobytes(), dtype=np.float32)[0])

    def fill_bf16(buf, val):
        v = 0.0 if val == 0.0 else BF16_ONE_PAIR
        P.op("dve", lambda e: e.memset(buf.f32, v), writes=[buf])
        buf.fence = True

    def recip(out, in_, reads, writes):
        P.op("dve", lambda e: e.reciprocal(out=out, in_=in_), reads=reads, writes=writes)

    guard = AR.f32(512, "guard")
    XR = [Buf(AR.f32(D, "xr%d" % t), "xr%d" % t) for t in range(12)]
    ident_f = Buf(AR.f32(128, "identf"))
    ident_b = Buf(AR.bf16(128, "identb"))
    tmf = Buf(AR.f32(128, "tmf"))
    tmb = Buf(AR.f32(128, "tmb"))
    ones_f = Buf(AR.f32(128, "onesf"))
    ones_b = Buf(AR.bf16(512, "onesb"))
    tmf_b = Buf(AR.bf16(128, "tmfb"))
    tmb_b = Buf(AR.bf16(128, "tmbb"))
    sel = Buf(AR.bf16(32, "sel", parts=64))
    NWB = 2
    WB = [Buf(AR.bf16(8 * 512, "wb%d" % i)) for i in range(NWB)]
    wb_i = [0]
    P.persist_ids = set(id(b_) for b_ in WB)
    small = Buf(AR.f32(1024, "small"))
    sm_off = [0]

    def sm(cols):
        o = sm_off[0]
        sm_off[0] += cols
        assert sm_off[0] <= 1024
        return small.ap[:, o:o + cols]

    condT_s = sm(16)
    bmodT_s = sm(48)
    gpreT_s = sm(16)
    lngT_s = sm(4)
    lnbT_s = sm(4)
    gnT_s = sm(16)
    aconvw_s = sm(124)
    cconvw_s = sm(120)
    eps_s = sm(1)
    modT = sm(2 * 48)
    gsT = sm(2 * 16)
    cconvb_s = sm(24)
    bcs = Buf(AR.f32(1024 + 256 + 128 + 64 + 64 + 32, "bcs"))
    gq_bc = bcs.ap[:, 1024:1280]
    gkv_bc = bcs.ap[:, 1280:1408]
    dtb_bc = bcs.ap[:, 1408:1472]
    aneg_bc = bcs.ap[:, 1472:1536]
    dsk_bc = bcs.ap[:, 1536:1568]
    scT = Buf(AR.bf16(16, "scT"))
    rope_s = Buf(AR.f32(8 * 32, "rope"))

    P.dma("sp", ident_f.ap, k_ident, writes=[ident_f])
    P.dma("pool", ident_b.ap, k_ident, writes=[ident_b])
    P.dma("sp", tmf.ap, k_tmf, writes=[tmf])
    P.dma("sp", tmb.ap, k_tmb, writes=[tmb])
    cp(tmf_b.ap, tmf.ap, [tmf], [tmf_b])
    cp(tmb_b.ap, tmb.ap, [tmb], [tmb_b])
    P.dma("pool", sel.ap, k_sel, writes=[sel])
    memset(ones_f.ap, 1.0, [ones_f])
    fill_bf16(ones_b, 1.0)
    for (dst, src) in ((condT_s, condT), (bmodT_s, bmodT), (gpreT_s, gpreT), (lngT_s, a_lngT), (lnbT_s, a_lnbT),
                       (gnT_s, c_gnT), (aconvw_s, a_convwT), (cconvw_s, c_convwT), (cconvb_s, c_convbT)):
        P.dma("sp", dst, src, writes=[small])
    memset(eps_s, EPS, [small], eng="dve")
    def pbc(src):
        return src.rearrange("o n -> (o n)").partition_broadcast(128)

    for (dst, src) in ((gq_bc, a_gq), (gkv_bc, a_gkv), (dtb_bc, c_dtb), (aneg_bc, c_alog), (dsk_bc, c_dskip)):
        P.dma("sp", dst, pbc(src), writes=[bcs])
    act(aneg_bc, aneg_bc, AF.Exp, [bcs], [bcs])
    ts(aneg_bc, aneg_bc, -1.0, None, ALU.mult, None, [bcs], [bcs])
    P.dma("sp", rope_s.ap.rearrange("p (t c) -> p t c", t=8), ropecs.rearrange("(t p) c -> p t c", p=128),
          writes=[rope_s])
    for t in range(12):
        P.dma("sp", XR[t].ap, x_all[t * 128:(t + 1) * 128, :], writes=[XR[t]])

    act(scT.ap, condT_s, AF.Silu, [small], [scT])

    def load_wblock(src_ap, ncols, kchunks=8, parts=128):
        wb = WB[wb_i[0]]
        wb_i[0] = (wb_i[0] + 1) % NWB
        view = wb.ap[0:parts, 0:kchunks * ncols].rearrange("p (k n) -> p k n", k=kchunks)
        P.dma("pool", view, src_ap.rearrange("(k p) n -> p k n", p=parts), writes=[wb])
        return wb, view

    def modulation_fm(l):
        wide[0] = False
        psm = lbank(0)
        psm_v = psm.ap[:, 0:32].rearrange("p (j g) -> p j g", g=2)
        for blk in range(4):
            wb, wv = load_wblock(w_mod[l][:, blk * 512:(blk + 1) * 512], 512)
            for jj in range(4):
                j = blk * 4 + jj
                for k in range(8):
                    mm(psm_v[:, j, :], wv[:, k, jj * 128:(jj + 1) * 128],
                       scT.ap.rearrange("p (k g) -> p k g", g=2)[:, k, :],
                       k == 0, k == 7, [wb, scT], psm, inc=(k == 7 and jj == 3))
        mo = modT[:, l * 32:(l + 1) * 32].rearrange("p (j g) -> p j g", g=2)
        tt(mo, psm_v, bcast_last(bmodT_s[:, l * 24: l * 24 + 16], 2), ALU.add, [psm, small], [small])
        gs = gsT[:, l * 16:(l + 1) * 16].rearrange("p (k g) -> p k g", g=2)
        ts(gs, mo[:, 8:16, :], 1.0, None, ALU.add, None, [small], [small])
        tt(gs, gs, bcast_last(gpreT_s[:, l * 8:(l + 1) * 8], 2), ALU.mult, [small], [small])
        wide[0] = True

    def gate_bc(l, g, gt):
        AR.push()
        screp = Buf(AR.bf16(8 * 128, "screp"))
        brow = Buf(AR.f32(1024, "brow"))
        srv = screp.ap.rearrange("p (k m) -> p k m", m=128)
        cp(srv, bcast_last(scT.ap.rearrange("p (k g) -> p k g", g=2)[:, :, g], 128), [scT], [screp])
        P.dma("sp", brow.ap, pbc(bmod_row[:, l * 3072 + 2048: l * 3072 + 3072]), writes=[brow])
        for nb in range(2):
            wb, wv = load_wblock(w_mod[l][:, 2048 + nb * 512: 2048 + (nb + 1) * 512], 512)
            pg = bank()
            for k in range(8):
                mm(pg.ap, srv[:, k, :], wv[:, k, :], k == 0, k == 7, [wb, screp], pg, inc=(k == 7))
            tt(gt.ap[:, nb * 512:(nb + 1) * 512], pg.ap, brow.ap[:, nb * 512:(nb + 1) * 512], ALU.add, [pg, brow], [gt])
        AR.pop()

    def build_hT(l, tiles, g, hT):
        nt = len(tiles)
        hv = hT.ap.rearrange("p (k t) -> p k t", k=8)
        AR.push()
        xn = [Buf(AR.bf16(D, "xn%d" % i)) for i in range(4)]
        junk = Buf(AR.f32(D, "junk"))
        st = Buf(AR.f32(8, "hst"))
        for tb in range(0, nt, 4):
            for i in range(4):
                t = tiles[tb + i]
                ssq = st.ap[:, i:i + 1]
                act(junk.ap, XR[t].ap, AF.Square, [XR[t]], [junk, st], accum_out=ssq)
                act(ssq, ssq, AF.Sqrt, [st, small], [st], scale=1.0 / D, bias=eps_s)
                recip(ssq, ssq, [st], [st])
                act(xn[i].ap, XR[t].ap, AF.Identity, [XR[t], st], [xn[i]], scale=ssq)
            for k in range(8):
                ps = bank()
                psb = ps.ap.bitcast(BF16)
                for i in range(4):
                    tr(psb[:, i * 128:(i + 1) * 128], xn[i].ap[:, k * 128:(k + 1) * 128], ident_b.ap,
                       [xn[i], ident_b], ps, inc=(i == 3))
                act(hv[:, k, tb * 128:(tb + 4) * 128], psb[:, 0:512], AF.Identity, [ps, small], [hT],
                    scale=gsT[:, l * 16 + k * 2 + g: l * 16 + k * 2 + g + 1],
                    bias=modT[:, l * 32 + k * 2 + g: l * 32 + k * 2 + g + 1])
        AR.pop()

    def layer_a():
        import os
        KSUB = int(os.environ.get("KSUB", "99"))
        KGRP = int(os.environ.get("KGRP", "2"))
        l = 0
        groups = [dict(g=0, tiles=[0, 1, 2, 3], seqs=[(0, 2), (2, 2)], ctx=0),
                  dict(g=1, tiles=list(range(4, 12)), seqs=[(0, 8)], ctx=4)]
        for grp in ([groups[1]] if os.environ.get("KGSEL") == "1" else groups[:KGRP]):
            g = grp["g"]
            tiles = grp["tiles"]
            nt = len(tiles)
            Tg = nt * 128
            nctx = grp["ctx"]
            nkt = nt + nctx
            Lk = nkt * 128
            AR.push()
            wga, vga = load_wblock(a_w_in[:, 416:928], 512)
            wgb, vgb = load_wblock(a_w_in[:, 928:1440], 512)
            hT = Buf(AR.bf16(8 * Tg, "hT"))
            build_hT(l, tiles, g, hT)
            hv = hT.ap.rearrange("p (k t) -> p k t", k=8)
            mixTa = Buf(AR.bf16(4 * Tg, "mixTa"))
            mav = mixTa.ap.rearrange("p (h t) -> p h t", h=4)
            mixTc = Buf(AR.bf16(4 * Tg, "mixTc"))
            mcv = mixTc.ap.rearrange("p (j t) -> p j t", j=4)
            if KSUB < 1:
                AR.pop()
                continue
            AR.push()
            seqs = grp["seqs"]
            Ls = seqs[0][1] * 128
            LP = Ls + 30
            cvin = Buf(AR.bf16(4 * len(seqs) * LP, "cvin"))
            cvin_f32 = AR.last_f32
            cvv = cvin.ap.rearrange("p (j s t) -> p j s t", j=4, s=len(seqs))
            sgc = Buf(AR.bf16(4 * Tg, "sgc"))
            sgv = sgc.ap.rearrange("p (j t) -> p j t", j=4)
            sig = Buf(AR.f32(512, "sig"))
            xnb = Buf(AR.bf16(4 * 512, "xnb"))
            lst = Buf(AR.f32(16, "lst"))
            abrow = Buf(AR.bf16(512, "abrow"))
            fill_bf16(abrow, 0.0)
            P.dma("pool", abrow.ap[0:1, :], a_convb, writes=[abrow])
            memset(cvin_f32, 0.0, [cvin], eng="dve")
            for j in range(4):
                for nb in range(Tg // 512):
                    pa = bank()
                    pb = bank()
                    for k in range(8):
                        mm(pa.ap, vga[:, k, j * 128:(j + 1) * 128], hv[:, k, nb * 512:(nb + 1) * 512], k == 0, k == 7,
                           [wga, hT], pa, inc=(k == 7))
                    for k in range(8):
                        mm(pb.ap, vgb[:, k, j * 128:(j + 1) * 128], hv[:, k, nb * 512:(nb + 1) * 512], k == 0, k == 7,
                           [wgb, hT], pb, inc=(k == 7))
                    act(sig.ap, pb.ap, AF.Sigmoid, [pb], [sig])
                    nseq_blk = 512 // Ls if Ls < 512 else 1
                    for si in range(nseq_blk):
                        if Ls < 512:
                            s_idx = nb * nseq_blk + si
                            c0, c1, d0 = si * Ls, (si + 1) * Ls, 15
                        else:
                            s_idx = 0
                            c0, c1, d0 = 0, 512, 15 + nb * 512
                        tt(cvv[:, j, s_idx, d0:d0 + (c1 - c0)], pa.ap[:, c0:c1], sig.ap[:, c0:c1], ALU.mult,
                           [pa, sig], [cvin])
            if g == 0:
                dump("abrow", abrow.ap, [abrow])
                dump("onesb", ones_b.ap, [ones_b])
                dump("cvin", cvin.ap, [cvin])
            wgc, vgc = load_wblock(a_w_in[:, 1952:2464], 512)
            for j in range(4):
                for nb in range(Tg // 512):
                    pa = bank()
                    for k in range(8):
                        mm(pa.ap, vgc[:, k, j * 128:(j + 1) * 128], hv[:, k, nb * 512:(nb + 1) * 512], k == 0, k == 7,
                           [wgc, hT], pa, inc=(k == 7))
                    act(sgv[:, j, nb * 512:(nb + 1) * 512], pa.ap, AF.Silu, [pa], [sgc])
            dgl = []
            for j in range(4):
                dg = Buf(AR.bf16(31 * 128, "dgj%d" % j))
                dgl.append(dg)
                tt(dg.ap.rearrange("p (k m) -> p k m", k=31),
                   ident_b.ap.unsqueeze(1).to_broadcast([128, 31, 128]),
                   bcast_last(aconvw_s[:, j * 31:(j + 1) * 31], 128), ALU.mult, [ident_b, small], [dg])
            if g == 0:
                dump("dg0", dgl[0].ap, [dgl[0]])
                dump("sgc", sgc.ap, [sgc])
            csb2 = [Buf(AR.f32(512, "convsb%d" % i)) for i in range(2)]
            cjunk = Buf(AR.f32(512, "cjunk"))
            for tb in range(0, nt, 4):
                for i in range(4):
                    ti = tb + i
                    s_idx = ti // seqs[0][1]
                    t0 = (ti % seqs[0][1]) * 128
                    cs_ = csb2[ti % 2]
                    for j in range(4):
                        ps = bank()
                        dgv = dgl[j].ap.rearrange("p (k m) -> p k m", k=31)
                        for k in range(31):
                            mm(ps.ap[:, 0:128], cvv[:, j, s_idx, t0 + k:t0 + k + 128], dgv[:, k, :],
                               k == 0, False, [cvin, dgl[j]], ps, inc=False)
                        mm(ps.ap[:, 0:128], ones_b.ap[:, 0:128],
                           abrow.ap[:, j * 128:(j + 1) * 128], False, True, [ones_b, abrow], ps, inc=True)
                        cp(cs_.ap[:, j * 128:(j + 1) * 128], ps.ap[:, 0:128], [ps], [cs_], eng="act")
                    act(cjunk.ap, cs_.ap, AF.Identity, [cs_], [cjunk, lst], accum_out=lst.ap[:, 0:1])
                    act(cjunk.ap, cs_.ap, AF.Square, [cs_], [cjunk, lst], accum_out=lst.ap[:, 1:2])
                    ts(lst.ap[:, 2:4], lst.ap[:, 0:2], 1.0 / 512, None, ALU.mult, None, [lst], [lst])
                    tt(lst.ap[:, 4:5], lst.ap[:, 2:3], lst.ap[:, 2:3], ALU.mult, [lst], [lst])
                    tt(lst.ap[:, 5:6], lst.ap[:, 3:4], lst.ap[:, 4:5], ALU.subtract, [lst], [lst])
                    act(lst.ap[:, 8:9], lst.ap[:, 5:6], AF.Sqrt, [lst, small], [lst], scale=1.0, bias=eps_s)
                    recip(lst.ap[:, 8:9], lst.ap[:, 8:9], [lst], [lst])
                    ts(xnb.ap[:, i * 512:(i + 1) * 512], cs_.ap, lst.ap[:, 2:3], lst.ap[:, 8:9], ALU.subtract, ALU.mult,
                       [cs_, lst], [xnb])
                if g == 0 and tb == 0:
                    dump("xnb", xnb.ap, [xnb])
                    dump("lst", lst.ap, [lst])
                for j in range(4):
                    ps = bank()
                    psb = ps.ap.bitcast(BF16)
                    for i in range(4):
                        tr(psb[:, i * 128:(i + 1) * 128], xnb.ap[:, i * 512 + j * 128: i * 512 + (j + 1) * 128],
                           ident_b.ap, [xnb, ident_b], ps, inc=(i == 3))
                    act(sig.ap, psb[:, 0:512], AF.Silu, [ps, small], [sig], scale=lngT_s[:, j:j + 1],
                        bias=lnbT_s[:, j:j + 1])
                    tt(mcv[:, j, tb * 128:(tb + 4) * 128], sig.ap, sgv[:, j, tb * 128:(tb + 4) * 128], ALU.mult,
                       [sig, sgc], [mixTc])
            AR.pop()

            if KSUB < 2:
                AR.pop()
                continue
            q_tm = Buf(AR.bf16(nt * 768, "q_tm"))
            qv = q_tm.ap.rearrange("p (t c) -> p t c", t=nt)
            ckvT = Buf(AR.bf16(Lk, "ckvT"))
            krT = Buf(AR.bf16(Lk, "krT", parts=96))
            VA = Buf(AR.bf16(nkt * 1024, "VA"))
            vav = VA.ap.rearrange("p (t h c) -> p t h c", t=nkt, h=8)
            wsm = Buf(AR.bf16(2 * 768 + 512 + 512, "wsm"))
            wuq = wsm.ap[:, 0:1536].rearrange("p (k n) -> p k n", k=2)
            wuk = wsm.ap[:, 1536:2048]
            wuv = wsm.ap[:, 2048:2560]
            P.dma("pool", wuq, a_w_uq.rearrange("(k p) n -> p k n", p=128), writes=[wsm])
            P.dma("pool", wuk, a_w_uk, writes=[wsm])
            P.dma("pool", wuv, a_w_uv, writes=[wsm])
            fill_bf16(VA, 1.0)

            AR.push()
            wb, wv = load_wblock(a_w_in[:, 0:416], 416)
            qnT = Buf(AR.bf16(2 * Tg, "qnT"))
            qnv = qnT.ap.rearrange("p (k t) -> p k t", k=2)
            tmp = Buf(AR.f32(512, "p2tmp"))
            st = Buf(AR.f32(8, "p2st"))
            qn = Buf(AR.bf16(256, "qn"))
            ckv_f = Buf(AR.f32(128, "ckv_f"))
            ckv_b = Buf(AR.bf16(128, "ckv_b"))
            kst = Buf(AR.bf16(96, "kst"))
            kr_f = Buf(AR.f32(32, "kr_f"))
            rt = Buf(AR.f32(64, "ropetmp"))
            qfb = [Buf(AR.f32(384, "qf%d" % i)) for i in range(2)]
            fill_bf16(kst, 0.0)

            def k_transposes(kt_idx, ckv_b, kst):
                ps = bank()
                psb = ps.ap.bitcast(BF16)
                tr(psb[:, 0:128], ckv_b.ap, ident_b.ap, [ckv_b, ident_b], ps, inc=False)
                tr(psb[0:96, 128:256], kst.ap, ident_b.ap, [kst, ident_b], ps, inc=True)
                cp(ckvT.ap[:, kt_idx * 128:(kt_idx + 1) * 128], psb[:, 0:128], [ps], [ckvT], eng="act")
                cp(krT.ap[64:96, kt_idx * 128:(kt_idx + 1) * 128], psb[64:96, 128:256], [ps], [krT], eng="act")

            for c in range(nctx):
                P.dma("sp", ckv_f.ap, cckv[c * 128:(c + 1) * 128, :], writes=[ckv_f])
                P.dma("sp", kr_f.ap, ckr[c * 128:(c + 1) * 128, :], writes=[kr_f])
                cp(ckv_b.ap, ckv_f.ap, [ckv_f], [ckv_b])
                cp(kst.ap[:, 64:96], kr_f.ap, [kr_f], [kst])
                k_transposes(c, ckv_b, kst)

            KP2 = int(os.environ.get("KP2", "99"))
            for ti in range(nt if KP2 >= 2 else 0):
                ps = bank()
                for k in range(8):
                    mm(ps.ap[:, 0:416], hv[:, k, ti * 128:(ti + 1) * 128], wv[:, k, :], k == 0, k == 7,
                       [hT, wb], ps, inc=(k == 7))
                act(tmp.ap[:, 0:256], ps.ap[:, 0:256], AF.Square, [ps], [tmp, st], accum_out=st.ap[:, 0:1])
                act(tmp.ap[:, 256:384], ps.ap[:, 256:384], AF.Square, [ps], [tmp, st], accum_out=st.ap[:, 1:2])
                act(st.ap[:, 0:1], st.ap[:, 0:1], AF.Sqrt, [st, small], [st], scale=1.0 / 256, bias=eps_s)
                act(st.ap[:, 1:2], st.ap[:, 1:2], AF.Sqrt, [st, small], [st], scale=1.0 / 128, bias=eps_s)
                recip(st.ap[:, 0:2], st.ap[:, 0:2], [st], [st])
                stt(qn.ap, ps.ap[:, 0:256], st.ap[:, 0:1], gq_bc, ALU.mult, ALU.mult, [ps, st, bcs], [qn])
                stt(ckv_f.ap, ps.ap[:, 256:384], st.ap[:, 1:2], gkv_bc, ALU.mult, ALU.mult, [ps, st, bcs], [ckv_f])
                cp(ckv_b.ap, ckv_f.ap, [ckv_f], [ckv_b])
                if g == 0:
                    cp(kr_f.ap, ps.ap[:, 384:416], [ps], [kr_f], eng="act")
                    P.dma("sp", o_ckv[ti * 128:(ti + 1) * 128, :], ckv_f.ap, reads=[ckv_f], is_output=True)
                    P.dma("sp", o_kr[ti * 128:(ti + 1) * 128, :], kr_f.ap, reads=[kr_f], is_output=True)
                    cp(kst.ap[:, 64:96], ps.ap[:, 384:416], [ps], [kst])
                else:
                    cs = rope_s.ap.rearrange("p (t c) -> p t c", t=8)[:, ti, :]
                    cosv, sinv = cs[:, 0:16], cs[:, 16:32]
                    kp = ps.ap[:, 384:416].rearrange("p (i two) -> p i two", two=2)
                    ko = kst.ap[:, 64:96].rearrange("p (i two) -> p i two", two=2)
                    tt(rt.ap[:, 0:16], kp[:, :, 0], cosv, ALU.mult, [ps, rope_s], [rt])
                    tt(rt.ap[:, 16:32], kp[:, :, 1], sinv, ALU.mult, [ps, rope_s], [rt])
                    tt(rt.ap[:, 32:48], kp[:, :, 0], sinv, ALU.mult, [ps, rope_s], [rt])
                    tt(rt.ap[:, 48:64], kp[:, :, 1], cosv, ALU.mult, [ps, rope_s], [rt])
                    tt(ko[:, :, 0], rt.ap[:, 0:16], rt.ap[:, 16:32], ALU.subtract, [rt], [kst])
                    tt(ko[:, :, 1], rt.ap[:, 32:48], rt.ap[:, 48:64], ALU.add, [rt], [kst])
                k_transposes(nctx + ti, ckv_b, kst)
                ps2 = bank()
                ps2b = ps2.ap.bitcast(BF16)
                for k in range(2):
                    tr(ps2b[:, k * 128:(k + 1) * 128], qn.ap[:, k * 128:(k + 1) * 128], ident_b.ap, [qn, ident_b], ps2,
                       inc=(k == 1))
                cp(qnv[:, :, ti * 128:(ti + 1) * 128], ps2b[:, 0:256].rearrange("p (k t) -> p k t", k=2), [ps2], [qnT],
                   eng="act")
            for kt in range(nkt if KP2 >= 3 else 0):
                ps = bank()
                mm(ps.ap, ckvT.ap[:, kt * 128:(kt + 1) * 128], wuv, True, True, [ckvT, wsm], ps, inc=True)
                pvv = ps.ap.rearrange("p (j two c) -> p j two c", j=4, two=2)
                vv5 = VA.ap.rearrange("p (t j two c) -> p t j two c", t=nkt, j=4, two=2)
                cp(vv5[:, kt, :, 0, 0:64], pvv[:, :, 0, :], [ps], [VA], eng="act")
                cp(vv5[:, kt, :, 1, 64:128], pvv[:, :, 1, :], [ps], [VA], eng="act")
            for ti in range(nt if KP2 >= 4 else 0):
                for half in range(2):
                    ps = bank()
                    for k in range(2):
                        mm(ps.ap[:, 0:384], qnv[:, k, ti * 128:(ti + 1) * 128], wuq[:, k, half * 384:(half + 1) * 384],
                           k == 0, k == 1, [qnT, wsm], ps, inc=(k == 1))
                    pv = ps.ap[:, 0:384].rearrange("p (h c) -> p h c", h=4)
                    qo = qv[:, ti, half * 384:(half + 1) * 384].rearrange("p (h c) -> p h c", h=4)
                    if g == 0:
                        act(qo, pv, AF.Copy, [ps], [q_tm], scale=ATTN_SCALE)
                    else:
                        qf = qfb[(ti * 2 + half) % 2]
                        act(qf.ap, ps.ap[:, 0:384], AF.Copy, [ps], [qf], scale=ATTN_SCALE)
                        qfv = qf.ap.rearrange("p (h c) -> p h c", h=4)
                        cp(qo[:, :, 0:64], qfv[:, :, 0:64], [qf], [q_tm])
                        cs = rope_s.ap.rearrange("p (t c) -> p t c", t=8)[:, ti, :]
                        cosv, sinv = cs[:, 0:16], cs[:, 16:32]
                        cosb = cosv.unsqueeze(1).to_broadcast([128, 4, 16])
                        sinb = sinv.unsqueeze(1).to_broadcast([128, 4, 16])
                        pr = qfv[:, :, 64:96].rearrange("p h (i two) -> p h i two", two=2)
                        qr = qo[:, :, 64:96].rearrange("p h (i two) -> p h i two", two=2)
                        rr = tmp.ap[:, 0:256].rearrange("p (a h i) -> p a h i", a=4, h=4)
                        tt(rr[:, 0], pr[:, :, :, 0], cosb, ALU.mult, [qf, rope_s], [tmp])
                        tt(rr[:, 1], pr[:, :, :, 1], sinb, ALU.mult, [qf, rope_s], [tmp])
                        tt(rr[:, 2], pr[:, :, :, 0], sinb, ALU.mult, [qf, rope_s], [tmp])
                        tt(rr[:, 3], pr[:, :, :, 1], cosb, ALU.mult, [qf, rope_s], [tmp])
                        tt(qr[:, :, :, 0], rr[:, 0], rr[:, 1], ALU.subtract, [tmp], [q_tm])
                        tt(qr[:, :, :, 1], rr[:, 2], rr[:, 3], ALU.add, [tmp], [q_tm])
            AR.pop()

            if KSUB < 3:
                AR.pop()
                continue
            AR.push()
            wgt, vgt = load_wblock(a_w_in[:, 1440:1952], 512)
            QT = [Buf(AR.bf16(Tg, "QT%d" % i, parts=96)) for i in range(2)]
            KT = [Buf(AR.bf16(Lk, "KT%d" % i, parts=96)) for i in range(2)]
            sgT = [Buf(AR.bf16(Tg, "sgT%d" % i)) for i in range(2)]
            PT = [Buf(AR.bf16(512, "PT%d" % i)) for i in range(3)]
            Rr = [Buf(AR.f32(512, "Rr%d" % i)) for i in range(2)]
            t1 = [Buf(AR.f32(512, "t1%d" % i)) for i in range(2)]
            pti = 0
            poi = [0]
            wide[0] = False
            for h in range(8):
                qt, ktb, sg = QT[h % 2], KT[h % 2], sgT[(h // 2) % 2]
                vlo, vhi = (0, 64) if h % 2 == 0 else (64, 128)
                dlo, dhi = (64, 128) if h % 2 == 0 else (0, 64)
                for tb in range(0, nt, 8):
                    nn = min(8, nt - tb)
                    ps = bank()
                    psb = ps.ap.bitcast(BF16)
                    for i in range(nn):
                        tr(psb[0:96, i * 128:(i + 1) * 128], qv[:, tb + i, h * 96:(h + 1) * 96], ident_b.ap,
                           [q_tm, ident_b], ps, inc=(i == nn - 1))
                    cp(qt.ap[:, tb * 128:(tb + nn) * 128], psb[0:96, 0:nn * 128], [ps], [qt], eng="act")
                for nb in range(0, Lk, 512):
                    w = min(512, Lk - nb)
                    ps = bank()
                    mm(ps.ap[0:64, 0:w], wuk[:, h * 64:(h + 1) * 64], ckvT.ap[:, nb:nb + w], True, True, [wsm, ckvT], ps,
                       inc=True)
                    cp(ktb.ap[0:64, nb:nb + w], ps.ap[0:64, 0:w], [ps], [ktb], eng="act")
                cp(ktb.ap[64:96, :], krT.ap[64:96, :], [krT], [ktb])
                for nb in range(Tg // 512 if h % 2 == 0 else 0):
                    ps = bank()
                    for k in range(8):
                        mm(ps.ap, vgt[:, k, h * 64:(h + 2) * 64], hv[:, k, nb * 512:(nb + 1) * 512], k == 0,
                           k == 7, [wgt, hT], ps, inc=(k == 7))
                    act(sg.ap[:, nb * 512:(nb + 1) * 512], ps.ap, AF.Silu, [ps], [sg])
                for (s0, sl) in seqs:
                    q0 = s0 * 128
                    qlen = sl * 128
                    if g == 0:
                        ktiles = list(range(s0, s0 + sl))
                    else:
                        ktiles = list(range(nkt))
                    for qb in range(0, qlen, 512):
                        w = min(512, qlen - qb)
                        po = lbank(poi[0])
                        poi[0] += 1
                        for ki, kt in enumerate(ktiles):
                            ps = bank()
                            mm(ps.ap[:, 0:w], ktb.ap[0:96, kt * 128:(kt + 1) * 128], qt.ap[0:96, q0 + qb:q0 + qb + w],
                               True, True, [ktb, qt], ps, inc=True)
                            pt = PT[pti % 3]
                            pti += 1
                            act(pt.ap[:, 0:w], ps.ap[:, 0:w], AF.Exp, [ps], [pt])
                            mm(po.ap[:, 0:w], vav[:, kt, h, :], pt.ap[:, 0:w], ki == 0, ki == len(ktiles) - 1, [VA, pt], po,
                               inc=(ki == len(ktiles) - 1))
                        rr = Rr[h % 2]
                        tt1 = t1[h % 2]
                        recip(rr.ap[dlo:dhi, 0:w], po.ap[dlo:dhi, 0:w], [po], [rr])
                        tt(tt1.ap[vlo:vhi, 0:w], po.ap[vlo:vhi, 0:w], rr.ap[dlo:dhi, 0:w], ALU.mult, [po, rr], [tt1])
                        tt(mav[vlo:vhi, h // 2, q0 + qb:q0 + qb + w], tt1.ap[vlo:vhi, 0:w],
                           sg.ap[vlo:vhi, q0 + qb:q0 + qb + w], ALU.mult, [tt1, sg], [mixTa])
            AR.pop()

            wide[0] = True
            if KSUB < 4:
                AR.pop()
                continue
            AR.push()
            otmp = [Buf(AR.f32(512, "otmp%d" % i)) for i in range(2)]
            gt = Buf(AR.f32(D, "gt_a"))
            gate_bc(l, g, gt)
            dump("mixTa%d" % g, mixTa.ap, [mixTa])
            dump("mixTc%d" % g, mixTc.ap, [mixTc])
            dump("gt%d" % g, gt.ap, [gt])
            dump("hT%d" % g, hT.ap, [hT])
            for nb in range(2):
                wo_b, wo = load_wblock(a_w_out[:, nb * 512:(nb + 1) * 512], 512)
                for ti in range(nt):
                    ps = bank()
                    for c in range(8):
                        lhs = mav[:, c, ti * 128:(ti + 1) * 128] if c < 4 else mcv[:, c - 4, ti * 128:(ti + 1) * 128]
                        mm(ps.ap, lhs, wo[:, c, :], c == 0, c == 7, [mixTa, mixTc, wo_b], ps, inc=(c == 7))
                    ot = otmp[ti % 2]
                    xr = XR[tiles[ti]]
                    tt(ot.ap, ps.ap, gt.ap[:, nb * 512:(nb + 1) * 512], ALU.mult, [ps, gt], [ot])
                    if ti == 0 and g == 0:
                        dump("ot%d" % nb, ot.ap, [ot])
                        dump("xrpre%d" % nb, xr.ap[:, nb * 512:(nb + 1) * 512], [xr])
                    tt(xr.ap[:, nb * 512:(nb + 1) * 512], xr.ap[:, nb * 512:(nb + 1) * 512], ot.ap, ALU.add, [xr, ot],
                       [xr], eng="dve")
            AR.pop()
            AR.pop()

    def layer_c():
        dump("xr0_afterA", XR[0].ap, [XR[0]])
        l = 1
        HP = 4
        CW = HP * 64
        groups = [dict(g=0, tiles=[0, 1, 2, 3], seqs=[(0, 2), (2, 2)]),
                  dict(g=1, tiles=list(range(4, 12)), seqs=[(0, 8)])]
        wdt_b = Buf(AR.bf16(8 * 64, "wdt"))
        wdtv = wdt_b.ap.rearrange("p (k n) -> p k n", k=8)
        P.dma("pool", wdtv, c_w_in[:, 5120:5184].rearrange("(k p) n -> p k n", p=128), writes=[wdt_b])
        selv = sel.ap
        import os
        KCG = int(os.environ.get("KCG", "2"))
        KCPASS = int(os.environ.get("KCPASS", "8"))
        KCSCAN = int(os.environ.get("KCSCAN", "1"))
        KCOUT = int(os.environ.get("KCOUT", "1"))
        def issue_pass_loads(hq_):
            sg_ = hq_ // 2
            wzx_ = WB[wb_i[0]]
            wb_i[0] = (wb_i[0] + 1) % NWB
            vzx_ = wzx_.ap[:, 0:8 * 512].rearrange("p (k n) -> p k n", k=8)
            P.dma("pool", vzx_[:, :, 0:CW], c_w_in[:, hq_ * CW:(hq_ + 1) * CW].rearrange("(k p) n -> p k n", p=128),
                  writes=[wzx_])
            P.dma("pool", vzx_[:, :, CW:2 * CW],
                  c_w_in[:, 2048 + hq_ * CW: 2048 + (hq_ + 1) * CW].rearrange("(k p) n -> p k n", p=128), writes=[wzx_])
            wbc_ = WB[wb_i[0]]
            wb_i[0] = (wb_i[0] + 1) % NWB
            vbc_ = wbc_.ap[:, 0:8 * 256].rearrange("p (k n) -> p k n", k=8)
            P.dma("pool", vbc_[:, :, 0:128],
                  c_w_in[:, 4096 + sg_ * 128: 4096 + (sg_ + 1) * 128].rearrange("(k p) n -> p k n", p=128), writes=[wbc_])
            P.dma("pool", vbc_[:, :, 128:256],
                  c_w_in[:, 4608 + sg_ * 128: 4608 + (sg_ + 1) * 128].rearrange("(k p) n -> p k n", p=128), writes=[wbc_])
            return (wzx_, vzx_, wbc_, vbc_)

        for grp in groups[:KCG]:
            g = grp["g"]
            tiles = grp["tiles"]
            seqs = grp["seqs"]
            nt = len(tiles)
            Tg = nt * 128
            AR.push()
            nxt_w = [issue_pass_loads(0) if KCPASS > 0 else None]
            hT = Buf(AR.bf16(8 * Tg, "hT"))
            build_hT(l, tiles, g, hT)
            hv = hT.ap.rearrange("p (k t) -> p k t", k=8)
            ygT = Buf(AR.bf16(16 * Tg, "ygT"))
            ygv = ygT.ap.rearrange("p (c t) -> p c t", c=16)
            A_tm = Buf(AR.f32(nt * 64, "A_tm"))
            eA = Buf(AR.f32(nt * 64, "eA"))
            cdec = Buf(AR.f32(nt * 64, "cdec"))
            wdt = Buf(AR.f32(nt * 64, "wdt_s"))
            wde = Buf(AR.f32(nt * 64, "wde_s"))
            ssq = Buf(AR.f32(nt * 8, "ssq"))
            Apc = [Buf(AR.bf16(Tg, "Apc%d" % d, parts=64)) for d in range(2)]
            v3 = lambda b_: b_.ap.rearrange("p (t c) -> p t c", t=nt)
            AR.push()
            a_s = Buf(AR.f32(nt * 64, "a_s"))
            a_res = Buf(AR.f32(nt * 64, "a_res"))
            a_hl = Buf(AR.bf16(nt * 128, "a_hl"))
            ahv = a_hl.ap.rearrange("p (t q c) -> p t q c", t=nt, q=2)
            arep = Buf(AR.bf16(nt * 256, "arep"))
            arv = arep.ap.rearrange("p (t d q r c) -> p t d q r c", t=nt, d=2, q=2, r=2)
            csb = Buf(AR.f32(128, "csb"))
            tmpd = Buf(AR.f32(64, "tmpd"))
            r1 = Buf(AR.f32(512, "r1", parts=64))
            for ti in range(nt):
                ps = bank()
                for k in range(8):
                    mm(ps.ap[:, 0:64], hv[:, k, ti * 128:(ti + 1) * 128], wdtv[:, k, :], k == 0, k == 7, [hT, wdt_b], ps,
                       inc=(k == 7))
                tt(tmpd.ap, ps.ap[:, 0:64], dtb_bc, ALU.add, [ps, bcs], [tmpd])
                act(tmpd.ap, tmpd.ap, AF.Exp, [tmpd], [tmpd])
                act(v3(wdt)[:, ti, :], tmpd.ap, AF.Ln, [tmpd], [wdt], bias=1.0)
                tt(v3(a_s)[:, ti, :], v3(wdt)[:, ti, :], aneg_bc, ALU.mult, [wdt, bcs], [a_s])
            a3 = a_s.ap.rearrange("p (t c) -> p t c", t=nt)
            cp(ahv[:, :, 0, :], a3, [a_s], [a_hl])
            tt(a_res.ap.rearrange("p (t c) -> p t c", t=nt), a3, ahv[:, :, 0, :], ALU.subtract, [a_s, a_hl], [a_res])
            cp(ahv[:, :, 1, :], a_res.ap.rearrange("p (t c) -> p t c", t=nt), [a_res], [a_hl])
            for q in range(2):
                for r_i in range(2):
                    cp(arv[:, :, :, q, r_i, :], ahv[:, :, q, :].rearrange("p t (d c) -> p t d c", d=2), [a_hl], [arep])
            for ti in range(nt):
                ps = bank()
                for q in range(2):
                    mm(ps.ap[:, 0:32], tmf_b.ap, ahv[:, ti, q, 0:32], q == 0, q == 1, [tmf_b, a_hl], ps, inc=False)
                for q in range(2):
                    mm(ps.ap[:, 32:64], tmb_b.ap, ahv[:, ti, q, 32:64], q == 0, q == 1, [tmb_b, a_hl], ps, inc=False)
                for q in range(2):
                    mm(ps.ap[:, 64:128], ones_b.ap[:, 0:128], ahv[:, ti, q, :], q == 0, q == 1, [ones_b, a_hl], ps,
                       inc=(q == 1))
                cp(csb.ap, ps.ap[:, 0:128], [ps], [csb])
                cp(v3(A_tm)[:, ti, :], csb.ap[:, 0:64], [csb], [A_tm], eng="pool")
                act(v3(eA)[:, ti, :], csb.ap[:, 0:64], AF.Exp, [csb], [eA])
                act(v3(cdec)[:, ti, :], csb.ap[:, 64:128], AF.Exp, [csb], [cdec])
                tt(tmpd.ap, csb.ap[:, 64:128], csb.ap[:, 0:64], ALU.subtract, [csb], [tmpd])
                act(tmpd.ap, tmpd.ap, AF.Exp, [tmpd], [tmpd])
                tt(v3(wde)[:, ti, :], tmpd.ap, v3(wdt)[:, ti, :], ALU.mult, [tmpd, wdt], [wde])
            for d in range(2):
                tmd = tmf_b if d == 0 else tmb_b
                for tb in range(0, nt, 4):
                    ps = bank()
                    for i in range(4):
                        for q in range(2):
                            mm(ps.ap[0:64, i * 128:(i + 1) * 128],
                               arv[:, tb + i, d, q, :, :].rearrange("p r c -> p (r c)"), tmd.ap, q == 0, q == 1,
                               [arep, tmd], ps, inc=(i == 3 and q == 1))
                    dst = Apc[d].ap[:, tb * 128:(tb + 4) * 128]
                    cp(dst, ps.ap[0:64, :], [ps], [Apc[d]])
                    tt(r1.ap[32:64, :], ps.ap[32:64, :], dst[32:64, :], ALU.subtract, [ps, Apc[d]], [r1])
                    cp(dst[32:64, :], r1.ap[32:64, :], [r1], [Apc[d]])
            AR.pop()

            for hq in range(KCPASS):
                sg = hq // 2
                h0 = hq * HP
                AR.push()
                sz = Buf(AR.bf16(nt * CW, "sz"))
                xs = Buf(AR.bf16(nt * CW, "xs"))
                Btm = Buf(AR.bf16(nt * 128, "Btm"))
                BT = Buf(AR.bf16(Tg, "BT"))
                CT = Buf(AR.bf16(Tg, "CT"))
                cbm = [Buf(AR.bf16(Tg, "cbm%d" % d)) for d in range(2)]
                prevb = Buf(AR.bf16(nt * CW, "prevb"))
                Sst = [Buf(AR.f32(CW, "S%d" % d)) for d in range(2)]
                brow = Buf(AR.bf16(CW + 128, "browc"))
                fill_bf16(brow, 0.0)
                Ls = seqs[0][1] * 128
                LP = Ls + 4
                raw = [Buf(AR.bf16(len(seqs) * LP, "raw%d" % i)) for i in range(2)]
                dg = [Buf(AR.bf16(5 * 128, "dgc%d" % i)) for i in range(2)]
                v4 = lambda b_: b_.ap.rearrange("p (t c) -> p t c", t=nt)
                for r_ in raw:
                    fill_bf16(r_, 0.0)
                P.dma("pool", brow.ap[0:1, 0:CW], c_convb[:, hq * CW:(hq + 1) * CW], writes=[brow])
                P.dma("pool", brow.ap[0:1, CW:CW + 128], c_convb[:, 2048 + sg * 128: 2048 + (sg + 1) * 128], writes=[brow])
                if nxt_w[0] is None:
                    nxt_w[0] = issue_pass_loads(hq)
                wzx, vzx, wbc_b, vbc = nxt_w[0]
                nxt_w[0] = None
                for ti in range(nt):
                    ps = bank()
                    for k in range(8):
                        mm(ps.ap[:, 0:CW], hv[:, k, ti * 128:(ti + 1) * 128], vzx[:, k, 0:CW], k == 0, k == 7, [hT, wzx], ps,
                           inc=(k == 7))
                    act(v4(sz)[:, ti, :], ps.ap[:, 0:CW], AF.Silu, [ps], [sz])
                chunk_specs = [("x", wzx, vzx[:, :, CW + j * 128: CW + (j + 1) * 128], hq * 2 + j, j) for j in range(2)]
                chunk_specs.append(("B", wbc_b, vbc[:, :, 0:128], 16 + sg, 0))
                chunk_specs.append(("C", wbc_b, vbc[:, :, 128:256], 20 + sg, 0))
                for ci, (kind, wbuf, wview, cch, j) in enumerate(chunk_specs):
                    rw = raw[ci % 2]
                    rv = rw.ap.rearrange("p (s t) -> p s t", s=len(seqs))
                    dgc = dg[ci % 2]
                    dgv = dgc.ap.rearrange("p (k m) -> p k m", k=5)
                    tt(dgv, ident_b.ap.unsqueeze(1).to_broadcast([128, 5, 128]),
                       bcast_last(cconvw_s[:, cch * 5:(cch + 1) * 5], 128), ALU.mult, [ident_b, small], [dgc])
                    for nb in range(Tg // 512):
                        ps = bank()
                        for k in range(8):
                            mm(ps.ap, wview[:, k, :], hv[:, k, nb * 512:(nb + 1) * 512], k == 0, k == 7, [wbuf, hT], ps,
                               inc=(k == 7))
                        if Ls < 512:
                            for si in range(512 // Ls):
                                cp(rv[:, nb * (512 // Ls) + si, 2:2 + Ls], ps.ap[:, si * Ls:(si + 1) * Ls], [ps], [rw],
                                   eng="act")
                        else:
                            cp(rv[:, 0, 2 + nb * 512: 2 + (nb + 1) * 512], ps.ap, [ps], [rw], eng="act")
                    if kind in ("x", "B"):
                        bias_row = brow.ap[:, j * 128:(j + 1) * 128] if kind == "x" else brow.ap[:, CW:CW + 128]
                        for tb in range(0, nt, 4):
                            ps = bank()
                            for i in range(4):
                                ti = tb + i
                                s_idx = ti // seqs[0][1]
                                t0 = (ti % seqs[0][1]) * 128
                                pso = ps.ap[:, i * 128:(i + 1) * 128]
                                for k in range(5):
                                    mm(pso, rv[:, s_idx, t0 + k:t0 + k + 128], dgv[:, k, :], k == 0, False, [rw, dgc], ps,
                                       inc=False)
                                mm(pso, ones_b.ap[:, 0:128], bias_row, False, True, [ones_b, brow], ps, inc=(i == 3))
                            pin = ps.ap.rearrange("p (i c) -> p i c", i=4)
                            if kind == "x":
                                act(v4(xs)[:, tb:tb + 4, j * 128:(j + 1) * 128], pin, AF.Silu, [ps], [xs])
                            else:
                                act(v4(Btm)[:, tb:tb + 4, :], pin, AF.Silu, [ps], [Btm])
                    if kind in ("B", "C"):
                        dstT = BT if kind == "B" else CT
                        for si in range(len(seqs)):
                            for nb in range(0, Ls, 512):
                                w = min(512, Ls - nb)
                                ps = bank()
                                for k in range(5):
                                    mm(ps.ap[:, 0:w], dgv[:, k, :], rv[:, si, nb + k:nb + k + w], k == 0, k == 4, [dgc, rw],
                                       ps, inc=(k == 4))
                                act(dstT.ap[:, si * Ls + nb: si * Ls + nb + w], ps.ap[:, 0:w], AF.Silu, [ps, small], [dstT],
                                    bias=cconvb_s[:, cch:cch + 1])
                if hq + 1 < KCPASS:
                    nxt_w[0] = issue_pass_loads(hq + 1)
                for tb in range(0, nt, 4):
                    ps = bank()
                    for i in range(4):
                        ti = tb + i
                        mm(ps.ap[:, i * 128:(i + 1) * 128], BT.ap[:, ti * 128:(ti + 1) * 128],
                           CT.ap[:, ti * 128:(ti + 1) * 128], True, True, [BT, CT], ps, inc=(i == 3))
                    pv = ps.ap.rearrange("p (i l) -> p i l", i=4)
                    for d in range(2):
                        tmd = tmf if d == 0 else tmb
                        tt(cbm[d].ap[:, tb * 128:(tb + 4) * 128].rearrange("p (i l) -> p i l", i=4), pv,
                           tmd.ap.unsqueeze(1).to_broadcast([128, 4, 128]), ALU.mult, [ps, tmd], [cbm[d]])
                if not KCSCAN:
                    AR.pop()
                    continue
                AR.push()
                arg = [Buf(AR.f32(512, "arg%d" % i)) for i in range(2)]
                Dm = [Buf(AR.bf16(512, "Dm%d" % i)) for i in range(2)]
                Mm = [Buf(AR.bf16(512, "Mm%d" % i)) for i in range(2)]
                xdt = Buf(AR.bf16(4 * 2 * CW, "xdt"))
                xde = [Buf(AR.bf16(CW, "xde%d" % i)) for i in range(2)]
                tfB = Buf(AR.f32(4 * CW, "tfB"))
                tbB = Buf(AR.f32(4 * CW, "tbB"))
                ytB = Buf(AR.f32(4 * CW, "ytB"))
                ygB = Buf(AR.bf16(4 * CW, "ygB"))
                AR_ygB_f32 = ygB.f32[:, 0:CW]
                xdeB = Buf(AR.bf16(4 * CW, "xdeB"))
                SbfB = [Buf(AR.bf16(CW, "SbfB%d" % i)) for i in range(4)]
                stmp = arg[0]
                stmp_ap = arg[0].ap[:, 0:CW]
                stmp2b = ygB
                stmp2_ap = AR_ygB_f32
                cnt = [0]
                hview = lambda ap_: ap_.rearrange("p (h q) -> p h q", h=HP)

                def scaled_x(dst_ap, dst_buf, ti, wsrc, d):
                    tt(hview(dst_ap), hview(v4(xs)[:, ti, :]),
                       bcast_last(v3(wsrc)[:, ti, d * 32 + h0: d * 32 + h0 + HP], 64), ALU.mult,
                       [xs, wsrc], [dst_buf])

                def state_step(d, ti):
                    xd = xde[cnt[0] % 2]
                    cnt[0] += 1
                    scaled_x(xd.ap, xd, ti, wde, d)
                    ps = bank()
                    mm(ps.ap[:, 0:CW], v4(Btm)[:, ti, :], xd.ap, True, True, [Btm, xd], ps, inc=True)
                    tt(hview(stmp_ap), hview(Sst[d].ap),
                       bcast_last(v3(cdec)[:, ti, d * 32 + h0: d * 32 + h0 + HP], 64), ALU.mult,
                       [Sst[d], cdec], [stmp])
                    tt(Sst[d].ap, stmp_ap, ps.ap[:, 0:CW], ALU.add, [stmp, ps], [Sst[d]])

                for si, (s0, sl) in enumerate(seqs):
                    for d in range(2):
                        if g == 1:
                            P.dma("sp", Sst[d].ap, st0[d][:, hq * CW:(hq + 1) * CW], writes=[Sst[d]])
                        else:
                            memset(Sst[d].ap, 0.0, [Sst[d]], eng="pool")
                    for ti in range(s0 + sl - 1, s0 - 1, -1):
                        cp(v4(prevb)[:, ti, :], Sst[1].ap, [Sst[1]], [prevb], eng="act")
                        state_step(1, ti)
                    if g == 0:
                        P.dma("sp", o_st[si][1][:, hq * CW:(hq + 1) * CW], Sst[1].ap, reads=[Sst[1]], is_output=True)
                    wide[0] = False
                    for tb in range(s0, s0 + sl, 4):
                        nb4 = min(4, s0 + sl - tb)
                        W = nb4 * 128
                        xbv = xdt.ap.rearrange("p (i d c) -> p i d c", i=4, d=2)
                        for d in range(2):
                            tt(xbv[:, 0:nb4, d, :].rearrange("p i (h q) -> p i h q", h=HP),
                               v4(xs)[:, tb:tb + nb4, :].rearrange("p i (h q) -> p i h q", h=HP),
                               bcast_last(v3(wdt)[:, tb:tb + nb4, d * 32 + h0: d * 32 + h0 + HP], 64), ALU.mult,
                               [xs, wdt], [xdt])
                        npair = nb4 // 2
                        yps = [lbank(pr) for pr in range(npair)]
                        xdv = xdeB.ap.rearrange("p (i c) -> p i c", i=4)
                        tt(xdv[:, 0:nb4, :].rearrange("p i (h q) -> p i h q", h=HP),
                           v4(xs)[:, tb:tb + nb4, :].rearrange("p i (h q) -> p i h q", h=HP),
                           bcast_last(v3(wde)[:, tb:tb + nb4, h0:h0 + HP], 64), ALU.mult, [xs, wde], [xdeB])
                        tfv = tfB.ap.rearrange("p (i c) -> p i c", i=4)
                        tbv = tbB.ap.rearrange("p (i c) -> p i c", i=4)

                        def rec_step(i):
                            ti = tb + i
                            cp(SbfB[i].ap, Sst[0].ap, [Sst[0]], [SbfB[i]], eng="act")
                            ps = bank()
                            mm(ps.ap[:, 0:CW], v4(Btm)[:, ti, :], xdv[:, i, :], True, True, [Btm, xdeB], ps, inc=True)
                            tt(hview(stmp2_ap), hview(Sst[0].ap),
                               bcast_last(v3(cdec)[:, ti, h0: h0 + HP], 64), ALU.mult, [Sst[0], cdec], [stmp2b])
                            tt(Sst[0].ap, stmp2_ap, ps.ap[:, 0:CW], ALU.add, [stmp2b, ps], [Sst[0]])

                        def yoff_pair(pr):
                            pf = bank()
                            pbk = bank()
                            for u in range(2):
                                ti = tb + pr * 2 + u
                                mm(pf.ap[:, u * CW:(u + 1) * CW], CT.ap[:, ti * 128:(ti + 1) * 128], SbfB[pr * 2 + u].ap,
                                   True, True, [CT, SbfB[pr * 2 + u]], pf, inc=(u == 1))
                            for u in range(2):
                                ti = tb + pr * 2 + u
                                mm(pbk.ap[:, u * CW:(u + 1) * CW], CT.ap[:, ti * 128:(ti + 1) * 128], v4(prevb)[:, ti, :],
                                   True, True, [CT, prevb], pbk, inc=(u == 1))
                            t0_ = tb + pr * 2
                            tt(tfv[:, pr * 2:pr * 2 + 2, :].rearrange("p i (h q) -> p i h q", h=HP),
                               pf.ap.rearrange("p (i h q) -> p i h q", i=2, h=HP),
                               bcast_last(v3(eA)[:, t0_:t0_ + 2, h0:h0 + HP], 64), ALU.mult, [pf, eA], [tfB])
                            tt(tbv[:, pr * 2:pr * 2 + 2, :].rearrange("p i (h q) -> p i h q", h=HP),
                               pbk.ap.rearrange("p (i h q) -> p i h q", i=2, h=HP),
                               bcast_last(v3(eA)[:, t0_:t0_ + 2, 32 + h0:32 + h0 + HP], 64), ALU.mult, [pbk, eA], [tbB])

                        for hh in range(HP):
                            hgl = h0 + hh
                            for d in range(2):
                                ps = bank()
                                mm(ps.ap[:, 0:W], bcast_last(selv[:, hgl], 128),
                                   Apc[d].ap[:, tb * 128: tb * 128 + W], True, True, [sel, Apc[d]], ps, inc=True)
                                a_, d_, m_ = arg[d], Dm[d], Mm[d]
                                tt(a_.ap[:, 0:W].rearrange("p (i l) -> p i l", i=nb4),
                                   ps.ap[:, 0:W].rearrange("p (i l) -> p i l", i=nb4),
                                   bcast_last(v3(A_tm)[:, tb:tb + nb4, d * 32 + hgl], 128), ALU.subtract,
                                   [ps, A_tm], [a_])
                                act(d_.ap[:, 0:W], a_.ap[:, 0:W], AF.Exp, [a_], [d_])
                                stt(m_.ap[:, 0:W], d_.ap[:, 0:W], 1.0, cbm[d].ap[:, tb * 128: tb * 128 + W], ALU.min,
                                    ALU.mult, [d_, cbm[d]], [m_])
                            keep_warm(4)
                            for i in range(nb4):
                                yp_ = yps[i // 2]
                                c0 = (i % 2) * CW + hh * 64
                                for d in range(2):
                                    mm(yp_.ap[:, c0:c0 + 64], Mm[d].ap[:, i * 128:(i + 1) * 128],
                                       xbv[:, i, d, hh * 64:(hh + 1) * 64], d == 0, d == 1, [Mm[d], xdt], yp_,
                                       inc=(d == 1))
                            if hh < nb4:
                                rec_step(hh)
                                if hh % 2 == 1:
                                    yoff_pair(hh // 2)
                        NB = nb4 * CW
                        tfa, tba, ya = tfB.ap[:, 0:NB], tbB.ap[:, 0:NB], ytB.ap[:, 0:NB]
                        tt(tfa, tfa, tba, ALU.add, [tfB, tbB], [tfB])
                        tt(ya.rearrange("p (i h q) -> p i h q", i=nb4, h=HP),
                           v4(xs)[:, tb:tb + nb4, :].rearrange("p i (h q) -> p i h q", h=HP),
                           bcast_last(dsk_bc[:, h0:h0 + HP].unsqueeze(1).to_broadcast([128, nb4, HP]), 64), ALU.mult,
                           [xs, bcs], [ytB])
                        tt(ya, ya, tfa, ALU.add, [ytB, tfB], [ytB])
                        for pr in range(npair):
                            tt(ya[:, pr * 2 * CW:(pr + 1) * 2 * CW], ya[:, pr * 2 * CW:(pr + 1) * 2 * CW],
                               yps[pr].ap[:, 0:2 * CW], ALU.add, [ytB, yps[pr]], [ytB])
                        tt(ya, ya, v4(sz)[:, tb:tb + nb4, :].rearrange("p i c -> p (i c)"), ALU.mult, [ytB, sz], [ytB])
                        for i in range(nb4):
                            ti = tb + i
                            act(tbB.ap[:, i * CW:(i + 1) * CW], ya[:, i * CW:(i + 1) * CW], AF.Square, [ytB], [tbB, ssq],
                                accum_out=ssq.ap[:, ti * 8 + hq: ti * 8 + hq + 1])
                        cp(ygB.ap[:, 0:NB], ya, [ytB], [ygB], eng="act")
                        ps = bank()
                        psb = ps.ap.bitcast(BF16)
                        for i in range(nb4):
                            for j in range(2):
                                tr(psb[:, (i * 2 + j) * 128:(i * 2 + j + 1) * 128],
                                   ygB.ap[:, i * CW + j * 128: i * CW + (j + 1) * 128], ident_b.ap,
                                   [ygB, ident_b], ps, inc=(i == nb4 - 1 and j == 1))
                        for j in range(2):
                            tt(ygv[:, hq * 2 + j, tb * 128:(tb + nb4) * 128].rearrange("p (i t) -> p i t", i=nb4),
                               psb[:, 0:nb4 * 256].rearrange("p (i j t) -> p i j t", i=nb4, j=2)[:, :, j, :],
                               gnT_s[:, hq * 2 + j: hq * 2 + j + 1].unsqueeze(2).to_broadcast([128, nb4, 128]),
                               ALU.mult, [ps, small], [ygT])
                    wide[0] = True
                    if g == 0:
                        P.dma("sp", o_st[si][0][:, hq * CW:(hq + 1) * CW], Sst[0].ap, reads=[Sst[0]], is_output=True)
                AR.pop()
                AR.pop()

            if not KCOUT:
                AR.pop()
                continue
            AR.push()
            rs = Buf(AR.f32(nt, "rs"))
            otmp = [Buf(AR.f32(512, "otmpc%d" % i)) for i in range(2)]
            gt = Buf(AR.f32(D, "gt_c"))
            gate_bc(l, g, gt)
            P.op("dve", lambda e, ssq=ssq, rs=rs, nt=nt: e.tensor_reduce(
                out=rs.ap, in_=ssq.ap.rearrange("p (t c) -> p t c", t=nt), axis=mybir.AxisListType.X, op=ALU.add),
                reads=[ssq], writes=[rs])
            act(rs.ap, rs.ap, AF.Sqrt, [rs, small], [rs], scale=1.0 / 2048, bias=eps_s)
            recip(rs.ap, rs.ap, [rs], [rs])
            for nb in range(2):
                w1, v1 = load_wblock(c_w_out[0:1024, nb * 512:(nb + 1) * 512], 512)
                w2, v2 = load_wblock(c_w_out[1024:2048, nb * 512:(nb + 1) * 512], 512)
                for ti in range(nt):
                    ps = bank()
                    for c in range(16):
                        wv_ = v1 if c < 8 else v2
                        wb_ = w1 if c < 8 else w2
                        mm(ps.ap, ygv[:, c, ti * 128:(ti + 1) * 128], wv_[:, c % 8, :], c == 0, c == 15, [ygT, wb_], ps,
                           inc=(c == 15))
                    ot = otmp[ti % 2]
                    xr = XR[tiles[ti]]
                    stt(ot.ap, ps.ap, rs.ap[:, ti:ti + 1], gt.ap[:, nb * 512:(nb + 1) * 512], ALU.mult, ALU.mult,
                        [ps, rs, gt], [ot])
                    tt(xr.ap[:, nb * 512:(nb + 1) * 512], xr.ap[:, nb * 512:(nb + 1) * 512], ot.ap, ALU.add, [xr, ot],
                       [xr], eng="dve")
            AR.pop()
            AR.pop()

    import os
    STAGE = int(os.environ.get("KSTAGE", "99"))
    if STAGE >= 1:
        modulation_fm(0)
    if STAGE >= 2:
        layer_a()
    if STAGE >= 3:
        modulation_fm(1)
    if STAGE >= 4:
        layer_c()
    dump("xr0_final", XR[0].ap, [XR[0]])
    dump("xr3_final", XR[3].ap, [XR[3]])
    AR.push()
    junk = Buf(AR.f32(D, "fjunk"))
    st = Buf(AR.f32(16, "fst"))
    gfb = Buf(AR.f32(D, "gfin_bc"))
    gfin_bc = gfb.ap
    P.dma("sp", gfin_bc, pbc(gfin), writes=[gfb])
    yo = [Buf(AR.f32(D, "yo%d" % i)) for i in range(2)]
    for t in range(12):
        s_ = st.ap[:, t:t + 1]
        act(junk.ap, XR[t].ap, AF.Square, [XR[t]], [junk, st], accum_out=s_)
        act(s_, s_, AF.Sqrt, [st, small], [st], scale=1.0 / D, bias=eps_s)
        recip(s_, s_, [st], [st])
        y_ = yo[t % 2]
        stt(y_.ap, XR[t].ap, s_, gfin_bc, ALU.mult, ALU.mult, [XR[t], st, gfb], [y_])
        P.dma("sp", y_all[t * 128:(t + 1) * 128, :], y_.ap, reads=[y_], is_output=True)
    AR.pop()
    P.finish()
    with nc.Block() as block:
        P.replay(block)
    es.close()
    return nc


def _consts():
    ident = np.eye(128, dtype=np.float32)
    s = np.arange(128)
    tmf = (s[:, None] <= s[None, :]).astype(np.float32)
    tmb = (s[:, None] >= s[None, :]).astype(np.float32)
    sel = np.zeros((64, 32), np.float32)
    for k in range(64):
        sel[k, k % 32] = 1.0
    rows_ = 1024 // 64
    row = np.repeat(np.arange(rows_, dtype=np.float32), 64)
    col = np.tile(np.arange(64, dtype=np.float32), rows_)
    n_freq = 8
    inv = np.power(np.float32(10000.0), -np.arange(n_freq, dtype=np.float32) / n_freq).astype(np.float32)
    ang = np.concatenate([row[:, None] * inv, col[:, None] * inv], axis=-1).astype(np.float32)
    ropecs = np.concatenate([np.cos(ang), np.sin(ang)], axis=-1).astype(np.float32)
    return ident, tmf, tmb, sel, ropecs


_NC_CACHE = {}


def kernel(x_prompt, x_sample, cache_ckv, cache_krope, state_ssd, c, c_ctx,
           w_mod, b_mod, g_pre, g_final,
           a_w_in, a_g_q, a_g_kv, a_w_uq, a_w_uk, a_w_uv, a_conv_w, a_conv_b, a_ln_g, a_ln_b, a_w_out,
           c_w_in, c_conv_w, c_conv_b, c_dt_bias, c_a_log, c_d, c_g_norm, c_w_out):
    f = lambda a: np.ascontiguousarray(np.asarray(a, dtype=np.float32))
    x_prompt, x_sample = f(x_prompt), f(x_sample)
    ident, tmf, tmb, sel, ropecs = _consts()
    fm = lambda v, n: f(np.asarray(v, np.float32).reshape(n, 128).T)
    b_mod = np.asarray(b_mod, np.float32)
    g_pre = np.asarray(g_pre, np.float32)
    shared = {
        "w_mod": f(w_mod),
        "bmodT": f(np.concatenate([fm(b_mod[0], 24), fm(b_mod[1], 24)], axis=1)),
        "bmod_row": f(b_mod.reshape(1, -1)),
        "gpreT": f(np.concatenate([fm(g_pre[0], 8), fm(g_pre[1], 8)], axis=1)),
        "gfin": f(np.asarray(g_final).reshape(1, -1)),
        "ropecs": ropecs,
        "a_w_in": f(a_w_in[0]), "a_gq": f(np.asarray(a_g_q[0]).reshape(1, -1)),
        "a_gkv": f(np.asarray(a_g_kv[0]).reshape(1, -1)),
        "a_w_uq": f(a_w_uq[0]), "a_w_uk": f(a_w_uk[0]), "a_w_uv": f(a_w_uv[0]),
        "a_convwT": f(np.asarray(a_conv_w[0], np.float32).reshape(31, 4, 128).transpose(2, 1, 0).reshape(128, 124)),
        "a_convb": f(np.asarray(a_conv_b[0]).reshape(1, -1)),
        "a_lngT": fm(a_ln_g[0], 4), "a_lnbT": fm(a_ln_b[0], 4),
        "a_w_out": f(a_w_out[0]),
        "c_w_in": f(c_w_in[0]),
        "c_convwT": f(np.asarray(c_conv_w[0], np.float32).reshape(5, 24, 128).transpose(2, 1, 0).reshape(128, 120)),
        "c_convb": f(np.asarray(c_conv_b[0]).reshape(1, -1)),
        "c_convbT": fm(c_conv_b[0], 24),
        "c_dtb": f(np.asarray(c_dt_bias[0]).reshape(1, 64)),
        "c_alog": f(np.asarray(c_a_log[0]).reshape(1, 64)),
        "c_dskip": f(np.asarray(c_d[0]).reshape(1, 32)),
        "c_gnT": fm(c_g_norm[0], 16),
        "c_w_out": f(c_w_out[0]),
        "k_ident": ident, "k_tmf": tmf, "k_tmb": tmb, "k_sel": sel,
    }
    c = np.asarray(c, np.float32)
    c_ctx = np.asarray(c_ctx, np.float32)
    in_maps = []
    for b in range(NCORES):
        m = dict(shared)
        m["x_all"] = f(np.concatenate([x_prompt[2 * b].reshape(256, D), x_prompt[2 * b + 1].reshape(256, D),
                                       x_sample[b]], axis=0))
        m["cckv"] = f(cache_ckv[b, 0])
        m["ckr"] = f(cache_krope[b, 0])
        m["st0"] = f(np.asarray(state_ssd[b, 0], np.float32).transpose(0, 3, 1, 2).reshape(2, 128, 2048))
        cond = np.stack([c_ctx, c[b]], axis=0)
        m["condT"] = f(cond.reshape(2, 8, 128).transpose(2, 1, 0).reshape(128, 16))
        in_maps.append(m)
    if "nc" not in _NC_CACHE:
        _NC_CACHE["nc"] = build_program()
    nc = _NC_CACHE["nc"]
    res = run_bass_kernel_spmd(nc, in_maps, core_ids=list(range(NCORES)))
    R = res.results
    import os as _os
    if _os.environ.get("KDBG"):
        _NC_CACHE["dbg"] = {k: np.asarray(v).astype(np.float32) for k, v in R[0].items() if k.startswith("dbg_")}
    y_prompt = np.zeros((16, 256, D), np.float32)
    y_sample = np.zeros((8, 1024, D), np.float32)
    n_ckv = np.zeros((16, 1, 256, 128), np.float32)
    n_kr = np.zeros((16, 1, 256, 32), np.float32)
    n_st = np.zeros((16, 1, 2, 32, 64, 128), np.float32)
    for b in range(NCORES):
        r = R[b]
        ya = np.asarray(r["y_all"])
        y_prompt[2 * b] = ya[0:256]
        y_prompt[2 * b + 1] = ya[256:512]
        y_sample[b] = ya[512:1536]
        ck = np.asarray(r["o_ckv"])
        kr = np.asarray(r["o_kr"])
        n_ckv[2 * b, 0] = ck[0:256]
        n_ckv[2 * b + 1, 0] = ck[256:512]
        n_kr[2 * b, 0] = kr[0:256]
        n_kr[2 * b + 1, 0] = kr[256:512]
        st = np.asarray(r["o_st"])
        for s_ in range(2):
            n_st[2 * b + s_, 0] = st[s_].reshape(2, 128, 32, 64).transpose(0, 2, 3, 1)
    return (y_prompt, y_sample, n_ckv, n_kr, n_st)
```
